# Optimizing a Trainium2 kernel written in Bass

```python
import math
import jax, jax.numpy as jnp
from jax import lax
import numpy as np


D_MODEL = 1024
BATCH = 4
SEQ = 4096
DEPTH = 4
DEC_BATCH = 16
DEC_SEQ = 2048
PAST_LEN = 128

D_ATTN = 512
N_Q_HEADS = 8
N_KV_HEADS = 2
HEAD_DIM = 64
Q_PER_KV = N_Q_HEADS // N_KV_HEADS
KV_DIM = N_KV_HEADS * HEAD_DIM
WINDOW = 128
BLOCK = 128
D_LRU = 512
N_LRU_BLOCKS = 8
LRU_BLOCK = D_LRU // N_LRU_BLOCKS
CONV_WIDTH = 4
CONV_PAD_LEFT = 2
CONV_PAD_RIGHT = CONV_WIDTH - 1 - CONV_PAD_LEFT
LRU_C = 8.0
N_DIR = 2
D_MIX = D_ATTN + D_LRU
D_IN = D_ATTN + 2 * KV_DIM + 2 * D_LRU
D_FF = ((8 * D_MODEL + 2) // 3 + 255) // 256 * 256
N_MOD = 6
EPS = 1e-6

kernel_name = 'hybrid_bidir_rglru_swa_encoder'


def rmsnorm(x, g):
    xf = x.astype(jnp.float32)
    y = xf * lax.rsqrt(jnp.mean(xf * xf, axis=-1, keepdims=True) + EPS)
    return (y * g.astype(jnp.float32)).astype(x.dtype)


def alibi_slopes():
    h = jnp.arange(N_Q_HEADS, dtype=jnp.float32) + 1.0
    return jnp.exp2(-8.0 * h / N_Q_HEADS)


def banded_attention(q, k, v, sink):
    B, S = q.shape[0], q.shape[1]
    nb = S // BLOCK
    f32 = jnp.float32
    qb = q.astype(f32).reshape(B, nb, BLOCK, N_KV_HEADS, Q_PER_KV, HEAD_DIM)
    pad = ((0, 0), (BLOCK, BLOCK), (0, 0))
    kp = jnp.pad(k.astype(f32), pad).reshape(B, nb + 2, BLOCK, N_KV_HEADS, HEAD_DIM)
    vp = jnp.pad(v.astype(f32), pad).reshape(B, nb + 2, BLOCK, N_KV_HEADS, HEAD_DIM)
    kb = jnp.concatenate([kp[:, :-2], kp[:, 1:-1], kp[:, 2:]], axis=2)
    vb = jnp.concatenate([vp[:, :-2], vp[:, 1:-1], vp[:, 2:]], axis=2)
    scores = jnp.einsum('bnqkgd,bnskd->bnkgqs', qb, kb) * (HEAD_DIM ** -0.5)
    tq = jnp.arange(S).reshape(nb, BLOCK)
    ts = (jnp.arange(nb)[:, None] - 1) * BLOCK + jnp.arange(3 * BLOCK)[None, :]
    dist = jnp.abs(tq[:, :, None] - ts[:, None, :])
    valid = (dist <= WINDOW) & (ts[:, None, :] >= 0) & (ts[:, None, :] < S)
    slopes = alibi_slopes().reshape(N_KV_HEADS, Q_PER_KV)
    bias = -slopes[None, :, :, None, None] * dist[:, None, None].astype(f32)
    logits = jnp.where(valid[None, :, None, None], scores + bias[None], -1e30)
    sink_col = jnp.broadcast_to(
        sink.astype(f32).reshape(N_KV_HEADS, Q_PER_KV)[None, None, :, :, None, None],
        logits.shape[:-1] + (1,))
    p = jax.nn.softmax(jnp.concatenate([logits, sink_col], axis=-1), axis=-1)[..., :-1]
    out = jnp.einsum('bnkgqs,bnskd->bnqkgd', p, vb)
    return out.reshape(B, S, D_ATTN).astype(q.dtype)


def _lin_combine(e1, e2):
    a1, b1 = e1
    a2, b2 = e2
    return a1 * a2, a2 * b1 + b2


def rglru_bidir(x, conv_w, conv_b, w_rg, b_rg, w_ig, b_ig, lam):
    B, S = x.shape[0], x.shape[1]
    f32 = jnp.float32
    xp = jnp.pad(x, ((0, 0), (CONV_PAD_LEFT, CONV_PAD_RIGHT), (0, 0)))
    xc = conv_b
    for j in range(CONV_WIDTH):
        xc = xc + xp[:, j:j + S] * conv_w[j]
    xf = xc.astype(f32)
    xblk = xf.reshape(B, S, N_LRU_BLOCKS, LRU_BLOCK)
    r = jax.nn.sigmoid(jnp.einsum('bsnc,dncm->dbsnm', xblk, w_rg.astype(f32)).reshape(N_DIR, B, S, D_LRU)
                       + b_rg.astype(f32)[:, None, None, :])
    i = jax.nn.sigmoid(jnp.einsum('bsnc,dncm->dbsnm', xblk, w_ig.astype(f32)).reshape(N_DIR, B, S, D_LRU)
                       + b_ig.astype(f32)[:, None, None, :])
    log_a = LRU_C * r * jax.nn.log_sigmoid(lam.astype(f32))[:, None, None, :]
    a = jnp.exp(log_a)
    u = jnp.sqrt(jnp.maximum(1.0 - a * a, 0.0)) * (i * xf[None])
    _, h_fwd = lax.associative_scan(_lin_combine, (a[0], u[0]), axis=1)
    _, h_bwd = lax.associative_scan(_lin_combine, (a[1], u[1]), axis=1, reverse=True)
    return (h_fwd + h_bwd).astype(x.dtype)


def trunk(x, c, w_mod, b_mod, g_norm1, w_in, sink, conv_w, conv_b, w_rg, b_rg, w_ig, b_ig, lam,
          g_attn_out, g_lru_out, w_out, g_norm2, w_ffn_in, w_ffn_out, g_final):
    c_act = jax.nn.silu(c)
    s1 = D_ATTN
    s2 = s1 + KV_DIM
    s3 = s2 + KV_DIM
    s4 = s3 + D_LRU
    for l in range(DEPTH):
        mod = (c_act @ w_mod[l] + b_mod[l])[:, None, :]
        sh1, sc1, gt1, sh2, sc2, gt2 = jnp.split(mod, N_MOD, axis=-1)
        h = rmsnorm(x, g_norm1[l]) * (1.0 + sc1) + sh1
        z = h @ w_in[l]
        q, k, v, xr, gate = jnp.split(z, [s1, s2, s3, s4], axis=-1)
        attn = banded_attention(q, k, v, sink[l])
        lru = rglru_bidir(xr, conv_w[l], conv_b[l], w_rg[l], b_rg[l], w_ig[l], b_ig[l], lam[l])
        lru = lru * jax.nn.gelu(gate)
        mix = jnp.concatenate([rmsnorm(attn, g_attn_out[l]), rmsnorm(lru, g_lru_out[l])], axis=-1)
        x = x + gt1 * (mix @ w_out[l])
        h = rmsnorm(x, g_norm2[l]) * (1.0 + sc2) + sh2
        gu = h @ w_ffn_in[l]
        g_, u_ = jnp.split(gu, 2, axis=-1)
        x = x + gt2 * ((jax.nn.silu(g_) * u_) @ w_ffn_out[l])
    return rmsnorm(x, g_final)


def setup_inputs(seed: int = 0) -> dict:
    key = jax.random.key(seed)
    ks = jax.random.split(key, 32)
    f32 = jnp.float32
    nrm = lambda k, shape, s: jax.random.normal(k, shape, f32) * s
    u = jax.random.uniform(ks[14], (DEPTH, N_DIR, D_LRU), f32, 0.9, 0.999)
    p = u ** (1.0 / LRU_C)
    lam = jnp.log(p) - jnp.log1p(-p)
    return {
        'x_prompt': nrm(ks[0], (BATCH, SEQ, D_MODEL), 1.0),
        'x_sample': nrm(ks[1], (DEC_BATCH, DEC_SEQ, D_MODEL), 1.0),
        'c_prompt': nrm(ks[2], (BATCH, D_MODEL), 1.0),
        'c_sample': nrm(ks[3], (DEC_BATCH, D_MODEL), 1.0),
        'w_mod': nrm(ks[4], (DEPTH, D_MODEL, N_MOD * D_MODEL), 0.5 * D_MODEL ** -0.5),
        'b_mod': nrm(ks[5], (DEPTH, N_MOD * D_MODEL), 0.02),
        'g_norm1': 1.0 + nrm(ks[6], (DEPTH, D_MODEL), 0.02),
        'w_in': nrm(ks[7], (DEPTH, D_MODEL, D_IN), D_MODEL ** -0.5),
        'sink': nrm(ks[8], (DEPTH, N_Q_HEADS), 0.5),
        'conv_w': nrm(ks[9], (DEPTH, CONV_WIDTH, D_LRU), CONV_WIDTH ** -0.5),
        'conv_b': nrm(ks[10], (DEPTH, D_LRU), 0.02),
        'w_rg': nrm(ks[11], (DEPTH, N_DIR, N_LRU_BLOCKS, LRU_BLOCK, LRU_BLOCK), LRU_BLOCK ** -0.5),
        'b_rg': nrm(ks[12], (DEPTH, N_DIR, D_LRU), 0.02),
        'w_ig': nrm(ks[13], (DEPTH, N_DIR, N_LRU_BLOCKS, LRU_BLOCK, LRU_BLOCK), LRU_BLOCK ** -0.5),
        'b_ig': nrm(ks[15], (DEPTH, N_DIR, D_LRU), 0.02),
        'lam': lam,
        'g_attn_out': 1.0 + nrm(ks[16], (DEPTH, D_ATTN), 0.02),
        'g_lru_out': 1.0 + nrm(ks[17], (DEPTH, D_LRU), 0.02),
        'w_out': nrm(ks[18], (DEPTH, D_MIX, D_MODEL), D_MIX ** -0.5),
        'g_norm2': 1.0 + nrm(ks[19], (DEPTH, D_MODEL), 0.02),
        'w_ffn_in': nrm(ks[20], (DEPTH, D_MODEL, 2 * D_FF), D_MODEL ** -0.5),
        'w_ffn_out': nrm(ks[21], (DEPTH, D_FF, D_MODEL), D_FF ** -0.5),
        'g_final': 1.0 + nrm(ks[22], (D_MODEL,), 0.02),
    }


def reference(x_prompt, x_sample, c_prompt, c_sample, w_mod, b_mod, g_norm1, w_in, sink, conv_w, conv_b,
              w_rg, b_rg, w_ig, b_ig, lam, g_attn_out, g_lru_out, w_out, g_norm2, w_ffn_in, w_ffn_out, g_final):
    y_prompt = trunk(x_prompt, c_prompt, w_mod, b_mod, g_norm1, w_in, sink, conv_w, conv_b, w_rg, b_rg,
                     w_ig, b_ig, lam, g_attn_out, g_lru_out, w_out, g_norm2, w_ffn_in, w_ffn_out, g_final)
    y_sample = trunk(x_sample, c_sample, w_mod, b_mod, g_norm1, w_in, sink, conv_w, conv_b, w_rg, b_rg,
                     w_ig, b_ig, lam, g_attn_out, g_lru_out, w_out, g_norm2, w_ffn_in, w_ffn_out, g_final)
    return (y_prompt, y_sample)
```

```python
import contextlib
import numpy as np
import concourse.bass as bass
import concourse.mybir as mybir
from concourse.bass_utils import run_bass_kernel_spmd

F32 = mybir.dt.float32
BF16 = mybir.dt.bfloat16
AF = mybir.ActivationFunctionType
ALU = mybir.AluOpType
AX = mybir.AxisListType

D = 1024
DEPTH = 4
NTOK = 6144
T = 512
NT = NTOK // T
DIN = 1792
DFF = 2816
EPS = 1e-6
NV = 124
V_G1, V_G2, V_BMOD, V_CW, V_CB, V_BRG, V_BIG, V_LAM, V_GA, V_GL, V_SINK = 0, 8, 16, 64, 80, 84, 92, 100, 108, 112, 116
UNITS = ((0, 8, 4), (8, 4, None))

ENGS = ("pe", "act", "dve", "pool", "sp")


class Trk:
    __slots__ = ("name", "w", "rs")

    def __init__(self, name=""):
        self.name = name
        self.w = None
        self.rs = []


def trks(name, n):
    return [Trk("%s%d" % (name, i)) for i in range(n)]


class Op:
    __slots__ = ("eng", "idx", "fn", "waits", "dma", "need_inc", "semval", "dkey")

    def __init__(self, eng, idx, fn, dma, dkey):
        self.eng = eng
        self.idx = idx
        self.fn = fn
        self.waits = []
        self.dma = dma
        self.need_inc = False
        self.semval = None
        self.dkey = dkey


class Prog:
    def __init__(self):
        self.ops = {e: [] for e in ENGS}
        self.waited = {}
        self.waited_dma = {}
        self.dma_keys = {}
        self.last_dma = {}
        self.last_comp = {}

    def op(self, eng, fn, reads=(), writes=(), dma=False, dkey=None):
        lst = self.ops[eng]
        o = Op(eng, len(lst), fn, dma, dkey)
        deps = []
        for t in reads:
            if t.w is not None:
                deps.append((t.w, True))
        for t in writes:
            if t.w is not None:
                deps.append((t.w, True))
            deps.extend((r, False) for r in t.rs)
        for d, isw in deps:
            self._add_wait(o, d, isw)
        for t in reads:
            t.rs.append(o)
        for t in writes:
            t.w = o
            t.rs = []
        lst.append(o)
        if dma:
            self.last_dma[dkey] = o
        else:
            self.last_comp[eng] = o
        return o

    def _add_wait(self, o, d, isw=True, force=False):
        if d is o:
            return
        if d.dma:
            key = (o.eng, d.dkey)
            prev = self.waited_dma.get(key)
            if prev is not None and prev >= d.semval:
                return
            self.waited_dma[key] = d.semval
            o.waits.append(d)
            return
        if d.eng == o.eng and not force:
            if o.eng == "pe":
                return
        key = (o.eng, d.eng)
        prev = self.waited.get(key, -1)
        if prev >= d.idx:
            return
        self.waited[key] = d.idx
        o.waits.append(d)
        d.need_inc = True

    def dma(self, eng, fn, dkey, reads=(), writes=()):
        self.dma_keys.setdefault(dkey, 0)
        self.dma_keys[dkey] += 16
        val = self.dma_keys[dkey]
        o = self.op(eng, fn, reads=reads, writes=writes, dma=True, dkey=dkey)
        o.semval = val
        return o

    def barrier(self):
        comp = dict(self.last_comp)
        dm = dict(self.last_dma)
        spn = self.op("sp", lambda h: h.nop())
        for d in list(comp.values()) + list(dm.values()):
            self._add_wait(spn, d, True, force=True)
        for e in ENGS:
            if e == "sp":
                continue
            n = self.op(e, lambda h: h.nop())
            self._add_wait(n, spn, True)

    def emit(self, nc, final_waits=()):
        for e in ENGS:
            c = 0
            for o in self.ops[e]:
                if o.dma:
                    continue
                if o.need_inc:
                    c += 1
                    o.semval = c
        with contextlib.ExitStack() as st:
            esem = {e: st.enter_context(nc.semaphore("S_" + e)) for e in ENGS}
            dsem = {k: st.enter_context(nc.semaphore("D_%d" % i)) for i, k in enumerate(self.dma_keys)}
            block = st.enter_context(nc.Block())
            prog = self

            def run(engname, h):
                for o in prog.ops[engname]:
                    for d in o.waits:
                        if d.dma:
                            h.wait_ge(dsem[d.dkey], d.semval)
                        else:
                            h.wait_ge(esem[d.eng], d.semval)
                    ins = o.fn(h)
                    if o.dma:
                        ins.then_inc(dsem[o.dkey], 16)
                    elif o.need_inc:
                        ins.then_inc(esem[o.eng], 1)
                if engname == "sp":
                    for o in final_waits:
                        h.wait_ge(dsem[o.dkey], o.semval)

            @block.tensor
            def _(h):
                run("pe", h)

            @block.scalar
            def _(h):
                run("act", h)

            @block.vector
            def _(h):
                run("dve", h)

            @block.gpsimd
            def _(h):
                run("pool", h)

            @block.sync
            def _(h):
                run("sp", h)


def f_mm(out, lhsT, rhs, start, stop):
    return lambda h: h.matmul(out, lhsT=lhsT, rhs=rhs, start=start, stop=stop)


def f_tr(out, in_, ident):
    return lambda h: h.transpose(out, in_, ident)


def f_act(out, in_, func, bias=None, scale=None, accum_out=None):
    kw = {}
    if bias is not None:
        kw["bias"] = bias
    if scale is not None:
        kw["scale"] = scale
    if accum_out is not None:
        kw["accum_out"] = accum_out
    return lambda h: h.activation(out=out, in_=in_, func=func, **kw)


def f_tt(out, in0, in1, op):
    return lambda h: h.tensor_tensor(out=out, in0=in0, in1=in1, op=op)


def f_ts(out, in0, s1, s2, op0, op1=None):
    if op1 is None:
        return lambda h: h.tensor_scalar(out=out, in0=in0, scalar1=s1, scalar2=None, op0=op0)
    return lambda h: h.tensor_scalar(out=out, in0=in0, scalar1=s1, scalar2=s2, op0=op0, op1=op1)


def f_stt(out, in0, scalar, in1, op0, op1):
    return lambda h: h.scalar_tensor_tensor(out=out, in0=in0, scalar=scalar, in1=in1, op0=op0, op1=op1)


def f_cp(out, in_):
    return lambda h: h.tensor_copy(out=out, in_=in_)


def f_dma(out, in_):
    return lambda h: h.dma_start(out=out, in_=in_)


def f_scan(out, d0, d1, init):
    return lambda h: h.tensor_tensor_scan(out=out, data0=d0, data1=d1, initial=init, op0=ALU.mult, op1=ALU.add)


class Arena:
    def __init__(self, nc, lo, hi):
        self.nc, self.lo, self.hi, self.cur, self.n = nc, lo, hi, lo, 0

    def reset(self, to=None):
        self.cur = self.lo if to is None else to

    def mark(self):
        return self.cur

    def alloc(self, name, shape, dt):
        esz = 4 if dt == F32 else 2
        nbytes = int(np.prod(shape[1:])) * esz
        off = (self.cur + 63) // 64 * 64
        assert off + nbytes <= self.hi, ("SBUF arena overflow", name, off, nbytes, self.hi)
        self.cur = off + nbytes
        self.n += 1
        return self.nc.alloc_sbuf_tensor_at("%s_%d" % (name, self.n), list(shape), dt, offset=off)


def build_program(depth=DEPTH):
    nc = bass.Bass("TRN2", target_bir_lowering=False)
    L = depth
    dr = lambda name, shape, dt=F32, kind="ExternalInput": nc.dram_tensor(name, list(shape), dt, kind=kind).ap()
    xT_d = dr("xT", [8, 128, NTOK])
    cT_d = dr("cT", [128, 24])
    link_d = dr("link", [128, 1])
    dtab_d = dr("dtab", [128, 3 * 384])
    vecs_d = dr("vecs", [128, DEPTH * NV + 8])
    wmod_d = dr("w_mod", [DEPTH, D, 6 * D])
    win_d = dr("w_in", [DEPTH, D, DIN])
    wg_d = dr("wg", [DEPTH, 16, 128, 128])
    wout_d = dr("w_out", [DEPTH, D, D])
    wfi_d = dr("w_ffn_in", [DEPTH, D, 2 * DFF])
    wfo_d = dr("w_ffn_out", [DEPTH, DFF, D])
    yT_d = dr("yT", [8, 128, NTOK], kind="ExternalOutput")
    xa_d = dr("xa_s", [8, 128, NTOK], kind="Internal")
    xb_d = dr("xb_s", [8, 128, NTOK], kind="Internal")
    xc_d = dr("xc_s", [4, 128, NTOK], kind="Internal")
    hf_d = dr("hf_s", [4, 128, NTOK], kind="Internal")
    gg_d = dr("gg_s", [4, 128, NTOK], BF16, kind="Internal")
    an_d = dr("an_s", [4, 128, NTOK], BF16, kind="Internal")

    def dview(d, t0, n=T):
        return d[:, :, t0:t0 + n].rearrange("c p t -> p c t")

    P = Prog()
    ar = Arena(nc, 16640, 229376)
    TX = {"in": trks("xin", NT), "xa": trks("xa", NT), "xb": trks("xb", NT), "y": trks("y", NT)}
    TXC, THF, TGG, TAN = trks("xcs", NT), trks("hfs", NT), trks("ggs", NT), trks("ans", NT)

    VEC = ar.alloc("vec", [128, DEPTH * NV + 8], F32)
    MOD = ar.alloc("mod", [128, DEPTH * 6 * 8 * 3], F32)
    C8 = ar.alloc("c8", [128, DEPTH * 8], F32)
    SK8 = ar.alloc("sk8", [128, DEPTH * 8], F32)
    HB2 = ar.alloc("hb2", [128, DEPTH * 16], F32)
    QUARTER = ar.alloc("quarter", [128, 1], F32)
    DT = ar.alloc("dtab", [128, 3 * 384], F32)
    LINK = ar.alloc("link", [128, 1], F32)
    ONES = ar.alloc("ones", [128, 128], BF16)
    ONES5 = ar.alloc("ones5", [128, 128], BF16)
    IDN = ar.alloc("idn", [128, 128], BF16)
    EPSC = ar.alloc("epsc", [128, 1], F32)
    ONEC = ar.alloc("onec", [128, 1], F32)
    NHALFC = ar.alloc("nhalfc", [128, 1], F32)
    tVEC, tMOD, tC8, tSK8, tDT, tLINK, tCONST = (Trk(n) for n in ("vec", "mod", "c8", "sk8", "dt", "link", "const"))
    base_mark = ar.mark()

    PSF = [nc.alloc_psum_tensor("psf%d" % i, [128, 512], F32) for i in range(6)]
    PSB = [nc.alloc_psum_tensor("psb%d" % i, [128, 1024], BF16) for i in range(2)]
    tPSF = trks("psf", 6)
    tPSB = trks("psb", 2)

    def mod_ap(l, k, c, seg):
        o = ((l * 6 + k) * 8 + c) * 3 + seg
        return MOD[:, o:o + 1]

    def vec_ap(l, off, n=1):
        return VEC[:, l * NV + off:l * NV + off + n]

    P.dma("sp", f_dma(VEC[:], vecs_d), "vec", writes=[tVEC])
    P.dma("sp", f_dma(DT[:], dtab_d), "dt", writes=[tDT])
    P.dma("sp", f_dma(LINK[:], link_d), "link", writes=[tLINK])
    P.op("pool", lambda h: h.memset(ONES[:], 1.0 / 1024), writes=[tCONST])
    P.op("pool", lambda h: h.memset(ONES5[:], 1.0 / 512), writes=[tCONST])
    P.op("pool", lambda h: h.memset(EPSC[:], EPS), writes=[tCONST])
    P.op("pool", lambda h: h.memset(ONEC[:], 1.0), writes=[tCONST])
    P.op("pool", lambda h: h.memset(NHALFC[:], -0.5), writes=[tCONST])
    P.op("pool", lambda h: h.memset(QUARTER[:], 0.25), writes=[tCONST])
    P.op("pool", lambda h: h.memset(IDN[:], 0.0), writes=[tCONST])
    P.op("pool", lambda h: h.affine_select(out=IDN[:], in_=IDN[:], pattern=[[-1, 128]], compare_op=ALU.not_equal,
                                            fill=1.0, base=0, channel_multiplier=1), reads=[tCONST], writes=[tCONST])
    CT = ar.alloc("ct", [128, 24], F32)
    CA = ar.alloc("ca", [128, 24], BF16)
    WM = [ar.alloc("wm", [128, 8, D], BF16) for _ in range(2)]
    tCT, tCA = Trk("ct"), Trk("ca")
    tWM = trks("wm", 2)
    P.dma("sp", f_dma(CT[:], cT_d), "ct", writes=[tCT])
    P.op("act", f_act(CA[:], CT[:], AF.Silu), reads=[tCT], writes=[tCA])
    it = 0
    for l in range(L):
        for k in range(6):
            s = it % 2
            src = wmod_d[l][:, k * D:(k + 1) * D].rearrange("(c p) n -> p c n", p=128)
            P.dma("pool", f_dma(WM[s][:], src), "wm%d" % s, writes=[tWM[s]])
            pb = it % 6
            ps = PSF[pb]
            for fc in range(8):
                for kc in range(8):
                    P.op("pe", f_mm(ps[:, fc * 3:fc * 3 + 3], WM[s][:, kc, fc * 128:(fc + 1) * 128],
                                    CA[:, kc * 3:kc * 3 + 3], kc == 0, kc == 7),
                         reads=[tWM[s], tCA], writes=[tPSF[pb]])
            o = (l * 6 + k) * 24
            for seg in range(3):
                P.op("dve", f_tt(MOD[:, o + seg:o + 24:3], ps[:, seg:24:3], vec_ap(l, V_BMOD + k * 8, 8), ALU.add),
                     reads=[tPSF[pb], tVEC], writes=[tMOD])
            it += 1
        for k, gof in ((1, V_G1), (4, V_G2)):
            o = (l * 6 + k) * 24
            for seg in range(3):
                P.op("dve", f_stt(MOD[:, o + seg:o + 24:3], MOD[:, o + seg:o + 24:3], 1.0, vec_ap(l, gof, 8),
                                  ALU.add, ALU.mult), reads=[tMOD, tVEC], writes=[tMOD])
    E1 = ar.alloc("e1", [128, DEPTH * 8], F32)
    E2 = ar.alloc("e2", [128, DEPTH * 8], F32)
    tE = Trk("e")
    for l in range(L):
        sl = slice(l * 8, l * 8 + 8)
        P.op("act", f_act(E1[:, sl], vec_ap(l, V_LAM, 8), AF.Exp, scale=-1.0), reads=[tVEC], writes=[tE])
        P.op("dve", f_ts(E2[:, sl], E1[:, sl], -0.25, 1.0 / 3, ALU.mult, ALU.add), reads=[tE], writes=[tE])
        P.op("dve", f_tt(E2[:, sl], E2[:, sl], E1[:, sl], ALU.mult), reads=[tE], writes=[tE])
        P.op("dve", f_ts(E2[:, sl], E2[:, sl], -1.0, 0.5, ALU.mult, ALU.add), reads=[tE], writes=[tE])
        P.op("dve", f_tt(E2[:, sl], E2[:, sl], E1[:, sl], ALU.mult), reads=[tE], writes=[tE])
        P.op("dve", f_ts(E2[:, sl], E2[:, sl], -1.0, 1.0, ALU.mult, ALU.add), reads=[tE], writes=[tE])
        P.op("dve", f_tt(E2[:, sl], E2[:, sl], E1[:, sl], ALU.mult), reads=[tE], writes=[tE])
        P.op("dve", f_ts(C8[:, sl], E2[:, sl], -4.0, None, ALU.mult), reads=[tE], writes=[tC8])
        P.op("dve", f_ts(HB2[:, l * 16:l * 16 + 16], vec_ap(l, V_BRG, 16), 0.5, None, ALU.mult), reads=[tVEC], writes=[tC8])
        P.op("dve", f_ts(SK8[:, sl], vec_ap(l, V_SINK, 8), 8.0, None, ALU.mult), reads=[tVEC], writes=[tSK8])
    P.barrier()
    ar.reset(base_mark)

    def norm_gen(XTt, tX, HN, tHN, TMP, tTMP, RS, tRS, l, kg, ksh, seg, mmb):
        P.op("act", f_act(HN[:, :, :], XTt[:, :, :], AF.Square), reads=tX, writes=tHN)
        yield
        ps = PSF[mmb]
        for c in range(8):
            P.op("pe", f_mm(ps[:, :], ONES[:, :], HN[:, c, :], c == 0, c == 7), reads=[tHN[c], tCONST], writes=[tPSF[mmb]])
        P.op("act", f_act(TMP[0][:, :], ps[:, :], AF.Sqrt, bias=EPSC[:, 0:1]), reads=[tPSF[mmb], tCONST], writes=[tTMP[0]])
        P.op("dve", lambda h, o=RS[:, :], i=TMP[0][:, :]: h.reciprocal(out=o, in_=i), reads=[tTMP[0]], writes=[tRS])
        yield
        for c in range(8):
            b = 1 + (c % 2)
            P.op("dve", f_stt(TMP[b][:, :], XTt[:, c, :], mod_ap(l, kg, c, seg), RS[:, :], ALU.mult, ALU.mult),
                 reads=[tX[c], tMOD, tRS], writes=[tTMP[b]])
            P.op("act", f_act(HN[:, c, :], TMP[b][:, :], AF.Identity, bias=mod_ap(l, ksh, c, seg)),
                 reads=[tTMP[b], tMOD], writes=[tHN[c]])
            if c % 2 == 1:
                yield

    def norm_mod(*a):
        for _ in norm_gen(*a):
            pass

    def interleave(gens):
        gens = list(gens)
        while gens:
            for g in list(gens):
                try:
                    next(g)
                except StopIteration:
                    gens.remove(g)

    def g1(l, d, c, XCB, tXCB, WG, tWG, GB, tGB, bankfn):
        for g, nm in ((0, "RA"), (1, "IU")):
            mb = bankfn()
            hb = HB2[:, l * 16 + g * 8 + d * 4 + c:l * 16 + g * 8 + d * 4 + c + 1]
            P.op("pe", f_mm(PSF[mb][:, :], WG[:, (d * 2 + g) * 4 + c, :], XCB[:, c, :], True, True),
                 reads=[tWG, tXCB[c]], writes=[tPSF[mb]])
            P.op("act", f_act(GB[nm][c][:, :], PSF[mb][:, :], AF.Tanh, bias=hb, scale=0.5),
                 reads=[tPSF[mb], tC8], writes=[tGB[nm][c]])
        hc8 = C8[:, l * 8 + d * 4 + c:l * 8 + d * 4 + c + 1]
        P.op("act", f_act(GB["RA"][c][:, :], GB["RA"][c][:, :], AF.Exp, bias=hc8, scale=hc8),
             reads=[tGB["RA"][c], tC8], writes=[tGB["RA"][c]])
        P.op("act", f_act(GB["S"][c][:, :], GB["RA"][c][:, :], AF.Square), reads=[tGB["RA"][c]], writes=[tGB["S"][c]])

    def g2(c, XC, tXC, GB, tGB):
        P.op("dve", f_ts(GB["S"][c][:, :], GB["S"][c][:, :], 0.99999994, -1.0, ALU.min, ALU.mult),
             reads=[tGB["S"][c]], writes=[tGB["S"][c]])
        P.op("dve", f_stt(GB["IU"][c][:, :], GB["IU"][c][:, :], 1.0, XC[:, c, :], ALU.add, ALU.mult),
             reads=[tGB["IU"][c], tXC[c]], writes=[tGB["IU"][c]])

    def gsq(c, GB, tGB):
        P.op("act", f_act(GB["S"][c][:, :], GB["S"][c][:, :], AF.Sqrt, bias=QUARTER[:, 0:1], scale=0.25),
             reads=[tGB["S"][c], tCONST], writes=[tGB["S"][c]])

    def g3(c, GB, tGB):
        P.op("dve", f_tt(GB["IU"][c][:, :], GB["IU"][c][:, :], GB["S"][c][:, :], ALU.mult),
             reads=[tGB["IU"][c], tGB["S"][c]], writes=[tGB["IU"][c]])

    def gates_gen(l, d, XC, tXC, XCB, tXCB, WG, tWG, GB, tGB, bankfn):
        for c in range(4):
            g1(l, d, c, XCB, tXCB, WG, tWG, GB, tGB, bankfn)
            yield
        for c in range(4):
            g2(c, XC, tXC, GB, tGB)
        yield
        for c in range(4):
            gsq(c, GB, tGB)
        yield
        for c in range(4):
            g3(c, GB, tGB)
        yield

    def alloc_gb():
        GB = {n: [ar.alloc("gb" + n, [128, T], F32) for _ in range(4)] for n in ("RA", "IU", "S")}
        tGB = {n: trks("gb" + n, 4) for n in ("RA", "IU", "S")}
        return GB, tGB

    final_out = []
    xin_d, xin_t = xT_d, TX["in"]
    for l in range(L):
        xout_d, xout_t = (yT_d, TX["y"]) if l == L - 1 else (xa_d, TX["xa"])
        ar.reset(base_mark)
        WIN = ar.alloc("win", [128, 8, DIN], BF16)
        WG = ar.alloc("wg", [128, 16, 128], BF16)
        tWIN, tWG = Trk("win"), Trk("wg")
        wsrc = win_d[l].rearrange("(c p) n -> p c n", p=128)
        for q4 in range(4):
            P.dma("pool", f_dma(WIN[:, :, q4 * 448:(q4 + 1) * 448], wsrc[:, :, q4 * 448:(q4 + 1) * 448]),
                  "win%d" % q4, writes=[tWIN])
        P.dma("pool", f_dma(WG[:], wg_d[l].rearrange("j p m -> p j m")), "wg", writes=[tWG])
        XTt = ar.alloc("xt", [128, 8, T], F32)
        tX = trks("x", 8)
        HNs = [ar.alloc("hn", [128, 8, T], BF16) for _ in range(2)]
        tHNs = [trks("hn", 8) for _ in range(2)]
        TMP = [ar.alloc("tmp", [128, T], F32) for _ in range(3)]
        tTMP = trks("tmp", 3)
        RS = ar.alloc("rs", [128, T], F32)
        tRS = Trk("rs")
        S = 8 * T
        KU = ar.alloc("ku", [128, S], BF16)
        tKU = trks("ku", 8)
        VU = ar.alloc("vu", [128, 32, 128], BF16)
        tVU = trks("vu", 8)
        QR = ar.alloc("qr", [128, 4, 2, T], BF16)
        tQR = trks("qr", 2)
        XR = ar.alloc("xr", [128, 4, 3, T + 3], F32)
        tXR = trks("xr", 3)
        XC = [ar.alloc("xc", [128, 4, T], F32)] * 2
        tXCs = [trks("xc", 4)] * 2
        CARRY = ar.alloc("carry", [128, 4], F32)
        tCARRY = trks("carry", 4)
        XCB = ar.alloc("xcb", [128, 4, T], BF16)
        tXCB = trks("xcb", 4)
        GB, tGB = alloc_gb()
        HF = [ar.alloc("hf", [128, 4, T], F32)] * 2
        tHF = [trks("hf", 4)] * 2
        GG = [ar.alloc("gg", [128, 4, T], BF16)] * 2
        tGG = [Trk("gg")] * 2
        ANT = [ar.alloc("ant", [128, 4, T], BF16) for _ in range(2)]
        tANT = trks("ant", 2)
        TT = [ar.alloc("tt", [128, 384], F32) for _ in range(4)]
        tTT = trks("tt", 4)
        PM = [ar.alloc("pm", [128, 384], BF16) for _ in range(4)]
        tPM = trks("pm", 4)
        PT = [ar.alloc("pt", [128, 3, 128], BF16) for _ in range(4)]
        tPT = trks("pt", 4)
        ST = ar.alloc("st", [128, 128], F32)
        tSTp = [{n: Trk("st" + n) for n in ("es", "rden", "ssq", "rsa")} for _ in range(2)]
        tMXp, tNEGBp, tRSUMp = [trks("mx", 8) for _ in range(2)], [trks("negb", 8) for _ in range(2)], [trks("rsum", 8) for _ in range(2)]
        tPSBh = trks("psbh", 2)
        ATT = ar.alloc("att", [128, 512], F32)
        tATT = Trk("att")
        ATN = ar.alloc("atn", [128, 512], BF16)
        tATN = Trk("atn")
        JUNK = ar.alloc("junk", [128, 512], BF16)
        CTMP = ar.alloc("ctmp", [128, T], F32)
        tCTMP = Trk("ctmp")
        tJUNK = Trk("junk")
        for (ut0, unt, ulink) in UNITS:
            mmrr = [0]

            def mmbank():
                b = mmrr[0] % 3
                mmrr[0] += 1
                return b

            gbank = [0]

            def gatebank():
                b = 3 + gbank[0] % 2
                gbank[0] += 1
                return b

            hcnt = [0]

            def lru_conv(j):
                s = j % 3
                for c in range(4):
                    cw = lambda tap: vec_ap(l, V_CW + tap * 4 + c)
                    P.op("pool", f_ts(XC[0][:, c, :], XR[:, c, s, 0:T], cw(0), vec_ap(l, V_CB + c), ALU.mult, ALU.add),
                         reads=[tXR[s], tVEC], writes=[tXCs[0][c]])
                    for tap in (1, 2, 3):
                        P.op("pool", f_ts(CTMP[:, :], XR[:, c, s, tap:tap + T], cw(tap), 0.0, ALU.mult, ALU.add),
                             reads=[tXR[s], tVEC], writes=[tCTMP])
                        P.op("pool", f_tt(XC[0][:, c, :], XC[0][:, c, :], CTMP[:, :], ALU.add),
                             reads=[tCTMP, tXCs[0][c]], writes=[tXCs[0][c]])
                    P.op("pool", f_cp(XCB[:, c, :], XC[0][:, c, :]), reads=[tXCs[0][c]], writes=[tXCB[c]])

            def lruG(j):
                gi = ut0 + j
                for c in range(4):
                    g1(l, 0, c, XCB, tXCB, WG, tWG, GB, tGB, gatebank)
                    yield
                for c in range(4):
                    g2(c, XC[0], tXCs[0], GB, tGB)
                yield
                for c in range(4):
                    gsq(c, GB, tGB)
                yield
                for c in range(4):
                    g3(c, GB, tGB)
                    if ulink is not None and j == ulink:
                        P.op("dve", f_ts(GB["RA"][c][:, 0:1], GB["RA"][c][:, 0:1], LINK[:, 0:1], None, ALU.mult),
                             reads=[tGB["RA"][c], tLINK], writes=[tGB["RA"][c]])
                    init = 0.0 if j == 0 else CARRY[:, c:c + 1]
                    rd = [tGB["RA"][c], tGB["IU"][c]] + ([] if j == 0 else [tCARRY[c]])
                    P.op("dve", f_scan(HF[0][:, c, :], GB["RA"][c][:, :], GB["IU"][c][:, :], init), reads=rd, writes=[tHF[0][c]])
                    P.op("dve", f_cp(CARRY[:, c:c + 1], HF[0][:, c, T - 1:T]), reads=[tHF[0][c]], writes=[tCARRY[c]])
                    yield
                t0 = gi * T
                P.dma("sp", f_dma(dview(xc_d, t0), XC[0][:, :, :]), "xco", reads=tXCs[0], writes=[TXC[gi]])
                P.dma("sp", f_dma(dview(hf_d, t0), HF[0][:, :, :]), "hfo", reads=tHF[0], writes=[THF[gi]])

            def mix(j, ngen=None):
                gi = ut0 + j
                qs = j % 2
                asl = j % 2
                nblk = unt * 4
                lru_conv(j)

                def lru_step(p):
                    pass

                def geom(b):
                    n = 4 * j + b
                    lo = max(n - 1, 0)
                    hi = min(n + 1, nblk - 1)
                    nkb = hi - lo + 1
                    doff = 0 if lo == n - 1 else 128
                    tab = 0
                    if ulink is not None and n == ulink * 4 - 1:
                        tab = 1
                    if ulink is not None and n == ulink * 4:
                        tab = 2
                    tl = sorted(set((lo // 4, hi // 4)))
                    return n, lo, hi, nkb, doff, tab, [tKU[tt] for tt in tl], [tVU[tt] for tt in tl]

                sbank = {}

                def SA(p):
                    b, c = p // 4, p % 4
                    n, lo, hi, nkb, doff, tab, ktr, vtr = geom(b)
                    nk = nkb * 128
                    for hh in range(2):
                        sb = hcnt[0] % 3
                        hcnt[0] += 1
                        sbank[(p, hh)] = sb
                        pl, ph = hh * 64, hh * 64 + 64
                        P.op("pe", f_mm(PSF[sb][:, 0:nk], QR[pl:ph, c, qs, b * 128:(b + 1) * 128],
                                        KU[pl:ph, lo * 128:(hi + 1) * 128], True, True),
                             reads=[tQR[qs]] + ktr, writes=[tPSF[sb]])

                def SB(p, hh):
                    b, c = p // 4, p % 4
                    n, lo, hi, nkb, doff, tab, ktr, vtr = geom(b)
                    nk = nkb * 128
                    so = (b % 2) * 64
                    Dv = DT[:, tab * 384 + doff:tab * 384 + doff + nk]
                    if True:
                        h_ = c + 4 * hh
                        sb = sbank[(p, hh)]
                        r4 = (c * 2 + hh) % 4
                        slope8 = 8.0 * 2.0 ** (-(h_ + 1))
                        P.op("dve", f_stt(TT[r4][:, 0:nk], Dv, slope8, PSF[sb][:, 0:nk], ALU.mult, ALU.add),
                             reads=[tDT, tPSF[sb]], writes=[tTT[r4]])
                        P.op("dve", lambda h, o=ST[:, so + h_:so + h_ + 1], i=TT[r4][:, 0:nk]: h.reduce_max(out=o, in_=i, axis=AX.X),
                             reads=[tTT[r4]], writes=[tMXp[b % 2][h_]])
                        P.op("dve", f_ts(ST[:, so + 8 + h_:so + 9 + h_], ST[:, so + h_:so + h_ + 1],
                                         SK8[:, l * 8 + h_:l * 8 + h_ + 1], -0.125, ALU.max, ALU.mult),
                             reads=[tMXp[b % 2][h_], tSK8], writes=[tNEGBp[b % 2][h_]])
                        P.op("act", f_act(PM[r4][:, 0:nk], TT[r4][:, 0:nk], AF.Exp, bias=ST[:, so + 8 + h_:so + 9 + h_], scale=0.125,
                                          accum_out=ST[:, so + 16 + h_:so + 17 + h_]),
                             reads=[tTT[r4], tNEGBp[b % 2][h_]], writes=[tPM[r4], tRSUMp[b % 2][h_]])

                def SC(p):
                    b, c = p // 4, p % 4
                    n, lo, hi, nkb, doff, tab, ktr, vtr = geom(b)
                    nk = nkb * 128
                    for hh in range(2):
                        r4 = (c * 2 + hh) % 4
                        pbk = hh
                        for jj in range(nkb):
                            P.op("pe", f_tr(PSB[pbk][:, jj * 128:(jj + 1) * 128],
                                            PM[r4][:, jj * 128:(jj + 1) * 128], IDN[:, :]),
                                 reads=[tPM[r4], tCONST], writes=[tPSB[pbk]])
                        P.op("act", f_act(PT[r4][:, 0:nkb, :],
                                          PSB[pbk][:, 0:nk].rearrange("p (j q) -> p j q", q=128), AF.Copy),
                             reads=[tPSB[pbk]], writes=[tPT[r4]])

                def SD(p):
                    b, c = p // 4, p % 4
                    n, lo, hi, nkb, doff, tab, ktr, vtr = geom(b)
                    for hh in range(2):
                        h_ = c + 4 * hh
                        r4 = (c * 2 + hh) % 4
                        for jj in range(nkb):
                            P.op("pe", f_mm(PSF[5][:, h_ * 64:(h_ + 1) * 64], PT[r4][:, jj, :],
                                            VU[:, lo + jj, hh * 64:(hh + 1) * 64], jj == 0, jj == nkb - 1),
                                 reads=[tPT[r4]] + vtr, writes=[tPSF[5]])

                def EPI(b):
                    so = (b % 2) * 64
                    tST = tSTp[b % 2]
                    P.op("dve", f_tt(ST[:, so + 24:so + 32], ST[:, so + 8:so + 16], vec_ap(l, V_SINK, 8), ALU.add),
                         reads=tNEGBp[b % 2] + [tVEC], writes=[tST["es"]])
                    P.op("act", f_act(ST[:, so + 24:so + 32], ST[:, so + 24:so + 32], AF.Exp), reads=[tST["es"]], writes=[tST["es"]])
                    P.op("dve", f_tt(ST[:, so + 32:so + 40], ST[:, so + 24:so + 32], ST[:, so + 16:so + 24], ALU.add),
                         reads=[tST["es"]] + tRSUMp[b % 2], writes=[tST["rden"]])
                    P.op("dve", lambda h, o=ST[:, so + 32:so + 40]: h.reciprocal(out=o, in_=o), reads=[tST["rden"]], writes=[tST["rden"]])
                    P.op("dve", f_tt(ATT[:, :].rearrange("p (h d) -> p h d", d=64),
                                     PSF[5][:, :].rearrange("p (h d) -> p h d", d=64),
                                     ST[:, so + 32:so + 40].unsqueeze(2).to_broadcast([128, 8, 64]), ALU.mult),
                         reads=[tPSF[5], tST["rden"]], writes=[tATT])
                    P.op("act", f_act(JUNK[:, :], ATT[:, :], AF.Square, accum_out=ST[:, so + 40:so + 41]),
                         reads=[tATT], writes=[tJUNK, tST["ssq"]])
                    P.op("dve", f_ts(ST[:, so + 41:so + 42], ST[:, so + 40:so + 41], 1.0 / 512, EPS, ALU.mult, ALU.add),
                         reads=[tST["ssq"]], writes=[tST["rsa"]])
                    P.op("pool", f_tt(ST[:, so + 41:so + 42], ST[:, so + 41:so + 42], NHALFC[:, 0:1], ALU.pow),
                         reads=[tST["rsa"], tCONST], writes=[tST["rsa"]])
                    P.op("dve", f_ts(ATN[:, :], ATT[:, :], ST[:, so + 41:so + 42], None, ALU.mult),
                         reads=[tATT, tST["rsa"]], writes=[tATN])
                    for cc in range(4):
                        P.op("pe", f_tr(PSB[1][:, 512 + cc * 128:512 + (cc + 1) * 128], ATN[:, cc * 128:(cc + 1) * 128], IDN[:, :]),
                             reads=[tATN, tCONST], writes=[tPSB[1]])
                    for cc in range(4):
                        P.op("act", f_act(ANT[asl][:, cc, b * 128:(b + 1) * 128], PSB[1][:, 512 + cc * 128:512 + (cc + 1) * 128],
                                          AF.Identity, scale=vec_ap(l, V_GA + cc)),
                             reads=[tPSB[1], tVEC], writes=[tANT[asl]])

                NP = 16
                SA(0)
                for p in range(NP + 2):
                    if p < NP:
                        SB(p, 0)
                    if p + 1 < NP:
                        SA(p + 1)
                    if p < NP:
                        SB(p, 1)
                    if 0 <= p - 2 < NP:
                        SD(p - 2)
                        if (p - 2) % 4 == 3:
                            EPI((p - 2) // 4)
                    if 0 <= p - 1 < NP:
                        SC(p - 1)
                    lru_step(p)
                    if ngen is not None and p >= 2:
                        next(ngen, None)
                if ngen is not None:
                    for _ in ngen:
                        pass
                P.dma("sp", f_dma(dview(an_d, gi * T), ANT[asl][:, :, :]), "ano%d" % asl, reads=[tANT[asl]], writes=[TAN[gi]])

            def xload(i):
                gi = ut0 + i
                P.dma("sp", f_dma(XTt[:, :, :], dview(xin_d, gi * T)), "xt", reads=[xin_t[gi]], writes=tX)

            def ngen_for(i):
                return norm_gen(XTt, tX, HNs[i % 2], tHNs[i % 2], TMP, tTMP, RS, tRS, l, 1, 0, (ut0 + i) // 4, gatebank())

            xload(0)
            for _ in ngen_for(0):
                pass
            def proj(i):
                HN, tHN = HNs[i % 2], tHNs[i % 2]
                gi = ut0 + i
                seg = gi // 4
                t0 = gi * T
                s = i % 2
                xs_ = i % 3
                xp_ = (i - 1) % 3
                for m in range(4):
                    mb = mmbank()
                    for c in range(8):
                        P.op("pe", f_mm(PSF[mb][:, :], WIN[:, c, m * 128:(m + 1) * 128], HN[:, c, :], c == 0, c == 7),
                             reads=[tWIN, tHN[c]], writes=[tPSF[mb]])
                    P.op("act", f_act(QR[:, m, s, :], PSF[mb][:, :], AF.Copy), reads=[tPSF[mb]], writes=[tQR[s]])
                    yield
                mb = mmbank()
                for c in range(8):
                    P.op("pe", f_mm(PSF[mb][:, :], WIN[:, c, 512:640], HN[:, c, :], c == 0, c == 7),
                         reads=[tWIN, tHN[c]], writes=[tPSF[mb]])
                P.op("dve", f_cp(KU[:, i * T:(i + 1) * T], PSF[mb][:, :]), reads=[tPSF[mb]], writes=[tKU[i]])
                yield
                mb = mmbank()
                for b in range(4):
                    for c in range(8):
                        P.op("pe", f_mm(PSF[mb][:, b * 128:(b + 1) * 128], HN[:, c, b * 128:(b + 1) * 128],
                                        WIN[:, c, 640:768], c == 0, c == 7),
                             reads=[tWIN, tHN[c]], writes=[tPSF[mb]])
                P.op("act", f_act(VU[:, i * 4:(i + 1) * 4, :], PSF[mb][:, :].rearrange("p (b f) -> p b f", f=128), AF.Copy),
                     reads=[tPSF[mb]], writes=[tVU[i]])
                yield
                for m in range(4):
                    mb = mmbank()
                    for c in range(8):
                        P.op("pe", f_mm(PSF[mb][:, :], WIN[:, c, 768 + m * 128:768 + (m + 1) * 128], HN[:, c, :], c == 0, c == 7),
                             reads=[tWIN, tHN[c]], writes=[tPSF[mb]])
                    P.op("dve", f_cp(XR[:, m, xs_, 2:T + 2], PSF[mb][:, :]), reads=[tPSF[mb]], writes=[tXR[xs_]])
                    yield
                linked = (ulink is not None and i == ulink)
                if i == 0:
                    P.op("pool", lambda h, o=XR[:, :, xs_, 0:2]: h.memset(o, 0.0), writes=[tXR[xs_]])
                else:
                    if linked:
                        P.op("pool", f_ts(XR[:, :, xs_, 0:2], XR[:, :, xp_, T:T + 2], LINK[:, 0:1], 0.0, ALU.mult, ALU.add),
                             reads=[tXR[xp_], tLINK], writes=[tXR[xs_]])
                        P.op("pool", f_ts(XR[:, :, xp_, T + 2:T + 3], XR[:, :, xs_, 2:3], LINK[:, 0:1], 0.0, ALU.mult, ALU.add),
                             reads=[tXR[xs_], tLINK], writes=[tXR[xp_]])
                    else:
                        P.op("pool", f_cp(XR[:, :, xs_, 0:2], XR[:, :, xp_, T:T + 2]), reads=[tXR[xp_]], writes=[tXR[xs_]])
                        P.op("pool", f_cp(XR[:, :, xp_, T + 2:T + 3], XR[:, :, xs_, 2:3]), reads=[tXR[xs_]], writes=[tXR[xp_]])
                if i == unt - 1:
                    P.op("pool", lambda h, o=XR[:, :, xs_, T + 2:T + 3]: h.memset(o, 0.0), writes=[tXR[xs_]])
                for m in range(4):
                    mb = mmbank()
                    for c in range(8):
                        P.op("pe", f_mm(PSF[mb][:, :], WIN[:, c, 1280 + m * 128:1280 + (m + 1) * 128], HN[:, c, :], c == 0, c == 7),
                             reads=[tWIN, tHN[c]], writes=[tPSF[mb]])
                    P.op("act", f_act(GG[s][:, m, :], PSF[mb][:, :], AF.Gelu_apprx_tanh), reads=[tPSF[mb]], writes=[tGG[s]])
                    yield
                P.dma("sp", f_dma(dview(gg_d, t0), GG[s][:, :, :]), "ggo", reads=[tGG[s]], writes=[TGG[gi]])

            for i in range(unt + 2):
                if i + 1 < unt:
                    xload(i + 1)
                gens = []
                if i < unt:
                    gens.append(proj(i))
                if 0 <= i - 2 < unt:
                    gens.append(lruG(i - 2))
                if i + 1 < unt:
                    gens.append(ngen_for(i + 1))
                interleave(gens)
                if 1 <= i <= unt:
                    mix(i - 1, None)
        P.barrier()

        ar.reset(base_mark)
        WG = ar.alloc("wg", [128, 16, 128], BF16)
        WOUT = ar.alloc("wout", [128, 8, D], BF16)
        tWG, tWOUT = Trk("wg"), Trk("wout")
        P.dma("pool", f_dma(WG[:], wg_d[l].rearrange("j p m -> p j m")), "wg", writes=[tWG])
        wsrc = wout_d[l].rearrange("(c p) n -> p c n", p=128)
        for q2 in range(2):
            P.dma("pool", f_dma(WOUT[:, :, q2 * 512:(q2 + 1) * 512], wsrc[:, :, q2 * 512:(q2 + 1) * 512]),
                  "wout%d" % q2, writes=[tWOUT])
        XTb = [ar.alloc("xt", [128, 8, T], F32) for _ in range(2)]
        tXb = [trks("x", 8) for _ in range(2)]
        XCi = [ar.alloc("xci", [128, 4, T], F32) for _ in range(2)]
        tXCi = [trks("xci", 4) for _ in range(2)]
        HFi = [ar.alloc("hfi", [128, 4, T], F32) for _ in range(2)]
        tHFi = trks("hfi", 2)
        GGi = [ar.alloc("ggi", [128, 4, T], BF16) for _ in range(2)]
        tGGi = trks("ggi", 2)
        ANi = [ar.alloc("ani", [128, 4, T], BF16) for _ in range(2)]
        tANi = trks("ani", 2)
        XCB = ar.alloc("xcb", [128, 4, T], BF16)
        tXCB = trks("xcb", 4)
        GB, tGB = alloc_gb()
        HB = [ar.alloc("hb", [128, 4, T], F32) for _ in range(2)]
        tHB = [trks("hb", 4) for _ in range(2)]
        LR = ar.alloc("lr", [128, 4, T], F32)
        tLR = trks("lr", 4)
        SQ = ar.alloc("sq", [128, 4, T], BF16)
        tSQ = Trk("sq")
        LN = ar.alloc("ln", [128, 4, T], BF16)
        tLN = trks("ln", 4)
        TMPb = ar.alloc("tmp", [128, T], F32)
        tTMPb = Trk("tmp")
        RSb = ar.alloc("rs", [128, T], F32)
        tRSb = Trk("rs")
        for (ut0, unt, ulink) in UNITS:
            rr = [0]

            def bank6():
                b = rr[0] % 6
                rr[0] += 1
                return b

            def frontB(idx):
                j = unt - 1 - idx
                gi = ut0 + j
                t0 = gi * T
                s = idx % 2
                P.dma("sp", f_dma(XCi[s][:, :, :], dview(xc_d, t0)), "xci%d" % s, reads=[TXC[gi]], writes=tXCi[s])
                P.dma("sp", f_dma(HFi[s][:, :, :], dview(hf_d, t0)), "hfi%d" % s, reads=[THF[gi]], writes=[tHFi[s]])
                P.dma("sp", f_dma(GGi[s][:, :, :], dview(gg_d, t0)), "ggi%d" % s, reads=[TGG[gi]], writes=[tGGi[s]])
                P.dma("sp", f_dma(ANi[s][:, :, :], dview(an_d, t0)), "ani%d" % s, reads=[TAN[gi]], writes=[tANi[s]])
                P.dma("sp", f_dma(XTb[s][:, :, :], dview(xin_d, t0)), "xtb%d" % s, reads=[xin_t[gi]], writes=tXb[s])
                for c in range(4):
                    P.op("pool", f_cp(XCB[:, c, :], XCi[s][:, c, :]), reads=[tXCi[s][c]], writes=[tXCB[c]])
                yield
                yield from gates_gen(l, 1, XCi[s], tXCi[s], XCB, tXCB, WG, tWG, GB, tGB, bank6)
                for c in range(4):
                    if ulink is not None and j == ulink - 1:
                        P.op("dve", f_ts(GB["RA"][c][:, T - 1:T], GB["RA"][c][:, T - 1:T], LINK[:, 0:1], None, ALU.mult),
                             reads=[tGB["RA"][c], tLINK], writes=[tGB["RA"][c]])
                    init = 0.0 if idx == 0 else HB[1 - s][:, c, 0:1]
                    rd = [tGB["RA"][c], tGB["IU"][c]] + ([] if idx == 0 else [tHB[1 - s][c]])
                    P.op("dve", f_scan(HB[s][:, c, ::-1], GB["RA"][c][:, ::-1], GB["IU"][c][:, ::-1], init),
                         reads=rd, writes=[tHB[s][c]])
                    if c % 2 == 1:
                        yield

            def backB(idx):
                j = unt - 1 - idx
                gi = ut0 + j
                seg = gi // 4
                t0 = gi * T
                s = idx % 2
                for c in range(4):
                    P.op("dve", f_tt(LR[:, c, :], HFi[s][:, c, :], HB[s][:, c, :], ALU.add),
                         reads=[tHFi[s], tHB[s][c]], writes=[tLR[c]])
                    P.op("dve", f_tt(LR[:, c, :], LR[:, c, :], GGi[s][:, c, :], ALU.mult),
                         reads=[tLR[c], tGGi[s]], writes=[tLR[c]])
                    if c % 2 == 1:
                        yield
                P.op("act", f_act(SQ[:, :, :], LR[:, :, :], AF.Square), reads=tLR, writes=[tSQ])
                mb = bank6()
                for c in range(4):
                    P.op("pe", f_mm(PSF[mb][:, :], ONES5[:, :], SQ[:, c, :], c == 0, c == 3), reads=[tSQ, tCONST], writes=[tPSF[mb]])
                P.op("act", f_act(TMPb[:, :], PSF[mb][:, :], AF.Ln, bias=EPSC[:, 0:1]), reads=[tPSF[mb], tCONST], writes=[tTMPb])
                P.op("act", f_act(RSb[:, :], TMPb[:, :], AF.Exp, scale=-0.5), reads=[tTMPb], writes=[tRSb])
                yield
                for c in range(4):
                    P.op("dve", f_stt(LN[:, c, :], LR[:, c, :], vec_ap(l, V_GL + c), RSb[:, :], ALU.mult, ALU.mult),
                         reads=[tLR[c], tVEC, tRSb], writes=[tLN[c]])
                yield
                for m in range(8):
                    mb = bank6()
                    for c in range(4):
                        P.op("pe", f_mm(PSF[mb][:, :], WOUT[:, c, m * 128:(m + 1) * 128], ANi[s][:, c, :], c == 0, False),
                             reads=[tWOUT, tANi[s]], writes=[tPSF[mb]])
                    for c in range(4):
                        P.op("pe", f_mm(PSF[mb][:, :], WOUT[:, 4 + c, m * 128:(m + 1) * 128], LN[:, c, :], False, c == 3),
                             reads=[tWOUT, tLN[c]], writes=[tPSF[mb]])
                    P.op("dve", f_stt(XTb[s][:, m, :], PSF[mb][:, :], mod_ap(l, 2, m, seg), XTb[s][:, m, :], ALU.mult, ALU.add),
                         reads=[tPSF[mb], tMOD, tXb[s][m]], writes=[tXb[s][m]])
                    if m % 2 == 1:
                        yield
                P.dma("sp", f_dma(dview(xb_d, t0), XTb[s][:, :, :]), "xbo%d" % s, reads=tXb[s], writes=[TX["xb"][gi]])

            interleave([frontB(0)])
            for idx in range(unt):
                gens = [backB(idx)]
                if idx + 1 < unt:
                    gens.append(frontB(idx + 1))
                interleave(gens)
        P.barrier()

        ar.reset(base_mark)
        WFI = ar.alloc("wfi", [128, 8, 2 * DFF], BF16)
        WFO = ar.alloc("wfo", [128, 22, D], BF16)
        tWFI = trks("wfi", 8)
        tWFO = trks("wfo", 4)
        JB = (0, 6, 12, 17, 22)
        wsrc = wfi_d[l].rearrange("(c p) n -> p c n", p=128)
        for q8 in range(8):
            P.dma("pool", f_dma(WFI[:, :, q8 * 704:(q8 + 1) * 704], wsrc[:, :, q8 * 704:(q8 + 1) * 704]),
                  "wfi%d" % q8, writes=[tWFI[q8]])
        wsrc = wfo_d[l].rearrange("(c p) n -> p c n", p=128)
        for q4 in range(4):
            P.dma("pool", f_dma(WFO[:, JB[q4]:JB[q4 + 1], :], wsrc[:, JB[q4]:JB[q4 + 1], :]),
                  "wfo%d" % q4, writes=[tWFO[q4]])
        XTf = [ar.alloc("xt", [128, 8, T], F32) for _ in range(2)]
        tXf = [trks("x", 8) for _ in range(2)]
        HN = ar.alloc("hn", [128, 8, T], BF16)
        tHN = trks("hn", 8)
        TMP = [ar.alloc("tmp", [128, T], F32) for _ in range(3)]
        tTMP = trks("tmp", 3)
        RS = ar.alloc("rs", [128, T], F32)
        tRS = Trk("rs")
        ACTB = ar.alloc("actb", [128, 11, T], BF16)
        tACTB = trks("actb", 11)
        SG = [ar.alloc("sg", [128, T], F32) for _ in range(2)]
        tSG = trks("sg", 2)
        wfo_t = lambda jj: tWFO[0 if jj < 6 else 1 if jj < 12 else 2 if jj < 17 else 3]

        def f_load(gi):
            P.dma("sp", f_dma(XTf[gi % 2][:, :, :], dview(xb_d, gi * T)), "xtf%d" % (gi % 2), reads=[TX["xb"][gi]], writes=tXf[gi % 2])

        def f_norm(gi):
            norm_mod(XTf[gi % 2], tXf[gi % 2], HN, tHN, TMP, tTMP, RS, tRS, l, 4, 3, gi // 4, 4)

        def f_gu(gi, j0):
            for jl in range(11):
                jj = j0 + jl
                gb, ub = jj % 2, 2 + jj % 2
                for half, pb in ((0, gb), (1, ub)):
                    col = half * DFF + jj * 128
                    for c in range(8):
                        P.op("pe", f_mm(PSF[pb][:, :], WFI[:, c, col:col + 128], HN[:, c, :], c == 0, c == 7),
                             reads=[tWFI[col // 704], tWFI[(col + 127) // 704], tHN[c]], writes=[tPSF[pb]])
                P.op("act", f_act(SG[jj % 2][:, :], PSF[gb][:, :], AF.Silu), reads=[tPSF[gb]], writes=[tSG[jj % 2]])
                P.op("dve", f_tt(ACTB[:, jl, :], SG[jj % 2][:, :], PSF[ub][:, :], ALU.mult),
                     reads=[tSG[jj % 2], tPSF[ub]], writes=[tACTB[jl]])

        def f_out(gi, j0):
            XTt, tX, seg = XTf[gi % 2], tXf[gi % 2], gi // 4
            for m in range(8):
                pb = 4 + m % 2
                for jl in range(11):
                    jj = j0 + jl
                    P.op("pe", f_mm(PSF[pb][:, :], WFO[:, jj, m * 128:(m + 1) * 128], ACTB[:, jl, :], jl == 0, jl == 10),
                         reads=[wfo_t(jj), tACTB[jl]], writes=[tPSF[pb]])
                P.op("dve", f_stt(XTt[:, m, :], PSF[pb][:, :], mod_ap(l, 5, m, seg), XTt[:, m, :], ALU.mult, ALU.add),
                     reads=[tPSF[pb], tMOD, tX[m]], writes=[tX[m]])

        def f_fin(gi):
            XTt, tX, t0 = XTf[gi % 2], tXf[gi % 2], gi * T
            if l == L - 1:
                P.op("act", f_act(ACTB[:, 0:8, :], XTt[:, :, :], AF.Square), reads=tX, writes=tACTB[0:8])
                for c in range(8):
                    P.op("pe", f_mm(PSF[4][:, :], ONES[:, :], ACTB[:, c, :], c == 0, c == 7), reads=[tACTB[c], tCONST], writes=[tPSF[4]])
                P.op("act", f_act(SG[0][:, :], PSF[4][:, :], AF.Sqrt, bias=EPSC[:, 0:1]), reads=[tPSF[4], tCONST], writes=[tSG[0]])
                P.op("dve", lambda h, o=SG[1][:, :], i=SG[0][:, :]: h.reciprocal(out=o, in_=i), reads=[tSG[0]], writes=[tSG[1]])
                for c in range(8):
                    P.op("dve", f_stt(XTt[:, c, :], XTt[:, c, :], VEC[:, DEPTH * NV + c:DEPTH * NV + c + 1], SG[1][:, :],
                                      ALU.mult, ALU.mult), reads=[tX[c], tVEC, tSG[1]], writes=[tX[c]])
            o = P.dma("sp", f_dma(dview(xout_d, t0), XTt[:, :, :]), "xfo%d" % (gi % 2), reads=tX, writes=[xout_t[gi]])
            if l == L - 1:
                final_out.append(o)

        f_load(0)
        f_norm(0)
        for gi in range(NT):
            if gi + 1 < NT:
                f_load(gi + 1)
            f_gu(gi, 0)
            f_out(gi, 0)
            f_gu(gi, 11)
            if gi + 1 < NT:
                f_norm(gi + 1)
            f_out(gi, 11)
            f_fin(gi)
        P.barrier()
        xin_d, xin_t = xa_d, TX["xa"]

    P.emit(nc, final_waits=final_out)
    return nc, P


def _segments():
    segs = []
    for k in range(4):
        segs.append([("p", k, 0), ("p", k, 2048), ("s", k, 0)])
    for k in range(4):
        segs.append([("s", 4 + 3 * k + r, 0) for r in range(3)])
    return segs


def _dtab(linked):
    q = np.arange(128)[:, None]
    sidx = np.arange(384)[None, :] - 128
    dist = np.abs(q - sidx).astype(np.float32)
    base = np.where(dist <= 128, -dist, -1.0e9).astype(np.float32)
    d15 = base.copy()
    d16 = base.copy()
    if not linked:
        d15[:, 256:384] = -1.0e9
        d16[:, 0:128] = -1.0e9
    return np.concatenate([base, d15, d16], axis=1).astype(np.float32)


def _fm(v):
    v = np.asarray(v, np.float32)
    return v.reshape(-1, 128).T


_CACHE = {}


def kernel(x_prompt, x_sample, c_prompt, c_sample, w_mod, b_mod, g_norm1, w_in, sink, conv_w, conv_b,
           w_rg, b_rg, w_ig, b_ig, lam, g_attn_out, g_lru_out, w_out, g_norm2, w_ffn_in, w_ffn_out, g_final,
           _depth=DEPTH):
    f32 = np.float32
    segs = _segments()
    xs = {"p": np.asarray(x_prompt, f32), "s": np.asarray(x_sample, f32)}
    cs = {"p": np.asarray(c_prompt, f32), "s": np.asarray(c_sample, f32)}
    vecs = np.zeros((128, DEPTH * NV + 8), f32)
    for l in range(DEPTH):
        o = l * NV
        vecs[:, o + V_G1:o + V_G1 + 8] = _fm(g_norm1[l])
        vecs[:, o + V_G2:o + V_G2 + 8] = _fm(g_norm2[l])
        vecs[:, o + V_BMOD:o + V_BMOD + 48] = _fm(b_mod[l])
        for tap in range(4):
            vecs[:, o + V_CW + tap * 4:o + V_CW + tap * 4 + 4] = _fm(conv_w[l][tap])
        vecs[:, o + V_CB:o + V_CB + 4] = _fm(conv_b[l])
        for d in range(2):
            vecs[:, o + V_BRG + d * 4:o + V_BRG + d * 4 + 4] = _fm(b_rg[l][d])
            vecs[:, o + V_BIG + d * 4:o + V_BIG + d * 4 + 4] = _fm(b_ig[l][d])
            vecs[:, o + V_LAM + d * 4:o + V_LAM + d * 4 + 4] = _fm(lam[l][d])
        vecs[:, o + V_GA:o + V_GA + 4] = _fm(g_attn_out[l])
        vecs[:, o + V_GL:o + V_GL + 4] = _fm(g_lru_out[l])
        vecs[:, o + V_SINK:o + V_SINK + 8] = np.asarray(sink[l], f32)[None, :]
    vecs[:, DEPTH * NV:DEPTH * NV + 8] = _fm(g_final)
    perm = np.concatenate([np.concatenate([np.arange(c * 64, c * 64 + 64), np.arange((4 + c) * 64, (4 + c) * 64 + 64)])
                           for c in range(4)] + [np.arange(512, DIN)])
    w_in_p = np.ascontiguousarray(np.asarray(w_in, f32)[:, :, perm])
    wg = np.zeros((DEPTH, 16, 128, 128), f32)
    for l in range(DEPTH):
        for d in range(2):
            for g, w in ((0, w_rg), (1, w_ig)):
                for c in range(4):
                    j = (d * 2 + g) * 4 + c
                    wg[l, j, 0:64, 0:64] = w[l][d][2 * c]
                    wg[l, j, 64:128, 64:128] = w[l][d][2 * c + 1]
    shared = {"vecs": vecs, "w_mod": np.asarray(w_mod, f32), "w_in": w_in_p, "wg": wg,
              "w_out": np.asarray(w_out, f32), "w_ffn_in": np.asarray(w_ffn_in, f32),
              "w_ffn_out": np.asarray(w_ffn_out, f32)}
    in_maps = []
    for k in range(8):
        xt = np.empty((8, 128, NTOK), f32)
        ct = np.empty((128, 8, 3), f32)
        for si, (kind, b, t0) in enumerate(segs[k]):
            xt[:, :, si * 2048:(si + 1) * 2048] = xs[kind][b, t0:t0 + 2048, :].T.reshape(8, 128, 2048)
            ct[:, :, si] = cs[kind][b].reshape(8, 128).T
        linked = k < 4
        m = dict(shared)
        m["xT"] = xt
        m["cT"] = np.ascontiguousarray(ct.reshape(128, 24))
        m["link"] = np.full((128, 1), 1.0 if linked else 0.0, f32)
        m["dtab"] = _dtab(linked)
        in_maps.append(m)
    if _depth not in _CACHE:
        _CACHE[_depth] = build_program(_depth)[0]
    nc = _CACHE[_depth]
    res = run_bass_kernel_spmd(nc, in_maps, core_ids=list(range(8)))
    y_p = np.empty((4, 4096, D), f32)
    y_s = np.empty((16, 2048, D), f32)
    ys = {"p": y_p, "s": y_s}
    for k in range(8):
        yt = np.asarray(res.results[k]["yT"], f32)
        for si, (kind, b, t0) in enumerate(segs[k]):
            ys[kind][b, t0:t0 + 2048, :] = yt[:, :, si * 2048:(si + 1) * 2048].reshape(D, 2048).T
    return (y_p, y_s)
```

```python
import contextlib
import numpy as np
import concourse.bass as bass
import concourse.mybir as mybir
from concourse.bass_utils import run_bass_kernel_spmd

F32 = mybir.dt.float32
BF16 = mybir.dt.bfloat16
AF = mybir.ActivationFunctionType
ALU = mybir.AluOpType
AX = mybir.AxisListType

D = 1024
DEPTH = 4
NTOK = 6144
T = 512
NT = NTOK // T
DIN = 1792
DFF = 2816
EPS = 1e-6
NV = 124
V_G1, V_G2, V_BMOD, V_CW, V_CB, V_BRG, V_BIG, V_LAM, V_GA, V_GL, V_SINK = 0, 8, 16, 64, 80, 84, 92, 100, 108, 112, 116
UNITS = ((0, 8, 4), (8, 4, None))

ENGS = ("pe", "act", "dve", "pool", "sp")


class Trk:
    __slots__ = ("name", "w", "rs")

    def __init__(self, name=""):
        self.name = name
        self.w = None
        self.rs = []


def trks(name, n):
    return [Trk("%s%d" % (name, i)) for i in range(n)]


class Op:
    __slots__ = ("eng", "idx", "fn", "waits", "dma", "need_inc", "semval", "dkey")

    def __init__(self, eng, idx, fn, dma, dkey):
        self.eng = eng
        self.idx = idx
        self.fn = fn
        self.waits = []
        self.dma = dma
        self.need_inc = False
        self.semval = None
        self.dkey = dkey


class Prog:
    def __init__(self):
        self.ops = {e: [] for e in ENGS}
        self.waited = {}
        self.waited_dma = {}
        self.dma_keys = {}
        self.last_dma = {}
        self.last_comp = {}

    def op(self, eng, fn, reads=(), writes=(), dma=False, dkey=None):
        lst = self.ops[eng]
        o = Op(eng, len(lst), fn, dma, dkey)
        deps = []
        for t in reads:
            if t.w is not None:
                deps.append((t.w, True))
        for t in writes:
            if t.w is not None:
                deps.append((t.w, True))
            deps.extend((r, False) for r in t.rs)
        for d, isw in deps:
            self._add_wait(o, d, isw)
        for t in reads:
            t.rs.append(o)
        for t in writes:
            t.w = o
            t.rs = []
        lst.append(o)
        if dma:
            self.last_dma[dkey] = o
        else:
            self.last_comp[eng] = o
        return o

    def _add_wait(self, o, d, isw=True, force=False):
        if d is o:
            return
        if d.dma:
            key = (o.eng, d.dkey)
            prev = self.waited_dma.get(key)
            if prev is not None and prev >= d.semval:
                return
            self.waited_dma[key] = d.semval
            o.waits.append(d)
            return
        if d.eng == o.eng and not force:
            if o.eng == "pe":
                return
        key = (o.eng, d.eng)
        prev = self.waited.get(key, -1)
        if prev >= d.idx:
            return
        self.waited[key] = d.idx
        o.waits.append(d)
        d.need_inc = True

    def dma(self, eng, fn, dkey, reads=(), writes=()):
        self.dma_keys.setdefault(dkey, 0)
        self.dma_keys[dkey] += 16
        val = self.dma_keys[dkey]
        o = self.op(eng, fn, reads=reads, writes=writes, dma=True, dkey=dkey)
        o.semval = val
        return o

    def barrier(self):
        comp = dict(self.last_comp)
        dm = dict(self.last_dma)
        spn = self.op("sp", lambda h: h.nop())
        for d in list(comp.values()) + list(dm.values()):
            self._add_wait(spn, d, True, force=True)
        for e in ENGS:
            if e == "sp":
                continue
            n = self.op(e, lambda h: h.nop())
            self._add_wait(n, spn, True)

    def emit(self, nc, final_waits=()):
        for e in ENGS:
            c = 0
            for o in self.ops[e]:
                if o.dma:
                    continue
                if o.need_inc:
                    c += 1
                    o.semval = c
        with contextlib.ExitStack() as st:
            esem = {e: st.enter_context(nc.semaphore("S_" + e)) for e in ENGS}
            dsem = {k: st.enter_context(nc.semaphore("D_%d" % i)) for i, k in enumerate(self.dma_keys)}
            block = st.enter_context(nc.Block())
            prog = self

            def run(engname, h):
                for o in prog.ops[engname]:
                    for d in o.waits:
                        if d.dma:
                            h.wait_ge(dsem[d.dkey], d.semval)
                        else:
                            h.wait_ge(esem[d.eng], d.semval)
                    ins = o.fn(h)
                    if o.dma:
                        ins.then_inc(dsem[o.dkey], 16)
                    elif o.need_inc:
                        ins.then_inc(esem[o.eng], 1)
                if engname == "sp":
                    for o in final_waits:
                        h.wait_ge(dsem[o.dkey], o.semval)

            @block.tensor
            def _(h):
                run("pe", h)

            @block.scalar
            def _(h):
                run("act", h)

            @block.vector
            def _(h):
                run("dve", h)

            @block.gpsimd
            def _(h):
                run("pool", h)

            @block.sync
            def _(h):
                run("sp", h)


def f_mm(out, lhsT, rhs, start, stop):
    return lambda h: h.matmul(out, lhsT=lhsT, rhs=rhs, start=start, stop=stop)


def f_tr(out, in_, ident):
    return lambda h: h.transpose(out, in_, ident)


def f_act(out, in_, func, bias=None, scale=None, accum_out=None):
    kw = {}
    if bias is not None:
        kw["bias"] = bias
    if scale is not None:
        kw["scale"] = scale
    if accum_out is not None:
        kw["accum_out"] = accum_out
    return lambda h: h.activation(out=out, in_=in_, func=func, **kw)


def f_tt(out, in0, in1, op):
    return lambda h: h.tensor_tensor(out=out, in0=in0, in1=in1, op=op)


def f_ts(out, in0, s1, s2, op0, op1=None):
    if op1 is None:
        return lambda h: h.tensor_scalar(out=out, in0=in0, scalar1=s1, scalar2=None, op0=op0)
    return lambda h: h.tensor_scalar(out=out, in0=in0, scalar1=s1, scalar2=s2, op0=op0, op1=op1)


def f_stt(out, in0, scalar, in1, op0, op1):
    return lambda h: h.scalar_tensor_tensor(out=out, in0=in0, scalar=scalar, in1=in1, op0=op0, op1=op1)


def f_cp(out, in_):
    return lambda h: h.tensor_copy(out=out, in_=in_)


def f_dma(out, in_):
    return lambda h: h.dma_start(out=out, in_=in_)


def f_scan(out, d0, d1, init):
    return lambda h: h.tensor_tensor_scan(out=out, data0=d0, data1=d1, initial=init, op0=ALU.mult, op1=ALU.add)


class Arena:
    def __init__(self, nc, lo, hi):
        self.nc, self.lo, self.hi, self.cur, self.n = nc, lo, hi, lo, 0

    def reset(self, to=None):
        self.cur = self.lo if to is None else to

    def mark(self):
        return self.cur

    def alloc(self, name, shape, dt):
        esz = 4 if dt == F32 else 2
        nbytes = int(np.prod(shape[1:])) * esz
        off = (self.cur + 63) // 64 * 64
        assert off + nbytes <= self.hi, ("SBUF arena overflow", name, off, nbytes, self.hi)
        self.cur = off + nbytes
        self.n += 1
        return self.nc.alloc_sbuf_tensor_at("%s_%d" % (name, self.n), list(shape), dt, offset=off)


def build_program(depth=DEPTH):
    nc = bass.Bass("TRN2", target_bir_lowering=False)
    L = depth
    dr = lambda name, shape, dt=F32, kind="ExternalInput": nc.dram_tensor(name, list(shape), dt, kind=kind).ap()
    xT_d = dr("xT", [8, 128, NTOK])
    cT_d = dr("cT", [128, 24])
    link_d = dr("link", [128, 1])
    dtab_d = dr("dtab", [128, 3 * 384])
    vecs_d = dr("vecs", [128, DEPTH * NV + 8])
    wmod_d = dr("w_mod", [DEPTH, D, 6 * D])
    win_d = dr("w_in", [DEPTH, D, DIN])
    wg_d = dr("wg", [DEPTH, 16, 128, 128])
    wout_d = dr("w_out", [DEPTH, D, D])
    wfi_d = dr("w_ffn_in", [DEPTH, D, 2 * DFF])
    wfo_d = dr("w_ffn_out", [DEPTH, DFF, D])
    yT_d = dr("yT", [8, 128, NTOK], kind="ExternalOutput")
    xa_d = dr("xa_s", [8, 128, NTOK], kind="Internal")
    xb_d = dr("xb_s", [8, 128, NTOK], kind="Internal")
    xc_d = dr("xc_s", [4, 128, NTOK], kind="Internal")
    hf_d = dr("hf_s", [4, 128, NTOK], kind="Internal")
    gg_d = dr("gg_s", [4, 128, NTOK], BF16, kind="Internal")
    an_d = dr("an_s", [4, 128, NTOK], BF16, kind="Internal")

    def dview(d, t0, n=T):
        return d[:, :, t0:t0 + n].rearrange("c p t -> p c t")

    P = Prog()
    ar = Arena(nc, 16640, 229376)
    TX = {"in": trks("xin", NT), "xa": trks("xa", NT), "xb": trks("xb", NT), "y": trks("y", NT)}
    TXC, THF, TGG, TAN = trks("xcs", NT), trks("hfs", NT), trks("ggs", NT), trks("ans", NT)

    VEC = ar.alloc("vec", [128, DEPTH * NV + 8], F32)
    MOD = ar.alloc("mod", [128, DEPTH * 6 * 8 * 3], F32)
    C8 = ar.alloc("c8", [128, DEPTH * 8], F32)
    SK8 = ar.alloc("sk8", [128, DEPTH * 8], F32)
    HB2 = ar.alloc("hb2", [128, DEPTH * 16], F32)
    QUARTER = ar.alloc("quarter", [128, 1], F32)
    DT = ar.alloc("dtab", [128, 3 * 384], F32)
    LINK = ar.alloc("link", [128, 1], F32)
    ONES = ar.alloc("ones", [128, 128], BF16)
    ONES5 = ar.alloc("ones5", [128, 128], BF16)
    IDN = ar.alloc("idn", [128, 128], BF16)
    EPSC = ar.alloc("epsc", [128, 1], F32)
    ONEC = ar.alloc("onec", [128, 1], F32)
    NHALFC = ar.alloc("nhalfc", [128, 1], F32)
    tVEC, tMOD, tC8, tSK8, tDT, tLINK, tCONST = (Trk(n) for n in ("vec", "mod", "c8", "sk8", "dt", "link", "const"))
    base_mark = ar.mark()

    PSF = [nc.alloc_psum_tensor("psf%d" % i, [128, 512], F32) for i in range(6)]
    PSB = [nc.alloc_psum_tensor("psb%d" % i, [128, 1024], BF16) for i in range(2)]
    tPSF = trks("psf", 6)
    tPSB = trks("psb", 2)

    def mod_ap(l, k, c, seg):
        o = ((l * 6 + k) * 8 + c) * 3 + seg
        return MOD[:, o:o + 1]

    def vec_ap(l, off, n=1):
        return VEC[:, l * NV + off:l * NV + off + n]

    P.dma("sp", f_dma(VEC[:], vecs_d), "vec", writes=[tVEC])
    P.dma("sp", f_dma(DT[:], dtab_d), "dt", writes=[tDT])
    P.dma("sp", f_dma(LINK[:], link_d), "link", writes=[tLINK])
    P.op("pool", lambda h: h.memset(ONES[:], 1.0 / 1024), writes=[tCONST])
    P.op("pool", lambda h: h.memset(ONES5[:], 1.0 / 512), writes=[tCONST])
    P.op("pool", lambda h: h.memset(EPSC[:], EPS), writes=[tCONST])
    P.op("pool", lambda h: h.memset(ONEC[:], 1.0), writes=[tCONST])
    P.op("pool", lambda h: h.memset(NHALFC[:], -0.5), writes=[tCONST])
    P.op("pool", lambda h: h.memset(QUARTER[:], 0.25), writes=[tCONST])
    P.op("pool", lambda h: h.memset(IDN[:], 0.0), writes=[tCONST])
    P.op("pool", lambda h: h.affine_select(out=IDN[:], in_=IDN[:], pattern=[[-1, 128]], compare_op=ALU.not_equal,
                                            fill=1.0, base=0, channel_multiplier=1), reads=[tCONST], writes=[tCONST])
    CT = ar.alloc("ct", [128, 24], F32)
    CA = ar.alloc("ca", [128, 24], BF16)
    WM = [ar.alloc("wm", [128, 8, D], BF16) for _ in range(2)]
    tCT, tCA = Trk("ct"), Trk("ca")
    tWM = trks("wm", 2)
    P.dma("sp", f_dma(CT[:], cT_d), "ct", writes=[tCT])
    P.op("act", f_act(CA[:], CT[:], AF.Silu), reads=[tCT], writes=[tCA])
    it = 0
    for l in range(L):
        for k in range(6):
            s = it % 2
            src = wmod_d[l][:, k * D:(k + 1) * D].rearrange("(c p) n -> p c n", p=128)
            P.dma("pool", f_dma(WM[s][:], src), "wm%d" % s, writes=[tWM[s]])
            pb = it % 6
            ps = PSF[pb]
            for fc in range(8):
                for kc in range(8):
                    P.op("pe", f_mm(ps[:, fc * 3:fc * 3 + 3], WM[s][:, kc, fc * 128:(fc + 1) * 128],
                                    CA[:, kc * 3:kc * 3 + 3], kc == 0, kc == 7),
                         reads=[tWM[s], tCA], writes=[tPSF[pb]])
            o = (l * 6 + k) * 24
            for seg in range(3):
                P.op("dve", f_tt(MOD[:, o + seg:o + 24:3], ps[:, seg:24:3], vec_ap(l, V_BMOD + k * 8, 8), ALU.add),
                     reads=[tPSF[pb], tVEC], writes=[tMOD])
            it += 1
        for k, gof in ((1, V_G1), (4, V_G2)):
            o = (l * 6 + k) * 24
            for seg in range(3):
                P.op("dve", f_stt(MOD[:, o + seg:o + 24:3], MOD[:, o + seg:o + 24:3], 1.0, vec_ap(l, gof, 8),
                                  ALU.add, ALU.mult), reads=[tMOD, tVEC], writes=[tMOD])
    E1 = ar.alloc("e1", [128, DEPTH * 8], F32)
    E2 = ar.alloc("e2", [128, DEPTH * 8], F32)
    tE = Trk("e")
    for l in range(L):
        sl = slice(l * 8, l * 8 + 8)
        P.op("act", f_act(E1[:, sl], vec_ap(l, V_LAM, 8), AF.Exp, scale=-1.0), reads=[tVEC], writes=[tE])
        P.op("dve", f_ts(E2[:, sl], E1[:, sl], -0.25, 1.0 / 3, ALU.mult, ALU.add), reads=[tE], writes=[tE])
        P.op("dve", f_tt(E2[:, sl], E2[:, sl], E1[:, sl], ALU.mult), reads=[tE], writes=[tE])
        P.op("dve", f_ts(E2[:, sl], E2[:, sl], -1.0, 0.5, ALU.mult, ALU.add), reads=[tE], writes=[tE])
        P.op("dve", f_tt(E2[:, sl], E2[:, sl], E1[:, sl], ALU.mult), reads=[tE], writes=[tE])
        P.op("dve", f_ts(E2[:, sl], E2[:, sl], -1.0, 1.0, ALU.mult, ALU.add), reads=[tE], writes=[tE])
        P.op("dve", f_tt(E2[:, sl], E2[:, sl], E1[:, sl], ALU.mult), reads=[tE], writes=[tE])
        P.op("dve", f_ts(C8[:, sl], E2[:, sl], -4.0, None, ALU.mult), reads=[tE], writes=[tC8])
        P.op("dve", f_ts(HB2[:, l * 16:l * 16 + 16], vec_ap(l, V_BRG, 16), 0.5, None, ALU.mult), reads=[tVEC], writes=[tC8])
        P.op("dve", f_ts(SK8[:, sl], vec_ap(l, V_SINK, 8), 8.0, None, ALU.mult), reads=[tVEC], writes=[tSK8])
    P.barrier()
    ar.reset(base_mark)

    def norm_gen(XTt, tX, HN, tHN, TMP, tTMP, RS, tRS, l, kg, ksh, seg, mmb):
        P.op("act", f_act(HN[:, :, :], XTt[:, :, :], AF.Square), reads=tX, writes=tHN)
        yield
        ps = PSF[mmb]
        for c in range(8):
            P.op("pe", f_mm(ps[:, :], ONES[:, :], HN[:, c, :], c == 0, c == 7), reads=[tHN[c], tCONST], writes=[tPSF[mmb]])
        P.op("act", f_act(TMP[0][:, :], ps[:, :], AF.Sqrt, bias=EPSC[:, 0:1]), reads=[tPSF[mmb], tCONST], writes=[tTMP[0]])
        P.op("dve", lambda h, o=RS[:, :], i=TMP[0][:, :]: h.reciprocal(out=o, in_=i), reads=[tTMP[0]], writes=[tRS])
        yield
        for c in range(8):
            b = 1 + (c % 2)
            P.op("dve", f_stt(TMP[b][:, :], XTt[:, c, :], mod_ap(l, kg, c, seg), RS[:, :], ALU.mult, ALU.mult),
                 reads=[tX[c], tMOD, tRS], writes=[tTMP[b]])
            P.op("act", f_act(HN[:, c, :], TMP[b][:, :], AF.Identity, bias=mod_ap(l, ksh, c, seg)),
                 reads=[tTMP[b], tMOD], writes=[tHN[c]])
            if c % 2 == 1:
                yield

    def norm_mod(*a):
        for _ in norm_gen(*a):
            pass

    def interleave(gens):
        gens = list(gens)
        while gens:
            for g in list(gens):
                try:
                    next(g)
                except StopIteration:
                    gens.remove(g)

    def g1(l, d, c, XCB, tXCB, WG, tWG, GB, tGB, bankfn):
        for g, nm in ((0, "RA"), (1, "IU")):
            mb = bankfn()
            hb = HB2[:, l * 16 + g * 8 + d * 4 + c:l * 16 + g * 8 + d * 4 + c + 1]
            P.op("pe", f_mm(PSF[mb][:, :], WG[:, (d * 2 + g) * 4 + c, :], XCB[:, c, :], True, True),
                 reads=[tWG, tXCB[c]], writes=[tPSF[mb]])
            P.op("act", f_act(GB[nm][c][:, :], PSF[mb][:, :], AF.Tanh, bias=hb, scale=0.5),
                 reads=[tPSF[mb], tC8], writes=[tGB[nm][c]])
        hc8 = C8[:, l * 8 + d * 4 + c:l * 8 + d * 4 + c + 1]
        P.op("act", f_act(GB["RA"][c][:, :], GB["RA"][c][:, :], AF.Exp, bias=hc8, scale=hc8),
             reads=[tGB["RA"][c], tC8], writes=[tGB["RA"][c]])
        P.op("act", f_act(GB["S"][c][:, :], GB["RA"][c][:, :], AF.Square), reads=[tGB["RA"][c]], writes=[tGB["S"][c]])

    def g2(c, XC, tXC, GB, tGB):
        P.op("dve", f_ts(GB["S"][c][:, :], GB["S"][c][:, :], 0.99999994, -1.0, ALU.min, ALU.mult),
             reads=[tGB["S"][c]], writes=[tGB["S"][c]])
        P.op("dve", f_stt(GB["IU"][c][:, :], GB["IU"][c][:, :], 1.0, XC[:, c, :], ALU.add, ALU.mult),
             reads=[tGB["IU"][c], tXC[c]], writes=[tGB["IU"][c]])

    def gsq(c, GB, tGB):
        P.op("act", f_act(GB["S"][c][:, :], GB["S"][c][:, :], AF.Sqrt, bias=QUARTER[:, 0:1], scale=0.25),
             reads=[tGB["S"][c], tCONST], writes=[tGB["S"][c]])

    def g3(c, GB, tGB):
        P.op("dve", f_tt(GB["IU"][c][:, :], GB["IU"][c][:, :], GB["S"][c][:, :], ALU.mult),
             reads=[tGB["IU"][c], tGB["S"][c]], writes=[tGB["IU"][c]])

    def gates_gen(l, d, XC, tXC, XCB, tXCB, WG, tWG, GB, tGB, bankfn):
        for c in range(4):
            g1(l, d, c, XCB, tXCB, WG, tWG, GB, tGB, bankfn)
            yield
        for c in range(4):
            g2(c, XC, tXC, GB, tGB)
        yield
        for c in range(4):
            gsq(c, GB, tGB)
        yield
        for c in range(4):
            g3(c, GB, tGB)
        yield

    def alloc_gb():
        GB = {n: [ar.alloc("gb" + n, [128, T], F32) for _ in range(4)] for n in ("RA", "IU", "S")}
        tGB = {n: trks("gb" + n, 4) for n in ("RA", "IU", "S")}
        return GB, tGB

    final_out = []
    xin_d, xin_t = xT_d, TX["in"]
    for l in range(L):
        xout_d, xout_t = (yT_d, TX["y"]) if l == L - 1 else (xa_d, TX["xa"])
        ar.reset(base_mark)
        WIN = ar.alloc("win", [128, 8, DIN], BF16)
        WG = ar.alloc("wg", [128, 16, 128], BF16)
        tWIN, tWG = Trk("win"), Trk("wg")
        wsrc = win_d[l].rearrange("(c p) n -> p c n", p=128)
        for q4 in range(4):
            P.dma("pool", f_dma(WIN[:, :, q4 * 448:(q4 + 1) * 448], wsrc[:, :, q4 * 448:(q4 + 1) * 448]),
                  "win%d" % q4, writes=[tWIN])
        P.dma("pool", f_dma(WG[:], wg_d[l].rearrange("j p m -> p j m")), "wg", writes=[tWG])
        XTt = ar.alloc("xt", [128, 8, T], F32)
        tX = trks("x", 8)
        HN = ar.alloc("hn", [128, 8, T], BF16)
        tHN = trks("hn", 8)
        TMP = [ar.alloc("tmp", [128, T], F32) for _ in range(3)]
        tTMP = trks("tmp", 3)
        RS = ar.alloc("rs", [128, T], F32)
        tRS = Trk("rs")
        S = 8 * T
        KU = ar.alloc("ku", [128, S], BF16)
        tKU = trks("ku", 8)
        VU = ar.alloc("vu", [128, 32, 128], BF16)
        tVU = trks("vu", 8)
        QR = ar.alloc("qr", [128, 4, 2, T], BF16)
        tQR = trks("qr", 2)
        XR = ar.alloc("xr", [128, 4, 3, T + 3], F32)
        tXR = trks("xr", 3)
        XC = [ar.alloc("xc", [128, 4, T], F32)] * 2
        tXCs = [trks("xc", 4)] * 2
        CARRY = ar.alloc("carry", [128, 4], F32)
        tCARRY = trks("carry", 4)
        XCB = ar.alloc("xcb", [128, 4, T], BF16)
        tXCB = trks("xcb", 4)
        GB, tGB = alloc_gb()
        HF = [ar.alloc("hf", [128, 4, T], F32)] * 2
        tHF = [trks("hf", 4)] * 2
        GG = [ar.alloc("gg", [128, 4, T], BF16)] * 2
        tGG = [Trk("gg")] * 2
        ANT = [ar.alloc("ant", [128, 4, T], BF16) for _ in range(2)]
        tANT = trks("ant", 2)
        TT = [ar.alloc("tt", [128, 384], F32) for _ in range(4)]
        tTT = trks("tt", 4)
        PM = [ar.alloc("pm", [128, 384], BF16) for _ in range(4)]
        tPM = trks("pm", 4)
        PT = [ar.alloc("pt", [128, 3, 128], BF16) for _ in range(4)]
        tPT = trks("pt", 4)
        ST = ar.alloc("st", [128, 128], F32)
        tSTp = [{n: Trk("st" + n) for n in ("es", "rden", "ssq", "rsa")} for _ in range(2)]
        tMXp, tNEGBp, tRSUMp = [trks("mx", 8) for _ in range(2)], [trks("negb", 8) for _ in range(2)], [trks("rsum", 8) for _ in range(2)]
        tPSBh = trks("psbh", 2)
        ATT = ar.alloc("att", [128, 512], F32)
        tATT = Trk("att")
        ATN = ar.alloc("atn", [128, 512], BF16)
        tATN = Trk("atn")
        JUNK = ar.alloc("junk", [128, 512], BF16)
        CTMP = ar.alloc("ctmp", [128, T], F32)
        tCTMP = Trk("ctmp")
        tJUNK = Trk("junk")
        for (ut0, unt, ulink) in UNITS:
            mmrr = [0]

            def mmbank():
                b = mmrr[0] % 3
                mmrr[0] += 1
                return b

            gbank = [0]

            def gatebank():
                b = 3 + gbank[0] % 2
                gbank[0] += 1
                return b

            hcnt = [0]

            def lru_conv(j):
                s = j % 3
                for c in range(4):
                    cw = lambda tap: vec_ap(l, V_CW + tap * 4 + c)
                    P.op("pool", f_ts(XC[0][:, c, :], XR[:, c, s, 0:T], cw(0), vec_ap(l, V_CB + c), ALU.mult, ALU.add),
                         reads=[tXR[s], tVEC], writes=[tXCs[0][c]])
                    for tap in (1, 2, 3):
                        P.op("pool", f_ts(CTMP[:, :], XR[:, c, s, tap:tap + T], cw(tap), 0.0, ALU.mult, ALU.add),
                             reads=[tXR[s], tVEC], writes=[tCTMP])
                        P.op("pool", f_tt(XC[0][:, c, :], XC[0][:, c, :], CTMP[:, :], ALU.add),
                             reads=[tCTMP, tXCs[0][c]], writes=[tXCs[0][c]])
                    P.op("pool", f_cp(XCB[:, c, :], XC[0][:, c, :]), reads=[tXCs[0][c]], writes=[tXCB[c]])

            def lruG(j):
                gi = ut0 + j
                for c in range(4):
                    g1(l, 0, c, XCB, tXCB, WG, tWG, GB, tGB, gatebank)
                    yield
                for c in range(4):
                    g2(c, XC[0], tXCs[0], GB, tGB)
                yield
                for c in range(4):
                    gsq(c, GB, tGB)
                yield
                for c in range(4):
                    g3(c, GB, tGB)
                    if ulink is not None and j == ulink:
                        P.op("dve", f_ts(GB["RA"][c][:, 0:1], GB["RA"][c][:, 0:1], LINK[:, 0:1], None, ALU.mult),
                             reads=[tGB["RA"][c], tLINK], writes=[tGB["RA"][c]])
                    init = 0.0 if j == 0 else CARRY[:, c:c + 1]
                    rd = [tGB["RA"][c], tGB["IU"][c]] + ([] if j == 0 else [tCARRY[c]])
                    P.op("dve", f_scan(HF[0][:, c, :], GB["RA"][c][:, :], GB["IU"][c][:, :], init), reads=rd, writes=[tHF[0][c]])
                    P.op("dve", f_cp(CARRY[:, c:c + 1], HF[0][:, c, T - 1:T]), reads=[tHF[0][c]], writes=[tCARRY[c]])
                    yield
                t0 = gi * T
                P.dma("sp", f_dma(dview(xc_d, t0), XC[0][:, :, :]), "xco", reads=tXCs[0], writes=[TXC[gi]])
                P.dma("sp", f_dma(dview(hf_d, t0), HF[0][:, :, :]), "hfo", reads=tHF[0], writes=[THF[gi]])

            def mix(j, ngen=None):
                gi = ut0 + j
                qs = j % 2
                asl = j % 2
                nblk = unt * 4
                lru_conv(j)

                def lru_step(p):
                    pass

                def geom(b):
                    n = 4 * j + b
                    lo = max(n - 1, 0)
                    hi = min(n + 1, nblk - 1)
                    nkb = hi - lo + 1
                    doff = 0 if lo == n - 1 else 128
                    tab = 0
                    if ulink is not None and n == ulink * 4 - 1:
                        tab = 1
                    if ulink is not None and n == ulink * 4:
                        tab = 2
                    tl = sorted(set((lo // 4, hi // 4)))
                    return n, lo, hi, nkb, doff, tab, [tKU[tt] for tt in tl], [tVU[tt] for tt in tl]

                sbank = {}

                def SA(p):
                    b, c = p // 4, p % 4
                    n, lo, hi, nkb, doff, tab, ktr, vtr = geom(b)
                    nk = nkb * 128
                    for hh in range(2):
                        sb = hcnt[0] % 3
                        hcnt[0] += 1
                        sbank[(p, hh)] = sb
                        pl, ph = hh * 64, hh * 64 + 64
                        P.op("pe", f_mm(PSF[sb][:, 0:nk], QR[pl:ph, c, qs, b * 128:(b + 1) * 128],
                                        KU[pl:ph, lo * 128:(hi + 1) * 128], True, True),
                             reads=[tQR[qs]] + ktr, writes=[tPSF[sb]])

                def SB(p, hh):
                    b, c = p // 4, p % 4
                    n, lo, hi, nkb, doff, tab, ktr, vtr = geom(b)
                    nk = nkb * 128
                    so = (b % 2) * 64
                    Dv = DT[:, tab * 384 + doff:tab * 384 + doff + nk]
                    if True:
                        h_ = c + 4 * hh
                        sb = sbank[(p, hh)]
                        r4 = (c * 2 + hh) % 4
                        slope8 = 8.0 * 2.0 ** (-(h_ + 1))
                        P.op("dve", f_stt(TT[r4][:, 0:nk], Dv, slope8, PSF[sb][:, 0:nk], ALU.mult, ALU.add),
                             reads=[tDT, tPSF[sb]], writes=[tTT[r4]])
                        P.op("dve", lambda h, o=ST[:, so + h_:so + h_ + 1], i=TT[r4][:, 0:nk]: h.reduce_max(out=o, in_=i, axis=AX.X),
                             reads=[tTT[r4]], writes=[tMXp[b % 2][h_]])
                        P.op("dve", f_ts(ST[:, so + 8 + h_:so + 9 + h_], ST[:, so + h_:so + h_ + 1],
                                         SK8[:, l * 8 + h_:l * 8 + h_ + 1], -0.125, ALU.max, ALU.mult),
                             reads=[tMXp[b % 2][h_], tSK8], writes=[tNEGBp[b % 2][h_]])
                        P.op("act", f_act(PM[r4][:, 0:nk], TT[r4][:, 0:nk], AF.Exp, bias=ST[:, so + 8 + h_:so + 9 + h_], scale=0.125,
                                          accum_out=ST[:, so + 16 + h_:so + 17 + h_]),
                             reads=[tTT[r4], tNEGBp[b % 2][h_]], writes=[tPM[r4], tRSUMp[b % 2][h_]])

                def SC(p):
                    b, c = p // 4, p % 4
                    n, lo, hi, nkb, doff, tab, ktr, vtr = geom(b)
                    nk = nkb * 128
                    for hh in range(2):
                        r4 = (c * 2 + hh) % 4
                        pbk = hh
                        for jj in range(nkb):
                            P.op("pe", f_tr(PSB[pbk][:, jj * 128:(jj + 1) * 128],
                                            PM[r4][:, jj * 128:(jj + 1) * 128], IDN[:, :]),
                                 reads=[tPM[r4], tCONST], writes=[tPSB[pbk]])
                        P.op("act", f_act(PT[r4][:, 0:nkb, :],
                                          PSB[pbk][:, 0:nk].rearrange("p (j q) -> p j q", q=128), AF.Copy),
                             reads=[tPSB[pbk]], writes=[tPT[r4]])

                def SD(p):
                    b, c = p // 4, p % 4
                    n, lo, hi, nkb, doff, tab, ktr, vtr = geom(b)
                    for hh in range(2):
                        h_ = c + 4 * hh
                        r4 = (c * 2 + hh) % 4
                        for jj in range(nkb):
                            P.op("pe", f_mm(PSF[5][:, h_ * 64:(h_ + 1) * 64], PT[r4][:, jj, :],
                                            VU[:, lo + jj, hh * 64:(hh + 1) * 64], jj == 0, jj == nkb - 1),
                                 reads=[tPT[r4]] + vtr, writes=[tPSF[5]])

                def EPI(b, k):
                    so = (b % 2) * 64
                    tST = tSTp[b % 2]
                    if k == 0:
                        P.op("dve", f_tt(ST[:, so + 24:so + 32], ST[:, so + 8:so + 16], vec_ap(l, V_SINK, 8), ALU.add),
                             reads=tNEGBp[b % 2] + [tVEC], writes=[tST["es"]])
                        P.op("act", f_act(ST[:, so + 24:so + 32], ST[:, so + 24:so + 32], AF.Exp), reads=[tST["es"]], writes=[tST["es"]])
                    elif k == 1:
                        P.op("dve", f_tt(ST[:, so + 32:so + 40], ST[:, so + 24:so + 32], ST[:, so + 16:so + 24], ALU.add),
                             reads=[tST["es"]] + tRSUMp[b % 2], writes=[tST["rden"]])
                        P.op("dve", lambda h, o=ST[:, so + 32:so + 40]: h.reciprocal(out=o, in_=o), reads=[tST["rden"]], writes=[tST["rden"]])
                        P.op("dve", f_tt(ATT[:, :].rearrange("p (h d) -> p h d", d=64),
                                         PSF[5][:, :].rearrange("p (h d) -> p h d", d=64),
                                         ST[:, so + 32:so + 40].unsqueeze(2).to_broadcast([128, 8, 64]), ALU.mult),
                             reads=[tPSF[5], tST["rden"]], writes=[tATT])
                    elif k == 2:
                        P.op("act", f_act(JUNK[:, :], ATT[:, :], AF.Square, accum_out=ST[:, so + 40:so + 41]),
                             reads=[tATT], writes=[tJUNK, tST["ssq"]])
                    elif k == 3:
                        P.op("dve", f_ts(ST[:, so + 41:so + 42], ST[:, so + 40:so + 41], 1.0 / 512, EPS, ALU.mult, ALU.add),
                             reads=[tST["ssq"]], writes=[tST["rsa"]])
                        P.op("pool", f_tt(ST[:, so + 41:so + 42], ST[:, so + 41:so + 42], NHALFC[:, 0:1], ALU.pow),
                             reads=[tST["rsa"], tCONST], writes=[tST["rsa"]])
                    elif k == 4:
                        P.op("dve", f_ts(ATN[:, :], ATT[:, :], ST[:, so + 41:so + 42], None, ALU.mult),
                             reads=[tATT, tST["rsa"]], writes=[tATN])
                    elif k == 5:
                        for cc in range(4):
                            P.op("pe", f_tr(PSB[1][:, 512 + cc * 128:512 + (cc + 1) * 128], ATN[:, cc * 128:(cc + 1) * 128], IDN[:, :]),
                                 reads=[tATN, tCONST], writes=[tPSB[1]])
                        for cc in range(4):
                            P.op("act", f_act(ANT[asl][:, cc, b * 128:(b + 1) * 128], PSB[1][:, 512 + cc * 128:512 + (cc + 1) * 128],
                                              AF.Identity, scale=vec_ap(l, V_GA + cc)),
                                 reads=[tPSB[1], tVEC], writes=[tANT[asl]])

                NP = 16
                SA(0)
                for p in range(NP + 6):
                    if p < NP:
                        SB(p, 0)
                    if p + 1 < NP:
                        SA(p + 1)
                    if p < NP:
                        SB(p, 1)
                    if 0 <= p - 2 < NP:
                        SD(p - 2)
                    for b_ in range(4):
                        k_ = p - (4 * b_ + 4)
                        if 0 <= k_ < 6:
                            EPI(b_, k_)
                    if 0 <= p - 1 < NP:
                        SC(p - 1)
                    lru_step(p)
                    if ngen is not None and p >= 2:
                        next(ngen, None)
                if ngen is not None:
                    for _ in ngen:
                        pass
                P.dma("sp", f_dma(dview(an_d, gi * T), ANT[asl][:, :, :]), "ano%d" % asl, reads=[tANT[asl]], writes=[TAN[gi]])

            def xload(i):
                gi = ut0 + i
                P.dma("sp", f_dma(XTt[:, :, :], dview(xin_d, gi * T)), "xt", reads=[xin_t[gi]], writes=tX)

            def ngen_for(i):
                return norm_gen(XTt, tX, HN, tHN, TMP, tTMP, RS, tRS, l, 1, 0, (ut0 + i) // 4, gatebank())

            xload(0)
            for _ in ngen_for(0):
                pass
            def proj(i):
                gi = ut0 + i
                seg = gi // 4
                t0 = gi * T
                s = i % 2
                xs_ = i % 3
                xp_ = (i - 1) % 3
                for m in range(4):
                    mb = mmbank()
                    for c in range(8):
                        P.op("pe", f_mm(PSF[mb][:, :], WIN[:, c, m * 128:(m + 1) * 128], HN[:, c, :], c == 0, c == 7),
                             reads=[tWIN, tHN[c]], writes=[tPSF[mb]])
                    P.op("act", f_act(QR[:, m, s, :], PSF[mb][:, :], AF.Copy), reads=[tPSF[mb]], writes=[tQR[s]])
                    yield
                mb = mmbank()
                for c in range(8):
                    P.op("pe", f_mm(PSF[mb][:, :], WIN[:, c, 512:640], HN[:, c, :], c == 0, c == 7),
                         reads=[tWIN, tHN[c]], writes=[tPSF[mb]])
                P.op("dve", f_cp(KU[:, i * T:(i + 1) * T], PSF[mb][:, :]), reads=[tPSF[mb]], writes=[tKU[i]])
                yield
                mb = mmbank()
                for b in range(4):
                    for c in range(8):
                        P.op("pe", f_mm(PSF[mb][:, b * 128:(b + 1) * 128], HN[:, c, b * 128:(b + 1) * 128],
                                        WIN[:, c, 640:768], c == 0, c == 7),
                             reads=[tWIN, tHN[c]], writes=[tPSF[mb]])
                P.op("act", f_act(VU[:, i * 4:(i + 1) * 4, :], PSF[mb][:, :].rearrange("p (b f) -> p b f", f=128), AF.Copy),
                     reads=[tPSF[mb]], writes=[tVU[i]])
                yield
                for m in range(4):
                    mb = mmbank()
                    for c in range(8):
                        P.op("pe", f_mm(PSF[mb][:, :], WIN[:, c, 768 + m * 128:768 + (m + 1) * 128], HN[:, c, :], c == 0, c == 7),
                             reads=[tWIN, tHN[c]], writes=[tPSF[mb]])
                    P.op("dve", f_cp(XR[:, m, xs_, 2:T + 2], PSF[mb][:, :]), reads=[tPSF[mb]], writes=[tXR[xs_]])
                    yield
                linked = (ulink is not None and i == ulink)
                if i == 0:
                    P.op("pool", lambda h, o=XR[:, :, xs_, 0:2]: h.memset(o, 0.0), writes=[tXR[xs_]])
                else:
                    if linked:
                        P.op("pool", f_ts(XR[:, :, xs_, 0:2], XR[:, :, xp_, T:T + 2], LINK[:, 0:1], 0.0, ALU.mult, ALU.add),
                             reads=[tXR[xp_], tLINK], writes=[tXR[xs_]])
                        P.op("pool", f_ts(XR[:, :, xp_, T + 2:T + 3], XR[:, :, xs_, 2:3], LINK[:, 0:1], 0.0, ALU.mult, ALU.add),
                             reads=[tXR[xs_], tLINK], writes=[tXR[xp_]])
                    else:
                        P.op("pool", f_cp(XR[:, :, xs_, 0:2], XR[:, :, xp_, T:T + 2]), reads=[tXR[xp_]], writes=[tXR[xs_]])
                        P.op("pool", f_cp(XR[:, :, xp_, T + 2:T + 3], XR[:, :, xs_, 2:3]), reads=[tXR[xs_]], writes=[tXR[xp_]])
                if i == unt - 1:
                    P.op("pool", lambda h, o=XR[:, :, xs_, T + 2:T + 3]: h.memset(o, 0.0), writes=[tXR[xs_]])
                for m in range(4):
                    mb = mmbank()
                    for c in range(8):
                        P.op("pe", f_mm(PSF[mb][:, :], WIN[:, c, 1280 + m * 128:1280 + (m + 1) * 128], HN[:, c, :], c == 0, c == 7),
                             reads=[tWIN, tHN[c]], writes=[tPSF[mb]])
                    P.op("act", f_act(GG[s][:, m, :], PSF[mb][:, :], AF.Gelu_apprx_tanh), reads=[tPSF[mb]], writes=[tGG[s]])
                    yield
                P.dma("sp", f_dma(dview(gg_d, t0), GG[s][:, :, :]), "ggo", reads=[tGG[s]], writes=[TGG[gi]])

            for i in range(unt + 2):
                if i + 1 < unt:
                    xload(i + 1)
                gens = []
                if i < unt:
                    gens.append(proj(i))
                if 0 <= i - 2 < unt:
                    gens.append(lruG(i - 2))
                interleave(gens)
                ng = ngen_for(i + 1) if i + 1 < unt else None
                if 1 <= i <= unt:
                    mix(i - 1, ng)
                elif ng is not None:
                    for _ in ng:
                        pass
        P.barrier()

        ar.reset(base_mark)
        WG = ar.alloc("wg", [128, 16, 128], BF16)
        WOUT = ar.alloc("wout", [128, 8, D], BF16)
        tWG, tWOUT = Trk("wg"), Trk("wout")
        P.dma("pool", f_dma(WG[:], wg_d[l].rearrange("j p m -> p j m")), "wg", writes=[tWG])
        wsrc = wout_d[l].rearrange("(c p) n -> p c n", p=128)
        for q2 in range(2):
            P.dma("pool", f_dma(WOUT[:, :, q2 * 512:(q2 + 1) * 512], wsrc[:, :, q2 * 512:(q2 + 1) * 512]),
                  "wout%d" % q2, writes=[tWOUT])
        XTb = [ar.alloc("xt", [128, 8, T], F32) for _ in range(2)]
        tXb = [trks("x", 8) for _ in range(2)]
        XCi = [ar.alloc("xci", [128, 4, T], F32) for _ in range(2)]
        tXCi = [trks("xci", 4) for _ in range(2)]
        HFi = [ar.alloc("hfi", [128, 4, T], F32) for _ in range(2)]
        tHFi = trks("hfi", 2)
        GGi = [ar.alloc("ggi", [128, 4, T], BF16) for _ in range(2)]
        tGGi = trks("ggi", 2)
        ANi = [ar.alloc("ani", [128, 4, T], BF16) for _ in range(2)]
        tANi = trks("ani", 2)
        XCB = ar.alloc("xcb", [128, 4, T], BF16)
        tXCB = trks("xcb", 4)
        GB, tGB = alloc_gb()
        HB = [ar.alloc("hb", [128, 4, T], F32) for _ in range(2)]
        tHB = [trks("hb", 4) for _ in range(2)]
        LR = ar.alloc("lr", [128, 4, T], F32)
        tLR = trks("lr", 4)
        SQ = ar.alloc("sq", [128, 4, T], BF16)
        tSQ = Trk("sq")
        LN = ar.alloc("ln", [128, 4, T], BF16)
        tLN = trks("ln", 4)
        TMPb = ar.alloc("tmp", [128, T], F32)
        tTMPb = Trk("tmp")
        RSb = ar.alloc("rs", [128, T], F32)
        tRSb = Trk("rs")
        for (ut0, unt, ulink) in UNITS:
            rr = [0]

            def bank6():
                b = rr[0] % 6
                rr[0] += 1
                return b

            def frontB(idx):
                j = unt - 1 - idx
                gi = ut0 + j
                t0 = gi * T
                s = idx % 2
                P.dma("sp", f_dma(XCi[s][:, :, :], dview(xc_d, t0)), "xci%d" % s, reads=[TXC[gi]], writes=tXCi[s])
                P.dma("sp", f_dma(HFi[s][:, :, :], dview(hf_d, t0)), "hfi%d" % s, reads=[THF[gi]], writes=[tHFi[s]])
                P.dma("sp", f_dma(GGi[s][:, :, :], dview(gg_d, t0)), "ggi%d" % s, reads=[TGG[gi]], writes=[tGGi[s]])
                P.dma("sp", f_dma(ANi[s][:, :, :], dview(an_d, t0)), "ani%d" % s, reads=[TAN[gi]], writes=[tANi[s]])
                P.dma("sp", f_dma(XTb[s][:, :, :], dview(xin_d, t0)), "xtb%d" % s, reads=[xin_t[gi]], writes=tXb[s])
                for c in range(4):
                    P.op("pool", f_cp(XCB[:, c, :], XCi[s][:, c, :]), reads=[tXCi[s][c]], writes=[tXCB[c]])
                yield
                yield from gates_gen(l, 1, XCi[s], tXCi[s], XCB, tXCB, WG, tWG, GB, tGB, bank6)
                for c in range(4):
                    if ulink is not None and j == ulink - 1:
                        P.op("dve", f_ts(GB["RA"][c][:, T - 1:T], GB["RA"][c][:, T - 1:T], LINK[:, 0:1], None, ALU.mult),
                             reads=[tGB["RA"][c], tLINK], writes=[tGB["RA"][c]])
                    init = 0.0 if idx == 0 else HB[1 - s][:, c, 0:1]
                    rd = [tGB["RA"][c], tGB["IU"][c]] + ([] if idx == 0 else [tHB[1 - s][c]])
                    P.op("dve", f_scan(HB[s][:, c, ::-1], GB["RA"][c][:, ::-1], GB["IU"][c][:, ::-1], init),
                         reads=rd, writes=[tHB[s][c]])
                    if c % 2 == 1:
                        yield

            def backB(idx):
                j = unt - 1 - idx
                gi = ut0 + j
                seg = gi // 4
                t0 = gi * T
                s = idx % 2
                for c in range(4):
                    P.op("dve", f_tt(LR[:, c, :], HFi[s][:, c, :], HB[s][:, c, :], ALU.add),
                         reads=[tHFi[s], tHB[s][c]], writes=[tLR[c]])
                    P.op("dve", f_tt(LR[:, c, :], LR[:, c, :], GGi[s][:, c, :], ALU.mult),
                         reads=[tLR[c], tGGi[s]], writes=[tLR[c]])
                    if c % 2 == 1:
                        yield
                P.op("act", f_act(SQ[:, :, :], LR[:, :, :], AF.Square), reads=tLR, writes=[tSQ])
                mb = bank6()
                for c in range(4):
                    P.op("pe", f_mm(PSF[mb][:, :], ONES5[:, :], SQ[:, c, :], c == 0, c == 3), reads=[tSQ, tCONST], writes=[tPSF[mb]])
                P.op("act", f_act(TMPb[:, :], PSF[mb][:, :], AF.Ln, bias=EPSC[:, 0:1]), reads=[tPSF[mb], tCONST], writes=[tTMPb])
                P.op("act", f_act(RSb[:, :], TMPb[:, :], AF.Exp, scale=-0.5), reads=[tTMPb], writes=[tRSb])
                yield
                for c in range(4):
                    P.op("dve", f_stt(LN[:, c, :], LR[:, c, :], vec_ap(l, V_GL + c), RSb[:, :], ALU.mult, ALU.mult),
                         reads=[tLR[c], tVEC, tRSb], writes=[tLN[c]])
                yield
                for m in range(8):
                    mb = bank6()
                    for c in range(4):
                        P.op("pe", f_mm(PSF[mb][:, :], WOUT[:, c, m * 128:(m + 1) * 128], ANi[s][:, c, :], c == 0, False),
                             reads=[tWOUT, tANi[s]], writes=[tPSF[mb]])
                    for c in range(4):
                        P.op("pe", f_mm(PSF[mb][:, :], WOUT[:, 4 + c, m * 128:(m + 1) * 128], LN[:, c, :], False, c == 3),
                             reads=[tWOUT, tLN[c]], writes=[tPSF[mb]])
                    P.op("dve", f_stt(XTb[s][:, m, :], PSF[mb][:, :], mod_ap(l, 2, m, seg), XTb[s][:, m, :], ALU.mult, ALU.add),
                         reads=[tPSF[mb], tMOD, tXb[s][m]], writes=[tXb[s][m]])
                    if m % 2 == 1:
                        yield
                P.dma("sp", f_dma(dview(xb_d, t0), XTb[s][:, :, :]), "xbo%d" % s, reads=tXb[s], writes=[TX["xb"][gi]])

            interleave([frontB(0)])
            for idx in range(unt):
                gens = [backB(idx)]
                if idx + 1 < unt:
                    gens.append(frontB(idx + 1))
                interleave(gens)
        P.barrier()

        ar.reset(base_mark)
        WFI = ar.alloc("wfi", [128, 8, 2 * DFF], BF16)
        WFO = ar.alloc("wfo", [128, 22, D], BF16)
        tWFI = trks("wfi", 8)
        tWFO = trks("wfo", 4)
        JB = (0, 6, 12, 17, 22)
        wsrc = wfi_d[l].rearrange("(c p) n -> p c n", p=128)
        for q8 in range(8):
            P.dma("pool", f_dma(WFI[:, :, q8 * 704:(q8 + 1) * 704], wsrc[:, :, q8 * 704:(q8 + 1) * 704]),
                  "wfi%d" % q8, writes=[tWFI[q8]])
        wsrc = wfo_d[l].rearrange("(c p) n -> p c n", p=128)
        for q4 in range(4):
            P.dma("pool", f_dma(WFO[:, JB[q4]:JB[q4 + 1], :], wsrc[:, JB[q4]:JB[q4 + 1], :]),
                  "wfo%d" % q4, writes=[tWFO[q4]])
        XTf = [ar.alloc("xt", [128, 8, T], F32) for _ in range(2)]
        tXf = [trks("x", 8) for _ in range(2)]
        HN = ar.alloc("hn", [128, 8, T], BF16)
        tHN = trks("hn", 8)
        TMP = [ar.alloc("tmp", [128, T], F32) for _ in range(3)]
        tTMP = trks("tmp", 3)
        RS = ar.alloc("rs", [128, T], F32)
        tRS = Trk("rs")
        ACTB = ar.alloc("actb", [128, 11, T], BF16)
        tACTB = trks("actb", 11)
        SG = [ar.alloc("sg", [128, T], F32) for _ in range(2)]
        tSG = trks("sg", 2)
        wfo_t = lambda jj: tWFO[0 if jj < 6 else 1 if jj < 12 else 2 if jj < 17 else 3]

        def f_load(gi):
            P.dma("sp", f_dma(XTf[gi % 2][:, :, :], dview(xb_d, gi * T)), "xtf%d" % (gi % 2), reads=[TX["xb"][gi]], writes=tXf[gi % 2])

        def f_norm(gi):
            norm_mod(XTf[gi % 2], tXf[gi % 2], HN, tHN, TMP, tTMP, RS, tRS, l, 4, 3, gi // 4, 4)

        def f_gu(gi, j0):
            for jl in range(11):
                jj = j0 + jl
                gb, ub = jj % 2, 2 + jj % 2
                for half, pb in ((0, gb), (1, ub)):
                    col = half * DFF + jj * 128
                    for c in range(8):
                        P.op("pe", f_mm(PSF[pb][:, :], WFI[:, c, col:col + 128], HN[:, c, :], c == 0, c == 7),
                             reads=[tWFI[col // 704], tWFI[(col + 127) // 704], tHN[c]], writes=[tPSF[pb]])
                P.op("act", f_act(SG[jj % 2][:, :], PSF[gb][:, :], AF.Silu), reads=[tPSF[gb]], writes=[tSG[jj % 2]])
                P.op("dve", f_tt(ACTB[:, jl, :], SG[jj % 2][:, :], PSF[ub][:, :], ALU.mult),
                     reads=[tSG[jj % 2], tPSF[ub]], writes=[tACTB[jl]])

        def f_out(gi, j0):
            XTt, tX, seg = XTf[gi % 2], tXf[gi % 2], gi // 4
            for m in range(8):
                pb = 4 + m % 2
                for jl in range(11):
                    jj = j0 + jl
                    P.op("pe", f_mm(PSF[pb][:, :], WFO[:, jj, m * 128:(m + 1) * 128], ACTB[:, jl, :], jl == 0, jl == 10),
                         reads=[wfo_t(jj), tACTB[jl]], writes=[tPSF[pb]])
                P.op("dve", f_stt(XTt[:, m, :], PSF[pb][:, :], mod_ap(l, 5, m, seg), XTt[:, m, :], ALU.mult, ALU.add),
                     reads=[tPSF[pb], tMOD, tX[m]], writes=[tX[m]])

        def f_fin(gi):
            XTt, tX, t0 = XTf[gi % 2], tXf[gi % 2], gi * T
            if l == L - 1:
                P.op("act", f_act(ACTB[:, 0:8, :], XTt[:, :, :], AF.Square), reads=tX, writes=tACTB[0:8])
                for c in range(8):
                    P.op("pe", f_mm(PSF[4][:, :], ONES[:, :], ACTB[:, c, :], c == 0, c == 7), reads=[tACTB[c], tCONST], writes=[tPSF[4]])
                P.op("act", f_act(SG[0][:, :], PSF[4][:, :], AF.Sqrt, bias=EPSC[:, 0:1]), reads=[tPSF[4], tCONST], writes=[tSG[0]])
                P.op("dve", lambda h, o=SG[1][:, :], i=SG[0][:, :]: h.reciprocal(out=o, in_=i), reads=[tSG[0]], writes=[tSG[1]])
                for c in range(8):
                    P.op("dve", f_stt(XTt[:, c, :], XTt[:, c, :], VEC[:, DEPTH * NV + c:DEPTH * NV + c + 1], SG[1][:, :],
                                      ALU.mult, ALU.mult), reads=[tX[c], tVEC, tSG[1]], writes=[tX[c]])
            o = P.dma("sp", f_dma(dview(xout_d, t0), XTt[:, :, :]), "xfo%d" % (gi % 2), reads=tX, writes=[xout_t[gi]])
            if l == L - 1:
                final_out.append(o)

        f_load(0)
        f_norm(0)
        for gi in range(NT):
            if gi + 1 < NT:
                f_load(gi + 1)
            f_gu(gi, 0)
            f_out(gi, 0)
            f_gu(gi, 11)
            if gi + 1 < NT:
                f_norm(gi + 1)
            f_out(gi, 11)
            f_fin(gi)
        P.barrier()
        xin_d, xin_t = xa_d, TX["xa"]

    P.emit(nc, final_waits=final_out)
    return nc, P


def _segments():
    segs = []
    for k in range(4):
        segs.append([("p", k, 0), ("p", k, 2048), ("s", k, 0)])
    for k in range(4):
        segs.append([("s", 4 + 3 * k + r, 0) for r in range(3)])
    return segs


def _dtab(linked):
    q = np.arange(128)[:, None]
    sidx = np.arange(384)[None, :] - 128
    dist = np.abs(q - sidx).astype(np.float32)
    base = np.where(dist <= 128, -dist, -1.0e9).astype(np.float32)
    d15 = base.copy()
    d16 = base.copy()
    if not linked:
        d15[:, 256:384] = -1.0e9
        d16[:, 0:128] = -1.0e9
    return np.concatenate([base, d15, d16], axis=1).astype(np.float32)


def _fm(v):
    v = np.asarray(v, np.float32)
    return v.reshape(-1, 128).T


_CACHE = {}


def kernel(x_prompt, x_sample, c_prompt, c_sample, w_mod, b_mod, g_norm1, w_in, sink, conv_w, conv_b,
           w_rg, b_rg, w_ig, b_ig, lam, g_attn_out, g_lru_out, w_out, g_norm2, w_ffn_in, w_ffn_out, g_final,
           _depth=DEPTH):
    f32 = np.float32
    segs = _segments()
    xs = {"p": np.asarray(x_prompt, f32), "s": np.asarray(x_sample, f32)}
    cs = {"p": np.asarray(c_prompt, f32), "s": np.asarray(c_sample, f32)}
    vecs = np.zeros((128, DEPTH * NV + 8), f32)
    for l in range(DEPTH):
        o = l * NV
        vecs[:, o + V_G1:o + V_G1 + 8] = _fm(g_norm1[l])
        vecs[:, o + V_G2:o + V_G2 + 8] = _fm(g_norm2[l])
        vecs[:, o + V_BMOD:o + V_BMOD + 48] = _fm(b_mod[l])
        for tap in range(4):
            vecs[:, o + V_CW + tap * 4:o + V_CW + tap * 4 + 4] = _fm(conv_w[l][tap])
        vecs[:, o + V_CB:o + V_CB + 4] = _fm(conv_b[l])
        for d in range(2):
            vecs[:, o + V_BRG + d * 4:o + V_BRG + d * 4 + 4] = _fm(b_rg[l][d])
            vecs[:, o + V_BIG + d * 4:o + V_BIG + d * 4 + 4] = _fm(b_ig[l][d])
            vecs[:, o + V_LAM + d * 4:o + V_LAM + d * 4 + 4] = _fm(lam[l][d])
        vecs[:, o + V_GA:o + V_GA + 4] = _fm(g_attn_out[l])
        vecs[:, o + V_GL:o + V_GL + 4] = _fm(g_lru_out[l])
        vecs[:, o + V_SINK:o + V_SINK + 8] = np.asarray(sink[l], f32)[None, :]
    vecs[:, DEPTH * NV:DEPTH * NV + 8] = _fm(g_final)
    perm = np.concatenate([np.concatenate([np.arange(c * 64, c * 64 + 64), np.arange((4 + c) * 64, (4 + c) * 64 + 64)])
                           for c in range(4)] + [np.arange(512, DIN)])
    w_in_p = np.ascontiguousarray(np.asarray(w_in, f32)[:, :, perm])
    wg = np.zeros((DEPTH, 16, 128, 128), f32)
    for l in range(DEPTH):
        for d in range(2):
            for g, w in ((0, w_rg), (1, w_ig)):
                for c in range(4):
                    j = (d * 2 + g) * 4 + c
                    wg[l, j, 0:64, 0:64] = w[l][d][2 * c]
                    wg[l, j, 64:128, 64:128] = w[l][d][2 * c + 1]
    shared = {"vecs": vecs, "w_mod": np.asarray(w_mod, f32), "w_in": w_in_p, "wg": wg,
              "w_out": np.asarray(w_out, f32), "w_ffn_in": np.asarray(w_ffn_in, f32),
              "w_ffn_out": np.asarray(w_ffn_out, f32)}
    in_maps = []
    for k in range(8):
        xt = np.empty((8, 128, NTOK), f32)
        ct = np.empty((128, 8, 3), f32)
        for si, (kind, b, t0) in enumerate(segs[k]):
            xt[:, :, si * 2048:(si + 1) * 2048] = xs[kind][b, t0:t0 + 2048, :].T.reshape(8, 128, 2048)
            ct[:, :, si] = cs[kind][b].reshape(8, 128).T
        linked = k < 4
        m = dict(shared)
        m["xT"] = xt
        m["cT"] = np.ascontiguousarray(ct.reshape(128, 24))
        m["link"] = np.full((128, 1), 1.0 if linked else 0.0, f32)
        m["dtab"] = _dtab(linked)
        in_maps.append(m)
    if _depth not in _CACHE:
        _CACHE[_depth] = build_program(_depth)[0]
    nc = _CACHE[_depth]
    res = run_bass_kernel_spmd(nc, in_maps, core_ids=list(range(8)))
    y_p = np.empty((4, 4096, D), f32)
    y_s = np.empty((16, 2048, D), f32)
    ys = {"p": y_p, "s": y_s}
    for k in range(8):
        yt = np.asarray(res.results[k]["yT"], f32)
        for si, (kind, b, t0) in enumerate(segs[k]):
            ys[kind][b, t0:t0 + 2048, :] = yt[:, :, si * 2048:(si + 1) * 2048].reshape(D, 2048).T
    return (y_p, y_s)
```

```python
import contextlib
import numpy as np
import concourse.bass as bass
import concourse.mybir as mybir
from concourse.bass_utils import run_bass_kernel_spmd

F32 = mybir.dt.float32
BF16 = mybir.dt.bfloat16
AF = mybir.ActivationFunctionType
ALU = mybir.AluOpType
AX = mybir.AxisListType

D = 1024
DEPTH = 4
NTOK = 6144
T = 512
NT = NTOK // T
DIN = 1792
DFF = 2816
EPS = 1e-6
NV = 124
V_G1, V_G2, V_BMOD, V_CW, V_CB, V_BRG, V_BIG, V_LAM, V_GA, V_GL, V_SINK = 0, 8, 16, 64, 80, 84, 92, 100, 108, 112, 116
UNITS = ((0, 8, 4), (8, 4, None))

ENGS = ("pe", "act", "dve", "pool", "sp")


class Trk:
    __slots__ = ("name", "w", "rs")

    def __init__(self, name=""):
        self.name = name
        self.w = None
        self.rs = []


def trks(name, n):
    return [Trk("%s%d" % (name, i)) for i in range(n)]


class Op:
    __slots__ = ("eng", "idx", "fn", "waits", "dma", "need_inc", "semval", "dkey")

    def __init__(self, eng, idx, fn, dma, dkey):
        self.eng = eng
        self.idx = idx
        self.fn = fn
        self.waits = []
        self.dma = dma
        self.need_inc = False
        self.semval = None
        self.dkey = dkey


class Prog:
    def __init__(self):
        self.ops = {e: [] for e in ENGS}
        self.waited = {}
        self.waited_dma = {}
        self.dma_keys = {}
        self.last_dma = {}
        self.last_comp = {}

    def op(self, eng, fn, reads=(), writes=(), dma=False, dkey=None):
        lst = self.ops[eng]
        o = Op(eng, len(lst), fn, dma, dkey)
        deps = []
        for t in reads:
            if t.w is not None:
                deps.append((t.w, True))
        for t in writes:
            if t.w is not None:
                deps.append((t.w, True))
            deps.extend((r, False) for r in t.rs)
        for d, isw in deps:
            self._add_wait(o, d, isw)
        for t in reads:
            t.rs.append(o)
        for t in writes:
            t.w = o
            t.rs = []
        lst.append(o)
        if dma:
            self.last_dma[dkey] = o
        else:
            self.last_comp[eng] = o
        return o

    def _add_wait(self, o, d, isw=True, force=False):
        if d is o:
            return
        if d.dma:
            key = (o.eng, d.dkey)
            prev = self.waited_dma.get(key)
            if prev is not None and prev >= d.semval:
                return
            self.waited_dma[key] = d.semval
            o.waits.append(d)
            return
        if d.eng == o.eng and not force:
            if o.eng == "pe":
                return
        key = (o.eng, d.eng)
        prev = self.waited.get(key, -1)
        if prev >= d.idx:
            return
        self.waited[key] = d.idx
        o.waits.append(d)
        d.need_inc = True

    def dma(self, eng, fn, dkey, reads=(), writes=()):
        self.dma_keys.setdefault(dkey, 0)
        self.dma_keys[dkey] += 16
        val = self.dma_keys[dkey]
        o = self.op(eng, fn, reads=reads, writes=writes, dma=True, dkey=dkey)
        o.semval = val
        return o

    def barrier(self):
        comp = dict(self.last_comp)
        dm = dict(self.last_dma)
        spn = self.op("sp", lambda h: h.nop())
        for d in list(comp.values()) + list(dm.values()):
            self._add_wait(spn, d, True, force=True)
        for e in ENGS:
            if e == "sp":
                continue
            n = self.op(e, lambda h: h.nop())
            self._add_wait(n, spn, True)

    def emit(self, nc, final_waits=()):
        for e in ENGS:
            c = 0
            for o in self.ops[e]:
                if o.dma:
                    continue
                if o.need_inc:
                    c += 1
                    o.semval = c
        with contextlib.ExitStack() as st:
            esem = {e: st.enter_context(nc.semaphore("S_" + e)) for e in ENGS}
            dsem = {k: st.enter_context(nc.semaphore("D_%d" % i)) for i, k in enumerate(self.dma_keys)}
            block = st.enter_context(nc.Block())
            prog = self

            def run(engname, h):
                for o in prog.ops[engname]:
                    for d in o.waits:
                        if d.dma:
                            h.wait_ge(dsem[d.dkey], d.semval)
                        else:
                            h.wait_ge(esem[d.eng], d.semval)
                    ins = o.fn(h)
                    if o.dma:
                        ins.then_inc(dsem[o.dkey], 16)
                    elif o.need_inc:
                        ins.then_inc(esem[o.eng], 1)
                if engname == "sp":
                    for o in final_waits:
                        h.wait_ge(dsem[o.dkey], o.semval)

            @block.tensor
            def _(h):
                run("pe", h)

            @block.scalar
            def _(h):
                run("act", h)

            @block.vector
            def _(h):
                run("dve", h)

            @block.gpsimd
            def _(h):
                run("pool", h)

            @block.sync
            def _(h):
                run("sp", h)


def f_mm(out, lhsT, rhs, start, stop):
    return lambda h: h.matmul(out, lhsT=lhsT, rhs=rhs, start=start, stop=stop)


def f_tr(out, in_, ident):
    return lambda h: h.transpose(out, in_, ident)


def f_act(out, in_, func, bias=None, scale=None, accum_out=None):
    kw = {}
    if bias is not None:
        kw["bias"] = bias
    if scale is not None:
        kw["scale"] = scale
    if accum_out is not None:
        kw["accum_out"] = accum_out
    return lambda h: h.activation(out=out, in_=in_, func=func, **kw)


def f_tt(out, in0, in1, op):
    return lambda h: h.tensor_tensor(out=out, in0=in0, in1=in1, op=op)


def f_ts(out, in0, s1, s2, op0, op1=None):
    if op1 is None:
        return lambda h: h.tensor_scalar(out=out, in0=in0, scalar1=s1, scalar2=None, op0=op0)
    return lambda h: h.tensor_scalar(out=out, in0=in0, scalar1=s1, scalar2=s2, op0=op0, op1=op1)


def f_stt(out, in0, scalar, in1, op0, op1):
    return lambda h: h.scalar_tensor_tensor(out=out, in0=in0, scalar=scalar, in1=in1, op0=op0, op1=op1)


def f_cp(out, in_):
    return lambda h: h.tensor_copy(out=out, in_=in_)


def f_dma(out, in_):
    return lambda h: h.dma_start(out=out, in_=in_)


def f_scan(out, d0, d1, init):
    return lambda h: h.tensor_tensor_scan(out=out, data0=d0, data1=d1, initial=init, op0=ALU.mult, op1=ALU.add)


class Arena:
    def __init__(self, nc, lo, hi):
        self.nc, self.lo, self.hi, self.cur, self.n = nc, lo, hi, lo, 0

    def reset(self, to=None):
        self.cur = self.lo if to is None else to

    def mark(self):
        return self.cur

    def alloc(self, name, shape, dt):
        esz = 4 if dt == F32 else 2
        nbytes = int(np.prod(shape[1:])) * esz
        off = (self.cur + 63) // 64 * 64
        assert off + nbytes <= self.hi, ("SBUF arena overflow", name, off, nbytes, self.hi)
        self.cur = off + nbytes
        self.n += 1
        return self.nc.alloc_sbuf_tensor_at("%s_%d" % (name, self.n), list(shape), dt, offset=off)


def build_program(depth=DEPTH):
    nc = bass.Bass("TRN2", target_bir_lowering=False)
    L = depth
    dr = lambda name, shape, dt=F32, kind="ExternalInput": nc.dram_tensor(name, list(shape), dt, kind=kind).ap()
    xT_d = dr("xT", [8, 128, NTOK])
    cT_d = dr("cT", [128, 24])
    link_d = dr("link", [128, 1])
    dtab_d = dr("dtab", [128, 3 * 384])
    vecs_d = dr("vecs", [128, DEPTH * NV + 8])
    wmod_d = dr("w_mod", [DEPTH, D, 6 * D])
    win_d = dr("w_in", [DEPTH, D, DIN])
    wg_d = dr("wg", [DEPTH, 16, 128, 128])
    wout_d = dr("w_out", [DEPTH, D, D])
    wfi_d = dr("w_ffn_in", [DEPTH, D, 2 * DFF])
    wfo_d = dr("w_ffn_out", [DEPTH, DFF, D])
    yT_d = dr("yT", [8, 128, NTOK], kind="ExternalOutput")
    xa_d = dr("xa_s", [8, 128, NTOK], kind="Internal")
    xb_d = dr("xb_s", [8, 128, NTOK], kind="Internal")
    xc_d = dr("xc_s", [4, 128, NTOK], kind="Internal")
    hf_d = dr("hf_s", [4, 128, NTOK], kind="Internal")
    gg_d = dr("gg_s", [4, 128, NTOK], BF16, kind="Internal")
    an_d = dr("an_s", [4, 128, NTOK], BF16, kind="Internal")

    def dview(d, t0, n=T):
        return d[:, :, t0:t0 + n].rearrange("c p t -> p c t")

    P = Prog()
    ar = Arena(nc, 16640, 229376)
    TX = {"in": trks("xin", NT), "xa": trks("xa", NT), "xb": trks("xb", NT), "y": trks("y", NT)}
    TXC, THF, TGG, TAN = trks("xcs", NT), trks("hfs", NT), trks("ggs", NT), trks("ans", NT)

    VEC = ar.alloc("vec", [128, DEPTH * NV + 8], F32)
    MOD = ar.alloc("mod", [128, DEPTH * 6 * 8 * 3], F32)
    C8 = ar.alloc("c8", [128, DEPTH * 8], F32)
    SK8 = ar.alloc("sk8", [128, DEPTH * 8], F32)
    HB2 = ar.alloc("hb2", [128, DEPTH * 16], F32)
    QUARTER = ar.alloc("quarter", [128, 1], F32)
    CA = ar.alloc("ca", [128, 24], BF16)
    DT = ar.alloc("dtab", [128, 3 * 384], F32)
    LINK = ar.alloc("link", [128, 1], F32)
    ONES = ar.alloc("ones", [128, 128], BF16)
    ONES5 = ar.alloc("ones5", [128, 128], BF16)
    IDN = ar.alloc("idn", [128, 128], BF16)
    EPSC = ar.alloc("epsc", [128, 1], F32)
    ONEC = ar.alloc("onec", [128, 1], F32)
    NHALFC = ar.alloc("nhalfc", [128, 1], F32)
    tVEC, tMOD, tC8, tSK8, tDT, tLINK, tCONST = (Trk(n) for n in ("vec", "mod", "c8", "sk8", "dt", "link", "const"))
    base_mark = ar.mark()

    PSF = [nc.alloc_psum_tensor("psf%d" % i, [128, 512], F32) for i in range(6)]
    PSB = [nc.alloc_psum_tensor("psb%d" % i, [128, 1024], BF16) for i in range(2)]
    tPSF = trks("psf", 6)
    tPSB = trks("psb", 2)

    def mod_ap(l, k, c, seg):
        o = ((l * 6 + k) * 8 + c) * 3 + seg
        return MOD[:, o:o + 1]

    def vec_ap(l, off, n=1):
        return VEC[:, l * NV + off:l * NV + off + n]

    P.dma("sp", f_dma(VEC[:], vecs_d), "vec", writes=[tVEC])
    P.dma("sp", f_dma(DT[:], dtab_d), "dt", writes=[tDT])
    P.dma("sp", f_dma(LINK[:], link_d), "link", writes=[tLINK])
    P.op("pool", lambda h: h.memset(ONES[:], 1.0 / 1024), writes=[tCONST])
    P.op("pool", lambda h: h.memset(ONES5[:], 1.0 / 512), writes=[tCONST])
    P.op("pool", lambda h: h.memset(EPSC[:], EPS), writes=[tCONST])
    P.op("pool", lambda h: h.memset(ONEC[:], 1.0), writes=[tCONST])
    P.op("pool", lambda h: h.memset(NHALFC[:], -0.5), writes=[tCONST])
    P.op("pool", lambda h: h.memset(QUARTER[:], 0.25), writes=[tCONST])
    P.op("pool", lambda h: h.memset(IDN[:], 0.0), writes=[tCONST])
    P.op("pool", lambda h: h.affine_select(out=IDN[:], in_=IDN[:], pattern=[[-1, 128]], compare_op=ALU.not_equal,
                                            fill=1.0, base=0, channel_multiplier=1), reads=[tCONST], writes=[tCONST])
    CT = ar.alloc("ct", [128, 24], F32)
    tCT, tCA = Trk("ct"), Trk("ca")
    P.dma("sp", f_dma(CT[:], cT_d), "ct", writes=[tCT])
    P.op("act", f_act(CA[:], CT[:], AF.Silu), reads=[tCT], writes=[tCA])
    modit = [0]

    def mod_gen(l, WM, tWM, bankfn):
        for k in range(6):
            s = modit[0] % 2
            modit[0] += 1
            src = wmod_d[l][:, k * D:(k + 1) * D].rearrange("(c p) n -> p c n", p=128)
            P.dma("pool", f_dma(WM[s][:], src), "wm%d" % s, writes=[tWM[s]])
            pb = bankfn()
            ps = PSF[pb]
            for fc in range(8):
                for kc in range(8):
                    P.op("pe", f_mm(ps[:, fc * 3:fc * 3 + 3], WM[s][:, kc, fc * 128:(fc + 1) * 128],
                                    CA[:, kc * 3:kc * 3 + 3], kc == 0, kc == 7),
                         reads=[tWM[s], tCA], writes=[tPSF[pb]])
            o = (l * 6 + k) * 24
            for seg in range(3):
                P.op("dve", f_tt(MOD[:, o + seg:o + 24:3], ps[:, seg:24:3], vec_ap(l, V_BMOD + k * 8, 8), ALU.add),
                     reads=[tPSF[pb], tVEC], writes=[tMOD])
            if k in (1, 4):
                gof = V_G1 if k == 1 else V_G2
                for seg in range(3):
                    P.op("dve", f_stt(MOD[:, o + seg:o + 24:3], MOD[:, o + seg:o + 24:3], 1.0, vec_ap(l, gof, 8),
                                      ALU.add, ALU.mult), reads=[tMOD, tVEC], writes=[tMOD])
            yield

    WM0 = [ar.alloc("wm", [128, 8, D], BF16) for _ in range(2)]
    tWM0 = trks("wm", 2)
    pbk0 = [0]

    def pbank0():
        pbk0[0] += 1
        return pbk0[0] % 6

    for _ in mod_gen(0, WM0, tWM0, pbank0):
        pass
    E1 = ar.alloc("e1", [128, DEPTH * 8], F32)
    E2 = ar.alloc("e2", [128, DEPTH * 8], F32)
    tE = Trk("e")
    for l in range(L):
        sl = slice(l * 8, l * 8 + 8)
        P.op("act", f_act(E1[:, sl], vec_ap(l, V_LAM, 8), AF.Exp, scale=-1.0), reads=[tVEC], writes=[tE])
        P.op("dve", f_ts(E2[:, sl], E1[:, sl], -0.25, 1.0 / 3, ALU.mult, ALU.add), reads=[tE], writes=[tE])
        P.op("dve", f_tt(E2[:, sl], E2[:, sl], E1[:, sl], ALU.mult), reads=[tE], writes=[tE])
        P.op("dve", f_ts(E2[:, sl], E2[:, sl], -1.0, 0.5, ALU.mult, ALU.add), reads=[tE], writes=[tE])
        P.op("dve", f_tt(E2[:, sl], E2[:, sl], E1[:, sl], ALU.mult), reads=[tE], writes=[tE])
        P.op("dve", f_ts(E2[:, sl], E2[:, sl], -1.0, 1.0, ALU.mult, ALU.add), reads=[tE], writes=[tE])
        P.op("dve", f_tt(E2[:, sl], E2[:, sl], E1[:, sl], ALU.mult), reads=[tE], writes=[tE])
        P.op("dve", f_ts(C8[:, sl], E2[:, sl], -4.0, None, ALU.mult), reads=[tE], writes=[tC8])
        P.op("dve", f_ts(HB2[:, l * 16:l * 16 + 16], vec_ap(l, V_BRG, 16), 0.5, None, ALU.mult), reads=[tVEC], writes=[tC8])
        P.op("dve", f_ts(SK8[:, sl], vec_ap(l, V_SINK, 8), 8.0, None, ALU.mult), reads=[tVEC], writes=[tSK8])
    P.barrier()
    ar.reset(base_mark)

    def norm_gen(XTt, tX, HN, tHN, TMP, tTMP, RS, tRS, l, kg, ksh, seg, mmb):
        P.op("act", f_act(HN[:, :, :], XTt[:, :, :], AF.Square), reads=tX, writes=tHN)
        yield
        ps = PSF[mmb]
        for c in range(8):
            P.op("pe", f_mm(ps[:, :], ONES[:, :], HN[:, c, :], c == 0, c == 7), reads=[tHN[c], tCONST], writes=[tPSF[mmb]])
        P.op("act", f_act(TMP[0][:, :], ps[:, :], AF.Sqrt, bias=EPSC[:, 0:1]), reads=[tPSF[mmb], tCONST], writes=[tTMP[0]])
        P.op("dve", lambda h, o=RS[:, :], i=TMP[0][:, :]: h.reciprocal(out=o, in_=i), reads=[tTMP[0]], writes=[tRS])
        yield
        for c in range(8):
            b = 1 + (c % 2)
            P.op("dve", f_stt(TMP[b][:, :], XTt[:, c, :], mod_ap(l, kg, c, seg), RS[:, :], ALU.mult, ALU.mult),
                 reads=[tX[c], tMOD, tRS], writes=[tTMP[b]])
            P.op("act", f_act(HN[:, c, :], TMP[b][:, :], AF.Identity, bias=mod_ap(l, ksh, c, seg)),
                 reads=[tTMP[b], tMOD], writes=[tHN[c]])
            if c % 2 == 1:
                yield

    def norm_mod(*a):
        for _ in norm_gen(*a):
            pass

    def interleave(gens):
        gens = list(gens)
        while gens:
            for g in list(gens):
                try:
                    next(g)
                except StopIteration:
                    gens.remove(g)

    def g1(l, d, c, XCB, tXCB, WG, tWG, GB, tGB, bankfn):
        for g, nm in ((0, "RA"), (1, "IU")):
            mb = bankfn()
            hb = HB2[:, l * 16 + g * 8 + d * 4 + c:l * 16 + g * 8 + d * 4 + c + 1]
            P.op("pe", f_mm(PSF[mb][:, :], WG[:, (d * 2 + g) * 4 + c, :], XCB[:, c, :], True, True),
                 reads=[tWG, tXCB[c]], writes=[tPSF[mb]])
            P.op("act", f_act(GB[nm][c][:, :], PSF[mb][:, :], AF.Tanh, bias=hb, scale=0.5),
                 reads=[tPSF[mb], tC8], writes=[tGB[nm][c]])
        hc8 = C8[:, l * 8 + d * 4 + c:l * 8 + d * 4 + c + 1]
        P.op("act", f_act(GB["RA"][c][:, :], GB["RA"][c][:, :], AF.Exp, bias=hc8, scale=hc8),
             reads=[tGB["RA"][c], tC8], writes=[tGB["RA"][c]])
        P.op("act", f_act(GB["S"][c][:, :], GB["RA"][c][:, :], AF.Square), reads=[tGB["RA"][c]], writes=[tGB["S"][c]])

    def g2(c, XC, tXC, GB, tGB):
        P.op("dve", f_ts(GB["S"][c][:, :], GB["S"][c][:, :], 0.99999994, -1.0, ALU.min, ALU.mult),
             reads=[tGB["S"][c]], writes=[tGB["S"][c]])
        P.op("dve", f_stt(GB["IU"][c][:, :], GB["IU"][c][:, :], 1.0, XC[:, c, :], ALU.add, ALU.mult),
             reads=[tGB["IU"][c], tXC[c]], writes=[tGB["IU"][c]])

    def gsq(c, GB, tGB):
        P.op("act", f_act(GB["S"][c][:, :], GB["S"][c][:, :], AF.Sqrt, bias=QUARTER[:, 0:1], scale=0.25),
             reads=[tGB["S"][c], tCONST], writes=[tGB["S"][c]])

    def g3(c, GB, tGB):
        P.op("dve", f_tt(GB["IU"][c][:, :], GB["IU"][c][:, :], GB["S"][c][:, :], ALU.mult),
             reads=[tGB["IU"][c], tGB["S"][c]], writes=[tGB["IU"][c]])

    def gates_gen(l, d, XC, tXC, XCB, tXCB, WG, tWG, GB, tGB, bankfn):
        for c in range(4):
            g1(l, d, c, XCB, tXCB, WG, tWG, GB, tGB, bankfn)
            yield
        for c in range(4):
            g2(c, XC, tXC, GB, tGB)
        yield
        for c in range(4):
            gsq(c, GB, tGB)
        yield
        for c in range(4):
            g3(c, GB, tGB)
        yield

    def alloc_gb():
        GB = {n: [ar.alloc("gb" + n, [128, T], F32) for _ in range(4)] for n in ("RA", "IU", "S")}
        tGB = {n: trks("gb" + n, 4) for n in ("RA", "IU", "S")}
        return GB, tGB

    final_out = []
    xin_d, xin_t = xT_d, TX["in"]
    for l in range(L):
        xout_d, xout_t = (yT_d, TX["y"]) if l == L - 1 else (xa_d, TX["xa"])
        ar.reset(base_mark)
        WIN = ar.alloc("win", [128, 8, DIN], BF16)
        WG = ar.alloc("wg", [128, 16, 128], BF16)
        tWIN, tWG = Trk("win"), Trk("wg")
        wsrc = win_d[l].rearrange("(c p) n -> p c n", p=128)
        for q4 in range(4):
            P.dma("pool", f_dma(WIN[:, :, q4 * 448:(q4 + 1) * 448], wsrc[:, :, q4 * 448:(q4 + 1) * 448]),
                  "win%d" % q4, writes=[tWIN])
        P.dma("pool", f_dma(WG[:], wg_d[l].rearrange("j p m -> p j m")), "wg", writes=[tWG])
        XTt = ar.alloc("xt", [128, 8, T], F32)
        tX = trks("x", 8)
        HN = ar.alloc("hn", [128, 8, T], BF16)
        tHN = trks("hn", 8)
        TMP = [ar.alloc("tmp", [128, T], F32) for _ in range(3)]
        tTMP = trks("tmp", 3)
        RS = ar.alloc("rs", [128, T], F32)
        tRS = Trk("rs")
        S = 8 * T
        KU = ar.alloc("ku", [128, S], BF16)
        tKU = trks("ku", 8)
        VU = ar.alloc("vu", [128, 32, 128], BF16)
        tVU = trks("vu", 8)
        QR = ar.alloc("qr", [128, 4, 2, T], BF16)
        tQR = trks("qr", 2)
        XR = ar.alloc("xr", [128, 4, 3, T + 3], F32)
        tXR = trks("xr", 3)
        XC = [ar.alloc("xc", [128, 4, T], F32)] * 2
        tXCs = [trks("xc", 4)] * 2
        CARRY = ar.alloc("carry", [128, 4], F32)
        tCARRY = trks("carry", 4)
        XCB = ar.alloc("xcb", [128, 4, T], BF16)
        tXCB = trks("xcb", 4)
        GB, tGB = alloc_gb()
        HF = [ar.alloc("hf", [128, 4, T], F32)] * 2
        tHF = [trks("hf", 4)] * 2
        GG = [ar.alloc("gg", [128, 4, T], BF16)] * 2
        tGG = [Trk("gg")] * 2
        ANT = [ar.alloc("ant", [128, 4, T], BF16) for _ in range(2)]
        tANT = trks("ant", 2)
        TT = [ar.alloc("tt", [128, 384], F32) for _ in range(4)]
        tTT = trks("tt", 4)
        PM = [ar.alloc("pm", [128, 384], BF16) for _ in range(4)]
        tPM = trks("pm", 4)
        PT = [ar.alloc("pt", [128, 3, 128], BF16) for _ in range(4)]
        tPT = trks("pt", 4)
        ST = ar.alloc("st", [128, 128], F32)
        tSTp = [{n: Trk("st" + n) for n in ("es", "rden", "ssq", "rsa")} for _ in range(2)]
        tMXp, tNEGBp, tRSUMp = [trks("mx", 8) for _ in range(2)], [trks("negb", 8) for _ in range(2)], [trks("rsum", 8) for _ in range(2)]
        tPSBh = trks("psbh", 2)
        ATT = ar.alloc("att", [128, 512], F32)
        tATT = Trk("att")
        ATN = ar.alloc("atn", [128, 512], BF16)
        tATN = Trk("atn")
        JUNK = ar.alloc("junk", [128, 512], BF16)
        CTMP = ar.alloc("ctmp", [128, T], F32)
        tCTMP = Trk("ctmp")
        tJUNK = Trk("junk")
        for (ut0, unt, ulink) in UNITS:
            mmrr = [0]

            def mmbank():
                b = mmrr[0] % 3
                mmrr[0] += 1
                return b

            gbank = [0]

            def gatebank():
                b = 3 + gbank[0] % 2
                gbank[0] += 1
                return b

            hcnt = [0]

            def lru_conv(j):
                s = j % 3
                for c in range(4):
                    cw = lambda tap: vec_ap(l, V_CW + tap * 4 + c)
                    P.op("pool", f_ts(XC[0][:, c, :], XR[:, c, s, 0:T], cw(0), vec_ap(l, V_CB + c), ALU.mult, ALU.add),
                         reads=[tXR[s], tVEC], writes=[tXCs[0][c]])
                    for tap in (1, 2, 3):
                        P.op("pool", f_ts(CTMP[:, :], XR[:, c, s, tap:tap + T], cw(tap), 0.0, ALU.mult, ALU.add),
                             reads=[tXR[s], tVEC], writes=[tCTMP])
                        P.op("pool", f_tt(XC[0][:, c, :], XC[0][:, c, :], CTMP[:, :], ALU.add),
                             reads=[tCTMP, tXCs[0][c]], writes=[tXCs[0][c]])
                    P.op("pool", f_cp(XCB[:, c, :], XC[0][:, c, :]), reads=[tXCs[0][c]], writes=[tXCB[c]])

            def lruG(j):
                gi = ut0 + j
                for c in range(4):
                    g1(l, 0, c, XCB, tXCB, WG, tWG, GB, tGB, gatebank)
                    yield
                for c in range(4):
                    g2(c, XC[0], tXCs[0], GB, tGB)
                yield
                for c in range(4):
                    gsq(c, GB, tGB)
                yield
                for c in range(4):
                    g3(c, GB, tGB)
                    if ulink is not None and j == ulink:
                        P.op("dve", f_ts(GB["RA"][c][:, 0:1], GB["RA"][c][:, 0:1], LINK[:, 0:1], None, ALU.mult),
                             reads=[tGB["RA"][c], tLINK], writes=[tGB["RA"][c]])
                    init = 0.0 if j == 0 else CARRY[:, c:c + 1]
                    rd = [tGB["RA"][c], tGB["IU"][c]] + ([] if j == 0 else [tCARRY[c]])
                    P.op("dve", f_scan(HF[0][:, c, :], GB["RA"][c][:, :], GB["IU"][c][:, :], init), reads=rd, writes=[tHF[0][c]])
                    P.op("dve", f_cp(CARRY[:, c:c + 1], HF[0][:, c, T - 1:T]), reads=[tHF[0][c]], writes=[tCARRY[c]])
                    yield
                t0 = gi * T
                P.dma("sp", f_dma(dview(xc_d, t0), XC[0][:, :, :]), "xco", reads=tXCs[0], writes=[TXC[gi]])
                P.dma("sp", f_dma(dview(hf_d, t0), HF[0][:, :, :]), "hfo", reads=tHF[0], writes=[THF[gi]])

            def mix(j, ngen=None):
                gi = ut0 + j
                qs = j % 2
                asl = j % 2
                nblk = unt * 4
                lru_conv(j)

                def lru_step(p):
                    pass

                def geom(b):
                    n = 4 * j + b
                    lo = max(n - 1, 0)
                    hi = min(n + 1, nblk - 1)
                    nkb = hi - lo + 1
                    doff = 0 if lo == n - 1 else 128
                    tab = 0
                    if ulink is not None and n == ulink * 4 - 1:
                        tab = 1
                    if ulink is not None and n == ulink * 4:
                        tab = 2
                    tl = sorted(set((lo // 4, hi // 4)))
                    return n, lo, hi, nkb, doff, tab, [tKU[tt] for tt in tl], [tVU[tt] for tt in tl]

                sbank = {}

                def SA(p):
                    b, c = p // 4, p % 4
                    n, lo, hi, nkb, doff, tab, ktr, vtr = geom(b)
                    nk = nkb * 128
                    for hh in range(2):
                        sb = hcnt[0] % 3
                        hcnt[0] += 1
                        sbank[(p, hh)] = sb
                        pl, ph = hh * 64, hh * 64 + 64
                        P.op("pe", f_mm(PSF[sb][:, 0:nk], QR[pl:ph, c, qs, b * 128:(b + 1) * 128],
                                        KU[pl:ph, lo * 128:(hi + 1) * 128], True, True),
                             reads=[tQR[qs]] + ktr, writes=[tPSF[sb]])

                def SB(p, hh):
                    b, c = p // 4, p % 4
                    n, lo, hi, nkb, doff, tab, ktr, vtr = geom(b)
                    nk = nkb * 128
                    so = (b % 2) * 64
                    Dv = DT[:, tab * 384 + doff:tab * 384 + doff + nk]
                    if True:
                        h_ = c + 4 * hh
                        sb = sbank[(p, hh)]
                        r4 = (c * 2 + hh) % 4
                        slope8 = 8.0 * 2.0 ** (-(h_ + 1))
                        P.op("dve", f_stt(TT[r4][:, 0:nk], Dv, slope8, PSF[sb][:, 0:nk], ALU.mult, ALU.add),
                             reads=[tDT, tPSF[sb]], writes=[tTT[r4]])
                        P.op("dve", lambda h, o=ST[:, so + h_:so + h_ + 1], i=TT[r4][:, 0:nk]: h.reduce_max(out=o, in_=i, axis=AX.X),
                             reads=[tTT[r4]], writes=[tMXp[b % 2][h_]])
                        P.op("dve", f_ts(ST[:, so + 8 + h_:so + 9 + h_], ST[:, so + h_:so + h_ + 1],
                                         SK8[:, l * 8 + h_:l * 8 + h_ + 1], -0.125, ALU.max, ALU.mult),
                             reads=[tMXp[b % 2][h_], tSK8], writes=[tNEGBp[b % 2][h_]])
                        P.op("act", f_act(PM[r4][:, 0:nk], TT[r4][:, 0:nk], AF.Exp, bias=ST[:, so + 8 + h_:so + 9 + h_], scale=0.125,
                                          accum_out=ST[:, so + 16 + h_:so + 17 + h_]),
                             reads=[tTT[r4], tNEGBp[b % 2][h_]], writes=[tPM[r4], tRSUMp[b % 2][h_]])

                def SC(p):
                    b, c = p // 4, p % 4
                    n, lo, hi, nkb, doff, tab, ktr, vtr = geom(b)
                    nk = nkb * 128
                    for hh in range(2):
                        r4 = (c * 2 + hh) % 4
                        pbk = hh
                        for jj in range(nkb):
                            P.op("pe", f_tr(PSB[pbk][:, jj * 128:(jj + 1) * 128],
                                            PM[r4][:, jj * 128:(jj + 1) * 128], IDN[:, :]),
                                 reads=[tPM[r4], tCONST], writes=[tPSB[pbk]])
                        P.op("act", f_act(PT[r4][:, 0:nkb, :],
                                          PSB[pbk][:, 0:nk].rearrange("p (j q) -> p j q", q=128), AF.Copy),
                             reads=[tPSB[pbk]], writes=[tPT[r4]])

                def SD(p):
                    b, c = p // 4, p % 4
                    n, lo, hi, nkb, doff, tab, ktr, vtr = geom(b)
                    for hh in range(2):
                        h_ = c + 4 * hh
                        r4 = (c * 2 + hh) % 4
                        for jj in range(nkb):
                            P.op("pe", f_mm(PSF[5][:, h_ * 64:(h_ + 1) * 64], PT[r4][:, jj, :],
                                            VU[:, lo + jj, hh * 64:(hh + 1) * 64], jj == 0, jj == nkb - 1),
                                 reads=[tPT[r4]] + vtr, writes=[tPSF[5]])

                def EPI(b, k):
                    so = (b % 2) * 64
                    tST = tSTp[b % 2]
                    if k == 0:
                        P.op("dve", f_tt(ST[:, so + 24:so + 32], ST[:, so + 8:so + 16], vec_ap(l, V_SINK, 8), ALU.add),
                             reads=tNEGBp[b % 2] + [tVEC], writes=[tST["es"]])
                        P.op("act", f_act(ST[:, so + 24:so + 32], ST[:, so + 24:so + 32], AF.Exp), reads=[tST["es"]], writes=[tST["es"]])
                    elif k == 1:
                        P.op("dve", f_tt(ST[:, so + 32:so + 40], ST[:, so + 24:so + 32], ST[:, so + 16:so + 24], ALU.add),
                             reads=[tST["es"]] + tRSUMp[b % 2], writes=[tST["rden"]])
                        P.op("dve", lambda h, o=ST[:, so + 32:so + 40]: h.reciprocal(out=o, in_=o), reads=[tST["rden"]], writes=[tST["rden"]])
                        P.op("dve", f_tt(ATT[:, :].rearrange("p (h d) -> p h d", d=64),
                                         PSF[5][:, :].rearrange("p (h d) -> p h d", d=64),
                                         ST[:, so + 32:so + 40].unsqueeze(2).to_broadcast([128, 8, 64]), ALU.mult),
                             reads=[tPSF[5], tST["rden"]], writes=[tATT])
                    elif k == 2:
                        P.op("act", f_act(JUNK[:, :], ATT[:, :], AF.Square, accum_out=ST[:, so + 40:so + 41]),
                             reads=[tATT], writes=[tJUNK, tST["ssq"]])
                    elif k == 3:
                        P.op("dve", f_ts(ST[:, so + 41:so + 42], ST[:, so + 40:so + 41], 1.0 / 512, EPS, ALU.mult, ALU.add),
                             reads=[tST["ssq"]], writes=[tST["rsa"]])
                        P.op("pool", f_tt(ST[:, so + 41:so + 42], ST[:, so + 41:so + 42], NHALFC[:, 0:1], ALU.pow),
                             reads=[tST["rsa"], tCONST], writes=[tST["rsa"]])
                    elif k == 4:
                        P.op("dve", f_ts(ATN[:, :], ATT[:, :], ST[:, so + 41:so + 42], None, ALU.mult),
                             reads=[tATT, tST["rsa"]], writes=[tATN])
                    elif k == 5:
                        for cc in range(4):
                            P.op("pe", f_tr(PSB[1][:, 512 + cc * 128:512 + (cc + 1) * 128], ATN[:, cc * 128:(cc + 1) * 128], IDN[:, :]),
                                 reads=[tATN, tCONST], writes=[tPSB[1]])
                        for cc in range(4):
                            P.op("act", f_act(ANT[asl][:, cc, b * 128:(b + 1) * 128], PSB[1][:, 512 + cc * 128:512 + (cc + 1) * 128],
                                              AF.Identity, scale=vec_ap(l, V_GA + cc)),
                                 reads=[tPSB[1], tVEC], writes=[tANT[asl]])

                NP = 16
                SA(0)
                for p in range(NP + 6):
                    if p < NP:
                        SB(p, 0)
                    if p + 1 < NP:
                        SA(p + 1)
                    if p < NP:
                        SB(p, 1)
                    if 0 <= p - 2 < NP:
                        SD(p - 2)
                    for b_ in range(4):
                        k_ = p - (4 * b_ + 4)
                        if 0 <= k_ < 6:
                            EPI(b_, k_)
                    if 0 <= p - 1 < NP:
                        SC(p - 1)
                    lru_step(p)
                    if ngen is not None and p >= 2:
                        next(ngen, None)
                if ngen is not None:
                    for _ in ngen:
                        pass
                P.dma("sp", f_dma(dview(an_d, gi * T), ANT[asl][:, :, :]), "ano%d" % asl, reads=[tANT[asl]], writes=[TAN[gi]])

            def xload(i):
                gi = ut0 + i
                P.dma("sp", f_dma(XTt[:, :, :], dview(xin_d, gi * T)), "xt", reads=[xin_t[gi]], writes=tX)

            def ngen_for(i):
                return norm_gen(XTt, tX, HN, tHN, TMP, tTMP, RS, tRS, l, 1, 0, (ut0 + i) // 4, gatebank())

            xload(0)
            for _ in ngen_for(0):
                pass
            def proj(i):
                gi = ut0 + i
                seg = gi // 4
                t0 = gi * T
                s = i % 2
                xs_ = i % 3
                xp_ = (i - 1) % 3
                for m in range(4):
                    mb = mmbank()
                    for c in range(8):
                        P.op("pe", f_mm(PSF[mb][:, :], WIN[:, c, m * 128:(m + 1) * 128], HN[:, c, :], c == 0, c == 7),
                             reads=[tWIN, tHN[c]], writes=[tPSF[mb]])
                    P.op("act", f_act(QR[:, m, s, :], PSF[mb][:, :], AF.Copy), reads=[tPSF[mb]], writes=[tQR[s]])
                    yield
                mb = mmbank()
                for c in range(8):
                    P.op("pe", f_mm(PSF[mb][:, :], WIN[:, c, 512:640], HN[:, c, :], c == 0, c == 7),
                         reads=[tWIN, tHN[c]], writes=[tPSF[mb]])
                P.op("dve", f_cp(KU[:, i * T:(i + 1) * T], PSF[mb][:, :]), reads=[tPSF[mb]], writes=[tKU[i]])
                yield
                mb = mmbank()
                for b in range(4):
                    for c in range(8):
                        P.op("pe", f_mm(PSF[mb][:, b * 128:(b + 1) * 128], HN[:, c, b * 128:(b + 1) * 128],
                                        WIN[:, c, 640:768], c == 0, c == 7),
                             reads=[tWIN, tHN[c]], writes=[tPSF[mb]])
                P.op("act", f_act(VU[:, i * 4:(i + 1) * 4, :], PSF[mb][:, :].rearrange("p (b f) -> p b f", f=128), AF.Copy),
                     reads=[tPSF[mb]], writes=[tVU[i]])
                yield
                for m in range(4):
                    mb = mmbank()
                    for c in range(8):
                        P.op("pe", f_mm(PSF[mb][:, :], WIN[:, c, 768 + m * 128:768 + (m + 1) * 128], HN[:, c, :], c == 0, c == 7),
                             reads=[tWIN, tHN[c]], writes=[tPSF[mb]])
                    P.op("dve", f_cp(XR[:, m, xs_, 2:T + 2], PSF[mb][:, :]), reads=[tPSF[mb]], writes=[tXR[xs_]])
                    yield
                linked = (ulink is not None and i == ulink)
                if i == 0:
                    P.op("pool", lambda h, o=XR[:, :, xs_, 0:2]: h.memset(o, 0.0), writes=[tXR[xs_]])
                else:
                    if linked:
                        P.op("pool", f_ts(XR[:, :, xs_, 0:2], XR[:, :, xp_, T:T + 2], LINK[:, 0:1], 0.0, ALU.mult, ALU.add),
                             reads=[tXR[xp_], tLINK], writes=[tXR[xs_]])
                        P.op("pool", f_ts(XR[:, :, xp_, T + 2:T + 3], XR[:, :, xs_, 2:3], LINK[:, 0:1], 0.0, ALU.mult, ALU.add),
                             reads=[tXR[xs_], tLINK], writes=[tXR[xp_]])
                    else:
                        P.op("pool", f_cp(XR[:, :, xs_, 0:2], XR[:, :, xp_, T:T + 2]), reads=[tXR[xp_]], writes=[tXR[xs_]])
                        P.op("pool", f_cp(XR[:, :, xp_, T + 2:T + 3], XR[:, :, xs_, 2:3]), reads=[tXR[xs_]], writes=[tXR[xp_]])
                if i == unt - 1:
                    P.op("pool", lambda h, o=XR[:, :, xs_, T + 2:T + 3]: h.memset(o, 0.0), writes=[tXR[xs_]])
                for m in range(4):
                    mb = mmbank()
                    for c in range(8):
                        P.op("pe", f_mm(PSF[mb][:, :], WIN[:, c, 1280 + m * 128:1280 + (m + 1) * 128], HN[:, c, :], c == 0, c == 7),
                             reads=[tWIN, tHN[c]], writes=[tPSF[mb]])
                    P.op("act", f_act(GG[s][:, m, :], PSF[mb][:, :], AF.Gelu_apprx_tanh), reads=[tPSF[mb]], writes=[tGG[s]])
                    yield
                P.dma("sp", f_dma(dview(gg_d, t0), GG[s][:, :, :]), "ggo", reads=[tGG[s]], writes=[TGG[gi]])

            for i in range(unt + 2):
                if i + 1 < unt:
                    xload(i + 1)
                gens = []
                if i < unt:
                    gens.append(proj(i))
                if 0 <= i - 2 < unt:
                    gens.append(lruG(i - 2))
                interleave(gens)
                ng = ngen_for(i + 1) if i + 1 < unt else None
                if 1 <= i <= unt:
                    mix(i - 1, ng)
                elif ng is not None:
                    for _ in ng:
                        pass
        P.barrier()

        ar.reset(base_mark)
        WG = ar.alloc("wg", [128, 16, 128], BF16)
        WOUT = ar.alloc("wout", [128, 8, D], BF16)
        tWG, tWOUT = Trk("wg"), Trk("wout")
        P.dma("pool", f_dma(WG[:], wg_d[l].rearrange("j p m -> p j m")), "wg", writes=[tWG])
        wsrc = wout_d[l].rearrange("(c p) n -> p c n", p=128)
        for q2 in range(2):
            P.dma("pool", f_dma(WOUT[:, :, q2 * 512:(q2 + 1) * 512], wsrc[:, :, q2 * 512:(q2 + 1) * 512]),
                  "wout%d" % q2, writes=[tWOUT])
        XTb = [ar.alloc("xt", [128, 8, T], F32) for _ in range(2)]
        tXb = [trks("x", 8) for _ in range(2)]
        XCi = [ar.alloc("xci", [128, 4, T], F32) for _ in range(2)]
        tXCi = [trks("xci", 4) for _ in range(2)]
        HFi = [ar.alloc("hfi", [128, 4, T], F32) for _ in range(2)]
        tHFi = trks("hfi", 2)
        GGi = [ar.alloc("ggi", [128, 4, T], BF16) for _ in range(2)]
        tGGi = trks("ggi", 2)
        ANi = [ar.alloc("ani", [128, 4, T], BF16) for _ in range(2)]
        tANi = trks("ani", 2)
        XCB = ar.alloc("xcb", [128, 4, T], BF16)
        tXCB = trks("xcb", 4)
        GB, tGB = alloc_gb()
        HB = [ar.alloc("hb", [128, 4, T], F32) for _ in range(2)]
        tHB = [trks("hb", 4) for _ in range(2)]
        LR = ar.alloc("lr", [128, 4, T], F32)
        tLR = trks("lr", 4)
        SQ = ar.alloc("sq", [128, 4, T], BF16)
        tSQ = Trk("sq")
        LN = ar.alloc("ln", [128, 4, T], BF16)
        tLN = trks("ln", 4)
        TMPb = ar.alloc("tmp", [128, T], F32)
        tTMPb = Trk("tmp")
        RSb = ar.alloc("rs", [128, T], F32)
        tRSb = Trk("rs")
        WMb = [ar.alloc("wm", [128, 8, D], BF16) for _ in range(2)]
        tWMb = trks("wm", 2)
        rr = [0]

        def bank6():
            b = rr[0] % 6
            rr[0] += 1
            return b

        modg = mod_gen(l + 1, WMb, tWMb, bank6) if l + 1 < L else None
        for (ut0, unt, ulink) in UNITS:

            def frontB(idx):
                j = unt - 1 - idx
                gi = ut0 + j
                t0 = gi * T
                s = idx % 2
                P.dma("sp", f_dma(XCi[s][:, :, :], dview(xc_d, t0)), "xci%d" % s, reads=[TXC[gi]], writes=tXCi[s])
                P.dma("sp", f_dma(HFi[s][:, :, :], dview(hf_d, t0)), "hfi%d" % s, reads=[THF[gi]], writes=[tHFi[s]])
                P.dma("sp", f_dma(GGi[s][:, :, :], dview(gg_d, t0)), "ggi%d" % s, reads=[TGG[gi]], writes=[tGGi[s]])
                P.dma("sp", f_dma(ANi[s][:, :, :], dview(an_d, t0)), "ani%d" % s, reads=[TAN[gi]], writes=[tANi[s]])
                P.dma("sp", f_dma(XTb[s][:, :, :], dview(xin_d, t0)), "xtb%d" % s, reads=[xin_t[gi]], writes=tXb[s])
                for c in range(4):
                    P.op("pool", f_cp(XCB[:, c, :], XCi[s][:, c, :]), reads=[tXCi[s][c]], writes=[tXCB[c]])
                yield
                yield from gates_gen(l, 1, XCi[s], tXCi[s], XCB, tXCB, WG, tWG, GB, tGB, bank6)
                for c in range(4):
                    if ulink is not None and j == ulink - 1:
                        P.op("dve", f_ts(GB["RA"][c][:, T - 1:T], GB["RA"][c][:, T - 1:T], LINK[:, 0:1], None, ALU.mult),
                             reads=[tGB["RA"][c], tLINK], writes=[tGB["RA"][c]])
                    init = 0.0 if idx == 0 else HB[1 - s][:, c, 0:1]
                    rd = [tGB["RA"][c], tGB["IU"][c]] + ([] if idx == 0 else [tHB[1 - s][c]])
                    P.op("dve", f_scan(HB[s][:, c, ::-1], GB["RA"][c][:, ::-1], GB["IU"][c][:, ::-1], init),
                         reads=rd, writes=[tHB[s][c]])
                    if c % 2 == 1:
                        yield

            def backB(idx):
                j = unt - 1 - idx
                gi = ut0 + j
                seg = gi // 4
                t0 = gi * T
                s = idx % 2
                for c in range(4):
                    P.op("dve", f_tt(LR[:, c, :], HFi[s][:, c, :], HB[s][:, c, :], ALU.add),
                         reads=[tHFi[s], tHB[s][c]], writes=[tLR[c]])
                    P.op("dve", f_tt(LR[:, c, :], LR[:, c, :], GGi[s][:, c, :], ALU.mult),
                         reads=[tLR[c], tGGi[s]], writes=[tLR[c]])
                    if c % 2 == 1:
                        yield
                P.op("act", f_act(SQ[:, :, :], LR[:, :, :], AF.Square), reads=tLR, writes=[tSQ])
                mb = bank6()
                for c in range(4):
                    P.op("pe", f_mm(PSF[mb][:, :], ONES5[:, :], SQ[:, c, :], c == 0, c == 3), reads=[tSQ, tCONST], writes=[tPSF[mb]])
                P.op("act", f_act(TMPb[:, :], PSF[mb][:, :], AF.Ln, bias=EPSC[:, 0:1]), reads=[tPSF[mb], tCONST], writes=[tTMPb])
                P.op("act", f_act(RSb[:, :], TMPb[:, :], AF.Exp, scale=-0.5), reads=[tTMPb], writes=[tRSb])
                yield
                for c in range(4):
                    P.op("dve", f_stt(LN[:, c, :], LR[:, c, :], vec_ap(l, V_GL + c), RSb[:, :], ALU.mult, ALU.mult),
                         reads=[tLR[c], tVEC, tRSb], writes=[tLN[c]])
                yield
                for m in range(8):
                    mb = bank6()
                    for c in range(4):
                        P.op("pe", f_mm(PSF[mb][:, :], WOUT[:, c, m * 128:(m + 1) * 128], ANi[s][:, c, :], c == 0, False),
                             reads=[tWOUT, tANi[s]], writes=[tPSF[mb]])
                    for c in range(4):
                        P.op("pe", f_mm(PSF[mb][:, :], WOUT[:, 4 + c, m * 128:(m + 1) * 128], LN[:, c, :], False, c == 3),
                             reads=[tWOUT, tLN[c]], writes=[tPSF[mb]])
                    P.op("dve", f_stt(XTb[s][:, m, :], PSF[mb][:, :], mod_ap(l, 2, m, seg), XTb[s][:, m, :], ALU.mult, ALU.add),
                         reads=[tPSF[mb], tMOD, tXb[s][m]], writes=[tXb[s][m]])
                    if m % 2 == 1:
                        yield
                P.dma("sp", f_dma(dview(xb_d, t0), XTb[s][:, :, :]), "xbo%d" % s, reads=tXb[s], writes=[TX["xb"][gi]])

            interleave([frontB(0)])
            for idx in range(unt):
                gens = [backB(idx)]
                if idx + 1 < unt:
                    gens.append(frontB(idx + 1))
                interleave(gens)
                if modg is not None:
                    next(modg, None)
        if modg is not None:
            for _ in modg:
                pass
        P.barrier()

        ar.reset(base_mark)
        WFI = ar.alloc("wfi", [128, 8, 2 * DFF], BF16)
        WFO = ar.alloc("wfo", [128, 22, D], BF16)
        tWFI = trks("wfi", 8)
        tWFO = trks("wfo", 4)
        JB = (0, 6, 12, 17, 22)
        wsrc = wfi_d[l].rearrange("(c p) n -> p c n", p=128)
        for q8 in (0, 4, 1, 5, 2, 6, 3, 7):
            P.dma("pool", f_dma(WFI[:, :, q8 * 704:(q8 + 1) * 704], wsrc[:, :, q8 * 704:(q8 + 1) * 704]),
                  "wfi%d" % q8, writes=[tWFI[q8]])
        wsrc = wfo_d[l].rearrange("(c p) n -> p c n", p=128)
        for q4 in range(4):
            P.dma("pool", f_dma(WFO[:, JB[q4]:JB[q4 + 1], :], wsrc[:, JB[q4]:JB[q4 + 1], :]),
                  "wfo%d" % q4, writes=[tWFO[q4]])
        XTf = [ar.alloc("xt", [128, 8, T], F32) for _ in range(2)]
        tXf = [trks("x", 8) for _ in range(2)]
        HN = ar.alloc("hn", [128, 8, T], BF16)
        tHN = trks("hn", 8)
        TMP = [ar.alloc("tmp", [128, T], F32) for _ in range(3)]
        tTMP = trks("tmp", 3)
        RS = ar.alloc("rs", [128, T], F32)
        tRS = Trk("rs")
        ACTB = ar.alloc("actb", [128, 11, T], BF16)
        tACTB = trks("actb", 11)
        SG = [ar.alloc("sg", [128, T], F32) for _ in range(2)]
        tSG = trks("sg", 2)
        wfo_t = lambda jj: tWFO[0 if jj < 6 else 1 if jj < 12 else 2 if jj < 17 else 3]

        def f_load(gi):
            P.dma("sp", f_dma(XTf[gi % 2][:, :, :], dview(xb_d, gi * T)), "xtf%d" % (gi % 2), reads=[TX["xb"][gi]], writes=tXf[gi % 2])

        def f_norm(gi):
            norm_mod(XTf[gi % 2], tXf[gi % 2], HN, tHN, TMP, tTMP, RS, tRS, l, 4, 3, gi // 4, 4)

        def f_gu(gi, j0):
            for jl in range(11):
                jj = j0 + jl
                gb, ub = jj % 2, 2 + jj % 2
                for half, pb in ((0, gb), (1, ub)):
                    col = half * DFF + jj * 128
                    for c in range(8):
                        P.op("pe", f_mm(PSF[pb][:, :], WFI[:, c, col:col + 128], HN[:, c, :], c == 0, c == 7),
                             reads=[tWFI[col // 704], tWFI[(col + 127) // 704], tHN[c]], writes=[tPSF[pb]])
                P.op("act", f_act(SG[jj % 2][:, :], PSF[gb][:, :], AF.Silu), reads=[tPSF[gb]], writes=[tSG[jj % 2]])
                P.op("dve", f_tt(ACTB[:, jl, :], SG[jj % 2][:, :], PSF[ub][:, :], ALU.mult),
                     reads=[tSG[jj % 2], tPSF[ub]], writes=[tACTB[jl]])

        def f_out(gi, j0):
            XTt, tX, seg = XTf[gi % 2], tXf[gi % 2], gi // 4
            for m in range(8):
                pb = 4 + m % 2
                for jl in range(11):
                    jj = j0 + jl
                    P.op("pe", f_mm(PSF[pb][:, :], WFO[:, jj, m * 128:(m + 1) * 128], ACTB[:, jl, :], jl == 0, jl == 10),
                         reads=[wfo_t(jj), tACTB[jl]], writes=[tPSF[pb]])
                P.op("dve", f_stt(XTt[:, m, :], PSF[pb][:, :], mod_ap(l, 5, m, seg), XTt[:, m, :], ALU.mult, ALU.add),
                     reads=[tPSF[pb], tMOD, tX[m]], writes=[tX[m]])

        def f_fin(gi):
            XTt, tX, t0 = XTf[gi % 2], tXf[gi % 2], gi * T
            if l == L - 1:
                P.op("act", f_act(ACTB[:, 0:8, :], XTt[:, :, :], AF.Square), reads=tX, writes=tACTB[0:8])
                for c in range(8):
                    P.op("pe", f_mm(PSF[4][:, :], ONES[:, :], ACTB[:, c, :], c == 0, c == 7), reads=[tACTB[c], tCONST], writes=[tPSF[4]])
                P.op("act", f_act(SG[0][:, :], PSF[4][:, :], AF.Sqrt, bias=EPSC[:, 0:1]), reads=[tPSF[4], tCONST], writes=[tSG[0]])
                P.op("dve", lambda h, o=SG[1][:, :], i=SG[0][:, :]: h.reciprocal(out=o, in_=i), reads=[tSG[0]], writes=[tSG[1]])
                for c in range(8):
                    P.op("dve", f_stt(XTt[:, c, :], XTt[:, c, :], VEC[:, DEPTH * NV + c:DEPTH * NV + c + 1], SG[1][:, :],
                                      ALU.mult, ALU.mult), reads=[tX[c], tVEC, tSG[1]], writes=[tX[c]])
            o = P.dma("sp", f_dma(dview(xout_d, t0), XTt[:, :, :]), "xfo%d" % (gi % 2), reads=tX, writes=[xout_t[gi]])
            if l == L - 1:
                final_out.append(o)

        f_load(0)
        f_norm(0)
        for gi in range(NT):
            if gi + 1 < NT:
                f_load(gi + 1)
            f_gu(gi, 0)
            f_out(gi, 0)
            f_gu(gi, 11)
            if gi + 1 < NT:
                f_norm(gi + 1)
            f_out(gi, 11)
            f_fin(gi)
        P.barrier()
        xin_d, xin_t = xa_d, TX["xa"]

    P.emit(nc, final_waits=final_out)
    return nc, P


def _segments():
    segs = []
    for k in range(4):
        segs.append([("p", k, 0), ("p", k, 2048), ("s", k, 0)])
    for k in range(4):
        segs.append([("s", 4 + 3 * k + r, 0) for r in range(3)])
    return segs


def _dtab(linked):
    q = np.arange(128)[:, None]
    sidx = np.arange(384)[None, :] - 128
    dist = np.abs(q - sidx).astype(np.float32)
    base = np.where(dist <= 128, -dist, -1.0e9).astype(np.float32)
    d15 = base.copy()
    d16 = base.copy()
    if not linked:
        d15[:, 256:384] = -1.0e9
        d16[:, 0:128] = -1.0e9
    return np.concatenate([base, d15, d16], axis=1).astype(np.float32)


def _fm(v):
    v = np.asarray(v, np.float32)
    return v.reshape(-1, 128).T


_CACHE = {}


def kernel(x_prompt, x_sample, c_prompt, c_sample, w_mod, b_mod, g_norm1, w_in, sink, conv_w, conv_b,
           w_rg, b_rg, w_ig, b_ig, lam, g_attn_out, g_lru_out, w_out, g_norm2, w_ffn_in, w_ffn_out, g_final,
           _depth=DEPTH):
    f32 = np.float32
    segs = _segments()
    xs = {"p": np.asarray(x_prompt, f32), "s": np.asarray(x_sample, f32)}
    cs = {"p": np.asarray(c_prompt, f32), "s": np.asarray(c_sample, f32)}
    vecs = np.zeros((128, DEPTH * NV + 8), f32)
    for l in range(DEPTH):
        o = l * NV
        vecs[:, o + V_G1:o + V_G1 + 8] = _fm(g_norm1[l])
        vecs[:, o + V_G2:o + V_G2 + 8] = _fm(g_norm2[l])
        vecs[:, o + V_BMOD:o + V_BMOD + 48] = _fm(b_mod[l])
        for tap in range(4):
            vecs[:, o + V_CW + tap * 4:o + V_CW + tap * 4 + 4] = _fm(conv_w[l][tap])
        vecs[:, o + V_CB:o + V_CB + 4] = _fm(conv_b[l])
        for d in range(2):
            vecs[:, o + V_BRG + d * 4:o + V_BRG + d * 4 + 4] = _fm(b_rg[l][d])
            vecs[:, o + V_BIG + d * 4:o + V_BIG + d * 4 + 4] = _fm(b_ig[l][d])
            vecs[:, o + V_LAM + d * 4:o + V_LAM + d * 4 + 4] = _fm(lam[l][d])
        vecs[:, o + V_GA:o + V_GA + 4] = _fm(g_attn_out[l])
        vecs[:, o + V_GL:o + V_GL + 4] = _fm(g_lru_out[l])
        vecs[:, o + V_SINK:o + V_SINK + 8] = np.asarray(sink[l], f32)[None, :]
    vecs[:, DEPTH * NV:DEPTH * NV + 8] = _fm(g_final)
    perm = np.concatenate([np.concatenate([np.arange(c * 64, c * 64 + 64), np.arange((4 + c) * 64, (4 + c) * 64 + 64)])
                           for c in range(4)] + [np.arange(512, DIN)])
    w_in_p = np.ascontiguousarray(np.asarray(w_in, f32)[:, :, perm])
    wg = np.zeros((DEPTH, 16, 128, 128), f32)
    for l in range(DEPTH):
        for d in range(2):
            for g, w in ((0, w_rg), (1, w_ig)):
                for c in range(4):
                    j = (d * 2 + g) * 4 + c
                    wg[l, j, 0:64, 0:64] = w[l][d][2 * c]
                    wg[l, j, 64:128, 64:128] = w[l][d][2 * c + 1]
    shared = {"vecs": vecs, "w_mod": np.asarray(w_mod, f32), "w_in": w_in_p, "wg": wg,
              "w_out": np.asarray(w_out, f32), "w_ffn_in": np.asarray(w_ffn_in, f32),
              "w_ffn_out": np.asarray(w_ffn_out, f32)}
    in_maps = []
    for k in range(8):
        xt = np.empty((8, 128, NTOK), f32)
        ct = np.empty((128, 8, 3), f32)
        for si, (kind, b, t0) in enumerate(segs[k]):
            xt[:, :, si * 2048:(si + 1) * 2048] = xs[kind][b, t0:t0 + 2048, :].T.reshape(8, 128, 2048)
            ct[:, :, si] = cs[kind][b].reshape(8, 128).T
        linked = k < 4
        m = dict(shared)
        m["xT"] = xt
        m["cT"] = np.ascontiguousarray(ct.reshape(128, 24))
        m["link"] = np.full((128, 1), 1.0 if linked else 0.0, f32)
        m["dtab"] = _dtab(linked)
        in_maps.append(m)
    if _depth not in _CACHE:
        _CACHE[_depth] = build_program(_depth)[0]
    nc = _CACHE[_depth]
    res = run_bass_kernel_spmd(nc, in_maps, core_ids=list(range(8)))
    y_p = np.empty((4, 4096, D), f32)
    y_s = np.empty((16, 2048, D), f32)
    ys = {"p": y_p, "s": y_s}
    for k in range(8):
        yt = np.asarray(res.results[k]["yT"], f32)
        for si, (kind, b, t0) in enumerate(segs[k]):
            ys[kind][b, t0:t0 + 2048, :] = yt[:, :, si * 2048:(si + 1) * 2048].reshape(D, 2048).T
    return (y_p, y_s)
```

```python
import contextlib
import numpy as np
import concourse.bass as bass
import concourse.mybir as mybir
from concourse.bass_utils import run_bass_kernel_spmd

F32 = mybir.dt.float32
BF16 = mybir.dt.bfloat16
AF = mybir.ActivationFunctionType
ALU = mybir.AluOpType
AX = mybir.AxisListType

D = 1024
DEPTH = 4
NTOK = 6144
T = 512
NT = NTOK // T
DIN = 1792
DFF = 2816
EPS = 1e-6
NV = 124
V_G1, V_G2, V_BMOD, V_CW, V_CB, V_BRG, V_BIG, V_LAM, V_GA, V_GL, V_SINK = 0, 8, 16, 64, 80, 84, 92, 100, 108, 112, 116
UNITS = ((0, 8, 4), (8, 4, None))

ENGS = ("pe", "act", "dve", "pool", "sp")


class Trk:
    __slots__ = ("name", "w", "rs")

    def __init__(self, name=""):
        self.name = name
        self.w = None
        self.rs = []


def trks(name, n):
    return [Trk("%s%d" % (name, i)) for i in range(n)]


class Op:
    __slots__ = ("eng", "idx", "fn", "waits", "dma", "need_inc", "semval", "dkey")

    def __init__(self, eng, idx, fn, dma, dkey):
        self.eng = eng
        self.idx = idx
        self.fn = fn
        self.waits = []
        self.dma = dma
        self.need_inc = False
        self.semval = None
        self.dkey = dkey


class Prog:
    def __init__(self):
        self.ops = {e: [] for e in ENGS}
        self.waited = {}
        self.waited_dma = {}
        self.dma_keys = {}
        self.last_dma = {}
        self.last_comp = {}

    def op(self, eng, fn, reads=(), writes=(), dma=False, dkey=None):
        lst = self.ops[eng]
        o = Op(eng, len(lst), fn, dma, dkey)
        deps = []
        for t in reads:
            if t.w is not None:
                deps.append((t.w, True))
        for t in writes:
            if t.w is not None:
                deps.append((t.w, True))
            deps.extend((r, False) for r in t.rs)
        for d, isw in deps:
            self._add_wait(o, d, isw)
        for t in reads:
            t.rs.append(o)
        for t in writes:
            t.w = o
            t.rs = []
        lst.append(o)
        if dma:
            self.last_dma[dkey] = o
        else:
            self.last_comp[eng] = o
        return o

    def _add_wait(self, o, d, isw=True, force=False):
        if d is o:
            return
        if d.dma:
            key = (o.eng, d.dkey)
            prev = self.waited_dma.get(key)
            if prev is not None and prev >= d.semval:
                return
            self.waited_dma[key] = d.semval
            o.waits.append(d)
            return
        if d.eng == o.eng and not force:
            if o.eng == "pe":
                return
        key = (o.eng, d.eng)
        prev = self.waited.get(key, -1)
        if prev >= d.idx:
            return
        self.waited[key] = d.idx
        o.waits.append(d)
        d.need_inc = True

    def dma(self, eng, fn, dkey, reads=(), writes=()):
        self.dma_keys.setdefault(dkey, 0)
        self.dma_keys[dkey] += 16
        val = self.dma_keys[dkey]
        o = self.op(eng, fn, reads=reads, writes=writes, dma=True, dkey=dkey)
        o.semval = val
        return o

    def barrier(self):
        comp = dict(self.last_comp)
        dm = dict(self.last_dma)
        spn = self.op("sp", lambda h: h.nop())
        for d in list(comp.values()) + list(dm.values()):
            self._add_wait(spn, d, True, force=True)
        for e in ENGS:
            if e == "sp":
                continue
            n = self.op(e, lambda h: h.nop())
            self._add_wait(n, spn, True)

    def emit(self, nc, final_waits=()):
        for e in ENGS:
            c = 0
            for o in self.ops[e]:
                if o.dma:
                    continue
                if o.need_inc:
                    c += 1
                    o.semval = c
        with contextlib.ExitStack() as st:
            esem = {e: st.enter_context(nc.semaphore("S_" + e)) for e in ENGS}
            dsem = {k: st.enter_context(nc.semaphore("D_%d" % i)) for i, k in enumerate(self.dma_keys)}
            block = st.enter_context(nc.Block())
            prog = self

            def run(engname, h):
                for o in prog.ops[engname]:
                    for d in o.waits:
                        if d.dma:
                            h.wait_ge(dsem[d.dkey], d.semval)
                        else:
                            h.wait_ge(esem[d.eng], d.semval)
                    ins = o.fn(h)
                    if o.dma:
                        ins.then_inc(dsem[o.dkey], 16)
                    elif o.need_inc:
                        ins.then_inc(esem[o.eng], 1)
                if engname == "sp":
                    for o in final_waits:
                        h.wait_ge(dsem[o.dkey], o.semval)

            @block.tensor
            def _(h):
                run("pe", h)

            @block.scalar
            def _(h):
                run("act", h)

            @block.vector
            def _(h):
                run("dve", h)

            @block.gpsimd
            def _(h):
                run("pool", h)

            @block.sync
            def _(h):
                run("sp", h)


def f_mm(out, lhsT, rhs, start, stop):
    return lambda h: h.matmul(out, lhsT=lhsT, rhs=rhs, start=start, stop=stop)


def f_tr(out, in_, ident):
    return lambda h: h.transpose(out, in_, ident)


def f_act(out, in_, func, bias=None, scale=None, accum_out=None):
    kw = {}
    if bias is not None:
        kw["bias"] = bias
    if scale is not None:
        kw["scale"] = scale
    if accum_out is not None:
        kw["accum_out"] = accum_out
    return lambda h: h.activation(out=out, in_=in_, func=func, **kw)


def f_tt(out, in0, in1, op):
    return lambda h: h.tensor_tensor(out=out, in0=in0, in1=in1, op=op)


def f_ts(out, in0, s1, s2, op0, op1=None):
    if op1 is None:
        return lambda h: h.tensor_scalar(out=out, in0=in0, scalar1=s1, scalar2=None, op0=op0)
    return lambda h: h.tensor_scalar(out=out, in0=in0, scalar1=s1, scalar2=s2, op0=op0, op1=op1)


def f_stt(out, in0, scalar, in1, op0, op1):
    return lambda h: h.scalar_tensor_tensor(out=out, in0=in0, scalar=scalar, in1=in1, op0=op0, op1=op1)


def f_cp(out, in_):
    return lambda h: h.tensor_copy(out=out, in_=in_)


def f_dma(out, in_):
    return lambda h: h.dma_start(out=out, in_=in_)


def f_scan(out, d0, d1, init):
    return lambda h: h.tensor_tensor_scan(out=out, data0=d0, data1=d1, initial=init, op0=ALU.mult, op1=ALU.add)


class Arena:
    def __init__(self, nc, lo, hi):
        self.nc, self.lo, self.hi, self.cur, self.n = nc, lo, hi, lo, 0

    def reset(self, to=None):
        self.cur = self.lo if to is None else to

    def mark(self):
        return self.cur

    def alloc(self, name, shape, dt):
        esz = 4 if dt == F32 else 2
        nbytes = int(np.prod(shape[1:])) * esz
        off = (self.cur + 63) // 64 * 64
        assert off + nbytes <= self.hi, ("SBUF arena overflow", name, off, nbytes, self.hi)
        self.cur = off + nbytes
        self.n += 1
        return self.nc.alloc_sbuf_tensor_at("%s_%d" % (name, self.n), list(shape), dt, offset=off)


def build_program(depth=DEPTH):
    nc = bass.Bass("TRN2", target_bir_lowering=False)
    L = depth
    dr = lambda name, shape, dt=F32, kind="ExternalInput": nc.dram_tensor(name, list(shape), dt, kind=kind).ap()
    xT_d = dr("xT", [8, 128, NTOK])
    cT_d = dr("cT", [128, 24])
    link_d = dr("link", [128, 1])
    dtab_d = dr("dtab", [128, 3 * 384])
    vecs_d = dr("vecs", [128, DEPTH * NV + 8])
    wmod_d = dr("w_mod", [DEPTH, D, 6 * D])
    win_d = dr("w_in", [DEPTH, D, DIN])
    wg_d = dr("wg", [DEPTH, 16, 128, 128])
    wout_d = dr("w_out", [DEPTH, D, D])
    wfi_d = dr("w_ffn_in", [DEPTH, D, 2 * DFF])
    wfo_d = dr("w_ffn_out", [DEPTH, DFF, D])
    yT_d = dr("yT", [8, 128, NTOK], kind="ExternalOutput")
    xa_d = dr("xa_s", [8, 128, NTOK], kind="Internal")
    xb_d = dr("xb_s", [8, 128, NTOK], kind="Internal")
    xc_d = dr("xc_s", [4, 128, NTOK], kind="Internal")
    hf_d = dr("hf_s", [4, 128, NTOK], kind="Internal")
    gg_d = dr("gg_s", [4, 128, NTOK], BF16, kind="Internal")
    an_d = dr("an_s", [4, 128, NTOK], BF16, kind="Internal")

    def dview(d, t0, n=T):
        return d[:, :, t0:t0 + n].rearrange("c p t -> p c t")

    P = Prog()
    ar = Arena(nc, 16640, 229376)
    TX = {"in": trks("xin", NT), "xa": trks("xa", NT), "xb": trks("xb", NT), "y": trks("y", NT)}
    TXC, THF, TGG, TAN = trks("xcs", NT), trks("hfs", NT), trks("ggs", NT), trks("ans", NT)

    VEC = ar.alloc("vec", [128, DEPTH * NV + 8], F32)
    MOD = ar.alloc("mod", [128, DEPTH * 6 * 8 * 3], F32)
    C8 = ar.alloc("c8", [128, DEPTH * 8], F32)
    SK8 = ar.alloc("sk8", [128, DEPTH * 8], F32)
    HB2 = ar.alloc("hb2", [128, DEPTH * 16], F32)
    QUARTER = ar.alloc("quarter", [128, 1], F32)
    CA = ar.alloc("ca", [128, 24], BF16)
    DT = ar.alloc("dtab", [128, 3 * 384], F32)
    LINK = ar.alloc("link", [128, 1], F32)
    ONES = ar.alloc("ones", [128, 128], BF16)
    ONES5 = ar.alloc("ones5", [128, 128], BF16)
    IDN = ar.alloc("idn", [128, 128], BF16)
    EPSC = ar.alloc("epsc", [128, 1], F32)
    ONEC = ar.alloc("onec", [128, 1], F32)
    NHALFC = ar.alloc("nhalfc", [128, 1], F32)
    tVEC, tMOD, tC8, tSK8, tDT, tLINK, tCONST = (Trk(n) for n in ("vec", "mod", "c8", "sk8", "dt", "link", "const"))
    base_mark = ar.mark()

    PSF = [nc.alloc_psum_tensor("psf%d" % i, [128, 512], F32) for i in range(6)]
    PSB = [nc.alloc_psum_tensor("psb%d" % i, [128, 1024], BF16) for i in range(2)]
    tPSF = trks("psf", 6)
    tPSB = trks("psb", 2)

    def mod_ap(l, k, c, seg):
        o = ((l * 6 + k) * 8 + c) * 3 + seg
        return MOD[:, o:o + 1]

    def vec_ap(l, off, n=1):
        return VEC[:, l * NV + off:l * NV + off + n]

    P.dma("sp", f_dma(VEC[:], vecs_d), "vec", writes=[tVEC])
    P.dma("sp", f_dma(DT[:], dtab_d), "dt", writes=[tDT])
    P.dma("sp", f_dma(LINK[:], link_d), "link", writes=[tLINK])
    P.op("pool", lambda h: h.memset(ONES[:], 1.0 / 1024), writes=[tCONST])
    P.op("pool", lambda h: h.memset(ONES5[:], 1.0 / 512), writes=[tCONST])
    P.op("pool", lambda h: h.memset(EPSC[:], EPS), writes=[tCONST])
    P.op("pool", lambda h: h.memset(ONEC[:], 1.0), writes=[tCONST])
    P.op("pool", lambda h: h.memset(NHALFC[:], -0.5), writes=[tCONST])
    P.op("pool", lambda h: h.memset(QUARTER[:], 0.25), writes=[tCONST])
    P.op("pool", lambda h: h.memset(IDN[:], 0.0), writes=[tCONST])
    P.op("pool", lambda h: h.affine_select(out=IDN[:], in_=IDN[:], pattern=[[-1, 128]], compare_op=ALU.not_equal,
                                            fill=1.0, base=0, channel_multiplier=1), reads=[tCONST], writes=[tCONST])
    CT = ar.alloc("ct", [128, 24], F32)
    tCT, tCA = Trk("ct"), Trk("ca")
    P.dma("sp", f_dma(CT[:], cT_d), "ct", writes=[tCT])
    P.op("act", f_act(CA[:], CT[:], AF.Silu), reads=[tCT], writes=[tCA])
    modit = [0]

    def mod_gen(l, WM, tWM, bankfn):
        for k in range(6):
            s = modit[0] % 2
            modit[0] += 1
            src = wmod_d[l][:, k * D:(k + 1) * D].rearrange("(c p) n -> p c n", p=128)
            P.dma("pool", f_dma(WM[s][:], src), "wm%d" % s, writes=[tWM[s]])
            pb = bankfn()
            ps = PSF[pb]
            for fc in range(8):
                for kc in range(8):
                    P.op("pe", f_mm(ps[:, fc * 3:fc * 3 + 3], WM[s][:, kc, fc * 128:(fc + 1) * 128],
                                    CA[:, kc * 3:kc * 3 + 3], kc == 0, kc == 7),
                         reads=[tWM[s], tCA], writes=[tPSF[pb]])
            o = (l * 6 + k) * 24
            for seg in range(3):
                P.op("dve", f_tt(MOD[:, o + seg:o + 24:3], ps[:, seg:24:3], vec_ap(l, V_BMOD + k * 8, 8), ALU.add),
                     reads=[tPSF[pb], tVEC], writes=[tMOD])
            if k in (1, 4):
                gof = V_G1 if k == 1 else V_G2
                for seg in range(3):
                    P.op("dve", f_stt(MOD[:, o + seg:o + 24:3], MOD[:, o + seg:o + 24:3], 1.0, vec_ap(l, gof, 8),
                                      ALU.add, ALU.mult), reads=[tMOD, tVEC], writes=[tMOD])
            yield

    WM0 = [ar.alloc("wm", [128, 8, D], BF16) for _ in range(2)]
    tWM0 = trks("wm", 2)
    pbk0 = [0]

    def pbank0():
        pbk0[0] += 1
        return pbk0[0] % 6

    for _ in mod_gen(0, WM0, tWM0, pbank0):
        pass
    E1 = ar.alloc("e1", [128, DEPTH * 8], F32)
    E2 = ar.alloc("e2", [128, DEPTH * 8], F32)
    tE = Trk("e")
    for l in range(L):
        sl = slice(l * 8, l * 8 + 8)
        P.op("act", f_act(E1[:, sl], vec_ap(l, V_LAM, 8), AF.Exp, scale=-1.0), reads=[tVEC], writes=[tE])
        P.op("dve", f_ts(E2[:, sl], E1[:, sl], -0.25, 1.0 / 3, ALU.mult, ALU.add), reads=[tE], writes=[tE])
        P.op("dve", f_tt(E2[:, sl], E2[:, sl], E1[:, sl], ALU.mult), reads=[tE], writes=[tE])
        P.op("dve", f_ts(E2[:, sl], E2[:, sl], -1.0, 0.5, ALU.mult, ALU.add), reads=[tE], writes=[tE])
        P.op("dve", f_tt(E2[:, sl], E2[:, sl], E1[:, sl], ALU.mult), reads=[tE], writes=[tE])
        P.op("dve", f_ts(E2[:, sl], E2[:, sl], -1.0, 1.0, ALU.mult, ALU.add), reads=[tE], writes=[tE])
        P.op("dve", f_tt(E2[:, sl], E2[:, sl], E1[:, sl], ALU.mult), reads=[tE], writes=[tE])
        P.op("dve", f_ts(C8[:, sl], E2[:, sl], -4.0, None, ALU.mult), reads=[tE], writes=[tC8])
        P.op("dve", f_ts(HB2[:, l * 16:l * 16 + 16], vec_ap(l, V_BRG, 16), 0.5, None, ALU.mult), reads=[tVEC], writes=[tC8])
        P.op("dve", f_ts(SK8[:, sl], vec_ap(l, V_SINK, 8), 8.0, None, ALU.mult), reads=[tVEC], writes=[tSK8])
    P.barrier()
    ar.reset(base_mark)

    def norm_gen(XTt, tX, HN, tHN, TMP, tTMP, RS, tRS, l, kg, ksh, seg, mmb):
        P.op("act", f_act(HN[:, :, :], XTt[:, :, :], AF.Square), reads=tX, writes=tHN)
        yield
        ps = PSF[mmb]
        for c in range(8):
            P.op("pe", f_mm(ps[:, :], ONES[:, :], HN[:, c, :], c == 0, c == 7), reads=[tHN[c], tCONST], writes=[tPSF[mmb]])
        P.op("act", f_act(TMP[0][:, :], ps[:, :], AF.Sqrt, bias=EPSC[:, 0:1]), reads=[tPSF[mmb], tCONST], writes=[tTMP[0]])
        P.op("dve", lambda h, o=RS[:, :], i=TMP[0][:, :]: h.reciprocal(out=o, in_=i), reads=[tTMP[0]], writes=[tRS])
        yield
        for c in range(8):
            b = 1 + (c % 2)
            P.op("dve", f_stt(TMP[b][:, :], XTt[:, c, :], mod_ap(l, kg, c, seg), RS[:, :], ALU.mult, ALU.mult),
                 reads=[tX[c], tMOD, tRS], writes=[tTMP[b]])
            P.op("act", f_act(HN[:, c, :], TMP[b][:, :], AF.Identity, bias=mod_ap(l, ksh, c, seg)),
                 reads=[tTMP[b], tMOD], writes=[tHN[c]])
            if c % 2 == 1:
                yield

    def norm_mod(*a):
        for _ in norm_gen(*a):
            pass

    def interleave(gens):
        gens = list(gens)
        while gens:
            for g in list(gens):
                try:
                    next(g)
                except StopIteration:
                    gens.remove(g)

    def g1(l, d, c, XCB, tXCB, WG, tWG, GB, tGB, bankfn):
        for g, nm in ((0, "RA"), (1, "IU")):
            mb = bankfn()
            hb = HB2[:, l * 16 + g * 8 + d * 4 + c:l * 16 + g * 8 + d * 4 + c + 1]
            P.op("pe", f_mm(PSF[mb][:, :], WG[:, (d * 2 + g) * 4 + c, :], XCB[:, c, :], True, True),
                 reads=[tWG, tXCB[c]], writes=[tPSF[mb]])
            P.op("act", f_act(GB[nm][c][:, :], PSF[mb][:, :], AF.Tanh, bias=hb, scale=0.5),
                 reads=[tPSF[mb], tC8], writes=[tGB[nm][c]])
        hc8 = C8[:, l * 8 + d * 4 + c:l * 8 + d * 4 + c + 1]
        P.op("act", f_act(GB["RA"][c][:, :], GB["RA"][c][:, :], AF.Exp, bias=hc8, scale=hc8),
             reads=[tGB["RA"][c], tC8], writes=[tGB["RA"][c]])
        P.op("act", f_act(GB["S"][c][:, :], GB["RA"][c][:, :], AF.Square), reads=[tGB["RA"][c]], writes=[tGB["S"][c]])

    def g2(c, XC, tXC, GB, tGB):
        P.op("dve", f_ts(GB["S"][c][:, :], GB["S"][c][:, :], 0.99999994, -1.0, ALU.min, ALU.mult),
             reads=[tGB["S"][c]], writes=[tGB["S"][c]])
        P.op("dve", f_stt(GB["IU"][c][:, :], GB["IU"][c][:, :], 1.0, XC[:, c, :], ALU.add, ALU.mult),
             reads=[tGB["IU"][c], tXC[c]], writes=[tGB["IU"][c]])

    def gsq(c, GB, tGB):
        P.op("act", f_act(GB["S"][c][:, :], GB["S"][c][:, :], AF.Sqrt, bias=QUARTER[:, 0:1], scale=0.25),
             reads=[tGB["S"][c], tCONST], writes=[tGB["S"][c]])

    def g3(c, GB, tGB):
        P.op("dve", f_tt(GB["IU"][c][:, :], GB["IU"][c][:, :], GB["S"][c][:, :], ALU.mult),
             reads=[tGB["IU"][c], tGB["S"][c]], writes=[tGB["IU"][c]])

    def gates_gen(l, d, XC, tXC, XCB, tXCB, WG, tWG, GB, tGB, bankfn):
        for c in range(4):
            g1(l, d, c, XCB, tXCB, WG, tWG, GB, tGB, bankfn)
            yield
        for c in range(4):
            g2(c, XC, tXC, GB, tGB)
        yield
        for c in range(4):
            gsq(c, GB, tGB)
        yield
        for c in range(4):
            g3(c, GB, tGB)
        yield

    def alloc_gb():
        GB = {n: [ar.alloc("gb" + n, [128, T], F32) for _ in range(4)] for n in ("RA", "IU", "S")}
        tGB = {n: trks("gb" + n, 4) for n in ("RA", "IU", "S")}
        return GB, tGB

    final_out = []
    xin_d, xin_t = xT_d, TX["in"]
    for l in range(L):
        xout_d, xout_t = (yT_d, TX["y"]) if l == L - 1 else (xa_d, TX["xa"])
        ar.reset(base_mark)
        WIN = ar.alloc("win", [128, 8, DIN], BF16)
        WG = ar.alloc("wg", [128, 16, 128], BF16)
        tWIN, tWG = Trk("win"), Trk("wg")
        wsrc = win_d[l].rearrange("(c p) n -> p c n", p=128)
        for q4 in range(4):
            P.dma("pool", f_dma(WIN[:, :, q4 * 448:(q4 + 1) * 448], wsrc[:, :, q4 * 448:(q4 + 1) * 448]),
                  "win%d" % q4, writes=[tWIN])
        P.dma("pool", f_dma(WG[:], wg_d[l].rearrange("j p m -> p j m")), "wg", writes=[tWG])
        XTt = ar.alloc("xt", [128, 8, T], F32)
        tX = trks("x", 8)
        HN = ar.alloc("hn", [128, 8, T], BF16)
        tHN = trks("hn", 8)
        TMP = [ar.alloc("tmp", [128, T], F32) for _ in range(3)]
        tTMP = trks("tmp", 3)
        RS = ar.alloc("rs", [128, T], F32)
        tRS = Trk("rs")
        S = 8 * T
        KU = ar.alloc("ku", [128, S], BF16)
        tKU = trks("ku", 8)
        VU = ar.alloc("vu", [128, 32, 128], BF16)
        tVU = trks("vu", 8)
        QR = ar.alloc("qr", [128, 4, 2, T], BF16)
        tQR = trks("qr", 2)
        XR = ar.alloc("xr", [128, 4, 3, T + 3], F32)
        tXR = trks("xr", 3)
        XC = [ar.alloc("xc", [128, 4, T], F32)] * 2
        tXCs = [trks("xc", 4)] * 2
        CARRY = ar.alloc("carry", [128, 4], F32)
        tCARRY = trks("carry", 4)
        XCB = ar.alloc("xcb", [128, 4, T], BF16)
        tXCB = trks("xcb", 4)
        GB, tGB = alloc_gb()
        HF = [ar.alloc("hf", [128, 4, T], F32)] * 2
        tHF = [trks("hf", 4)] * 2
        GG = [ar.alloc("gg", [128, 4, T], BF16)] * 2
        tGG = [Trk("gg")] * 2
        ANT = [ar.alloc("ant", [128, 4, T], BF16) for _ in range(2)]
        tANT = trks("ant", 2)
        TT = [ar.alloc("tt", [128, 384], F32) for _ in range(4)]
        tTT = trks("tt", 4)
        PM = [ar.alloc("pm", [128, 384], BF16) for _ in range(4)]
        tPM = trks("pm", 4)
        PT = [ar.alloc("pt", [128, 3, 128], BF16) for _ in range(4)]
        tPT = trks("pt", 4)
        ST = ar.alloc("st", [128, 128], F32)
        tSTp = [{n: Trk("st" + n) for n in ("es", "rden", "ssq", "rsa")} for _ in range(2)]
        tMXp, tNEGBp, tRSUMp = [trks("mx", 8) for _ in range(2)], [trks("negb", 8) for _ in range(2)], [trks("rsum", 8) for _ in range(2)]
        tPSBh = trks("psbh", 2)
        ATT = ar.alloc("att", [128, 512], F32)
        tATT = Trk("att")
        ATN = ar.alloc("atn", [128, 512], BF16)
        tATN = Trk("atn")
        JUNK = ar.alloc("junk", [128, 512], BF16)
        CTMP = ar.alloc("ctmp", [128, T], F32)
        tCTMP = Trk("ctmp")
        tJUNK = Trk("junk")
        pending_tail = [None]
        for (ut0, unt, ulink) in UNITS:
            mmrr = [0]

            def mmbank():
                b = mmrr[0] % 3
                mmrr[0] += 1
                return b

            gbank = [0]

            def gatebank():
                b = 3 + gbank[0] % 2
                gbank[0] += 1
                return b

            hcnt = [0]

            def lru_conv(j):
                s = j % 3
                for c in range(4):
                    cw = lambda tap: vec_ap(l, V_CW + tap * 4 + c)
                    P.op("pool", f_ts(XC[0][:, c, :], XR[:, c, s, 0:T], cw(0), vec_ap(l, V_CB + c), ALU.mult, ALU.add),
                         reads=[tXR[s], tVEC], writes=[tXCs[0][c]])
                    for tap in (1, 2, 3):
                        P.op("pool", f_ts(CTMP[:, :], XR[:, c, s, tap:tap + T], cw(tap), 0.0, ALU.mult, ALU.add),
                             reads=[tXR[s], tVEC], writes=[tCTMP])
                        P.op("pool", f_tt(XC[0][:, c, :], XC[0][:, c, :], CTMP[:, :], ALU.add),
                             reads=[tCTMP, tXCs[0][c]], writes=[tXCs[0][c]])
                    P.op("pool", f_cp(XCB[:, c, :], XC[0][:, c, :]), reads=[tXCs[0][c]], writes=[tXCB[c]])

            def lruG(j, ut0=ut0, ulink=ulink):
                gi = ut0 + j
                for c in range(4):
                    g1(l, 0, c, XCB, tXCB, WG, tWG, GB, tGB, gatebank)
                    yield
                for c in range(4):
                    g2(c, XC[0], tXCs[0], GB, tGB)
                yield
                for c in range(4):
                    gsq(c, GB, tGB)
                yield
                for c in range(4):
                    g3(c, GB, tGB)
                    if ulink is not None and j == ulink:
                        P.op("dve", f_ts(GB["RA"][c][:, 0:1], GB["RA"][c][:, 0:1], LINK[:, 0:1], None, ALU.mult),
                             reads=[tGB["RA"][c], tLINK], writes=[tGB["RA"][c]])
                    init = 0.0 if j == 0 else CARRY[:, c:c + 1]
                    rd = [tGB["RA"][c], tGB["IU"][c]] + ([] if j == 0 else [tCARRY[c]])
                    P.op("dve", f_scan(HF[0][:, c, :], GB["RA"][c][:, :], GB["IU"][c][:, :], init), reads=rd, writes=[tHF[0][c]])
                    P.op("dve", f_cp(CARRY[:, c:c + 1], HF[0][:, c, T - 1:T]), reads=[tHF[0][c]], writes=[tCARRY[c]])
                    yield
                t0 = gi * T
                P.dma("sp", f_dma(dview(xc_d, t0), XC[0][:, :, :]), "xco", reads=tXCs[0], writes=[TXC[gi]])
                P.dma("sp", f_dma(dview(hf_d, t0), HF[0][:, :, :]), "hfo", reads=tHF[0], writes=[THF[gi]])

            def mix(j, ngen=None):
                gi = ut0 + j
                qs = j % 2
                asl = j % 2
                nblk = unt * 4
                lru_conv(j)

                def lru_step(p):
                    pass

                def geom(b):
                    n = 4 * j + b
                    lo = max(n - 1, 0)
                    hi = min(n + 1, nblk - 1)
                    nkb = hi - lo + 1
                    doff = 0 if lo == n - 1 else 128
                    tab = 0
                    if ulink is not None and n == ulink * 4 - 1:
                        tab = 1
                    if ulink is not None and n == ulink * 4:
                        tab = 2
                    tl = sorted(set((lo // 4, hi // 4)))
                    return n, lo, hi, nkb, doff, tab, [tKU[tt] for tt in tl], [tVU[tt] for tt in tl]

                sbank = {}

                def SA(p):
                    b, c = p // 4, p % 4
                    n, lo, hi, nkb, doff, tab, ktr, vtr = geom(b)
                    nk = nkb * 128
                    for hh in range(2):
                        sb = hcnt[0] % 3
                        hcnt[0] += 1
                        sbank[(p, hh)] = sb
                        pl, ph = hh * 64, hh * 64 + 64
                        P.op("pe", f_mm(PSF[sb][:, 0:nk], QR[pl:ph, c, qs, b * 128:(b + 1) * 128],
                                        KU[pl:ph, lo * 128:(hi + 1) * 128], True, True),
                             reads=[tQR[qs]] + ktr, writes=[tPSF[sb]])

                def SB(p, hh):
                    b, c = p // 4, p % 4
                    n, lo, hi, nkb, doff, tab, ktr, vtr = geom(b)
                    nk = nkb * 128
                    so = (b % 2) * 64
                    Dv = DT[:, tab * 384 + doff:tab * 384 + doff + nk]
                    if True:
                        h_ = c + 4 * hh
                        sb = sbank[(p, hh)]
                        r4 = (c * 2 + hh) % 4
                        slope8 = 8.0 * 2.0 ** (-(h_ + 1))
                        P.op("dve", f_stt(TT[r4][:, 0:nk], Dv, slope8, PSF[sb][:, 0:nk], ALU.mult, ALU.add),
                             reads=[tDT, tPSF[sb]], writes=[tTT[r4]])
                        P.op("dve", lambda h, o=ST[:, so + h_:so + h_ + 1], i=TT[r4][:, 0:nk]: h.reduce_max(out=o, in_=i, axis=AX.X),
                             reads=[tTT[r4]], writes=[tMXp[b % 2][h_]])
                        P.op("dve", f_ts(ST[:, so + 8 + h_:so + 9 + h_], ST[:, so + h_:so + h_ + 1],
                                         SK8[:, l * 8 + h_:l * 8 + h_ + 1], -0.125, ALU.max, ALU.mult),
                             reads=[tMXp[b % 2][h_], tSK8], writes=[tNEGBp[b % 2][h_]])
                        P.op("act", f_act(PM[r4][:, 0:nk], TT[r4][:, 0:nk], AF.Exp, bias=ST[:, so + 8 + h_:so + 9 + h_], scale=0.125,
                                          accum_out=ST[:, so + 16 + h_:so + 17 + h_]),
                             reads=[tTT[r4], tNEGBp[b % 2][h_]], writes=[tPM[r4], tRSUMp[b % 2][h_]])

                def SC(p):
                    b, c = p // 4, p % 4
                    n, lo, hi, nkb, doff, tab, ktr, vtr = geom(b)
                    nk = nkb * 128
                    for hh in range(2):
                        r4 = (c * 2 + hh) % 4
                        pbk = hh
                        for jj in range(nkb):
                            P.op("pe", f_tr(PSB[pbk][:, jj * 128:(jj + 1) * 128],
                                            PM[r4][:, jj * 128:(jj + 1) * 128], IDN[:, :]),
                                 reads=[tPM[r4], tCONST], writes=[tPSB[pbk]])
                        P.op("act", f_act(PT[r4][:, 0:nkb, :],
                                          PSB[pbk][:, 0:nk].rearrange("p (j q) -> p j q", q=128), AF.Copy),
                             reads=[tPSB[pbk]], writes=[tPT[r4]])

                def SD(p):
                    b, c = p // 4, p % 4
                    n, lo, hi, nkb, doff, tab, ktr, vtr = geom(b)
                    for hh in range(2):
                        h_ = c + 4 * hh
                        r4 = (c * 2 + hh) % 4
                        for jj in range(nkb):
                            P.op("pe", f_mm(PSF[5][:, h_ * 64:(h_ + 1) * 64], PT[r4][:, jj, :],
                                            VU[:, lo + jj, hh * 64:(hh + 1) * 64], jj == 0, jj == nkb - 1),
                                 reads=[tPT[r4]] + vtr, writes=[tPSF[5]])

                def EPI(b, k):
                    so = (b % 2) * 64
                    tST = tSTp[b % 2]
                    if k == 0:
                        P.op("dve", f_tt(ST[:, so + 24:so + 32], ST[:, so + 8:so + 16], vec_ap(l, V_SINK, 8), ALU.add),
                             reads=tNEGBp[b % 2] + [tVEC], writes=[tST["es"]])
                        P.op("act", f_act(ST[:, so + 24:so + 32], ST[:, so + 24:so + 32], AF.Exp), reads=[tST["es"]], writes=[tST["es"]])
                    elif k == 1:
                        P.op("dve", f_tt(ST[:, so + 32:so + 40], ST[:, so + 24:so + 32], ST[:, so + 16:so + 24], ALU.add),
                             reads=[tST["es"]] + tRSUMp[b % 2], writes=[tST["rden"]])
                        P.op("dve", lambda h, o=ST[:, so + 32:so + 40]: h.reciprocal(out=o, in_=o), reads=[tST["rden"]], writes=[tST["rden"]])
                        P.op("dve", f_tt(ATT[:, :].rearrange("p (h d) -> p h d", d=64),
                                         PSF[5][:, :].rearrange("p (h d) -> p h d", d=64),
                                         ST[:, so + 32:so + 40].unsqueeze(2).to_broadcast([128, 8, 64]), ALU.mult),
                             reads=[tPSF[5], tST["rden"]], writes=[tATT])
                    elif k == 2:
                        P.op("act", f_act(JUNK[:, :], ATT[:, :], AF.Square, accum_out=ST[:, so + 40:so + 41]),
                             reads=[tATT], writes=[tJUNK, tST["ssq"]])
                    elif k == 3:
                        P.op("dve", f_ts(ST[:, so + 41:so + 42], ST[:, so + 40:so + 41], 1.0 / 512, EPS, ALU.mult, ALU.add),
                             reads=[tST["ssq"]], writes=[tST["rsa"]])
                        P.op("pool", f_tt(ST[:, so + 41:so + 42], ST[:, so + 41:so + 42], NHALFC[:, 0:1], ALU.pow),
                             reads=[tST["rsa"], tCONST], writes=[tST["rsa"]])
                    elif k == 4:
                        P.op("dve", f_ts(ATN[:, :], ATT[:, :], ST[:, so + 41:so + 42], None, ALU.mult),
                             reads=[tATT, tST["rsa"]], writes=[tATN])
                    elif k == 5:
                        for cc in range(4):
                            P.op("pe", f_tr(PSB[1][:, 512 + cc * 128:512 + (cc + 1) * 128], ATN[:, cc * 128:(cc + 1) * 128], IDN[:, :]),
                                 reads=[tATN, tCONST], writes=[tPSB[1]])
                        for cc in range(4):
                            P.op("act", f_act(ANT[asl][:, cc, b * 128:(b + 1) * 128], PSB[1][:, 512 + cc * 128:512 + (cc + 1) * 128],
                                              AF.Identity, scale=vec_ap(l, V_GA + cc)),
                                 reads=[tPSB[1], tVEC], writes=[tANT[asl]])

                NP = 16
                SA(0)
                for p in range(NP + 6):
                    if p < NP:
                        SB(p, 0)
                    if p + 1 < NP:
                        SA(p + 1)
                    if p < NP:
                        SB(p, 1)
                    if 0 <= p - 2 < NP:
                        SD(p - 2)
                    for b_ in range(4):
                        k_ = p - (4 * b_ + 4)
                        if 0 <= k_ < 6:
                            EPI(b_, k_)
                    if 0 <= p - 1 < NP:
                        SC(p - 1)
                    lru_step(p)
                    if ngen is not None and p >= 2:
                        next(ngen, None)
                if ngen is not None:
                    for _ in ngen:
                        pass
                P.dma("sp", f_dma(dview(an_d, gi * T), ANT[asl][:, :, :]), "ano%d" % asl, reads=[tANT[asl]], writes=[TAN[gi]])

            def xload(i):
                gi = ut0 + i
                P.dma("sp", f_dma(XTt[:, :, :], dview(xin_d, gi * T)), "xt", reads=[xin_t[gi]], writes=tX)

            def ngen_for(i):
                return norm_gen(XTt, tX, HN, tHN, TMP, tTMP, RS, tRS, l, 1, 0, (ut0 + i) // 4, gatebank())

            tail = pending_tail[0]
            pending_tail[0] = None
            xload(0)
            for _ in ngen_for(0):
                if tail is not None:
                    next(tail, None)
            def proj(i):
                gi = ut0 + i
                seg = gi // 4
                t0 = gi * T
                s = i % 2
                xs_ = i % 3
                xp_ = (i - 1) % 3
                for m in range(4):
                    mb = mmbank()
                    for c in range(8):
                        P.op("pe", f_mm(PSF[mb][:, :], WIN[:, c, m * 128:(m + 1) * 128], HN[:, c, :], c == 0, c == 7),
                             reads=[tWIN, tHN[c]], writes=[tPSF[mb]])
                    P.op("act", f_act(QR[:, m, s, :], PSF[mb][:, :], AF.Copy), reads=[tPSF[mb]], writes=[tQR[s]])
                    yield
                mb = mmbank()
                for c in range(8):
                    P.op("pe", f_mm(PSF[mb][:, :], WIN[:, c, 512:640], HN[:, c, :], c == 0, c == 7),
                         reads=[tWIN, tHN[c]], writes=[tPSF[mb]])
                P.op("dve", f_cp(KU[:, i * T:(i + 1) * T], PSF[mb][:, :]), reads=[tPSF[mb]], writes=[tKU[i]])
                yield
                mb = mmbank()
                for b in range(4):
                    for c in range(8):
                        P.op("pe", f_mm(PSF[mb][:, b * 128:(b + 1) * 128], HN[:, c, b * 128:(b + 1) * 128],
                                        WIN[:, c, 640:768], c == 0, c == 7),
                             reads=[tWIN, tHN[c]], writes=[tPSF[mb]])
                P.op("act", f_act(VU[:, i * 4:(i + 1) * 4, :], PSF[mb][:, :].rearrange("p (b f) -> p b f", f=128), AF.Copy),
                     reads=[tPSF[mb]], writes=[tVU[i]])
                yield
                for m in range(4):
                    mb = mmbank()
                    for c in range(8):
                        P.op("pe", f_mm(PSF[mb][:, :], WIN[:, c, 768 + m * 128:768 + (m + 1) * 128], HN[:, c, :], c == 0, c == 7),
                             reads=[tWIN, tHN[c]], writes=[tPSF[mb]])
                    P.op("dve", f_cp(XR[:, m, xs_, 2:T + 2], PSF[mb][:, :]), reads=[tPSF[mb]], writes=[tXR[xs_]])
                    yield
                linked = (ulink is not None and i == ulink)
                if i == 0:
                    P.op("pool", lambda h, o=XR[:, :, xs_, 0:2]: h.memset(o, 0.0), writes=[tXR[xs_]])
                else:
                    if linked:
                        P.op("pool", f_ts(XR[:, :, xs_, 0:2], XR[:, :, xp_, T:T + 2], LINK[:, 0:1], 0.0, ALU.mult, ALU.add),
                             reads=[tXR[xp_], tLINK], writes=[tXR[xs_]])
                        P.op("pool", f_ts(XR[:, :, xp_, T + 2:T + 3], XR[:, :, xs_, 2:3], LINK[:, 0:1], 0.0, ALU.mult, ALU.add),
                             reads=[tXR[xs_], tLINK], writes=[tXR[xp_]])
                    else:
                        P.op("pool", f_cp(XR[:, :, xs_, 0:2], XR[:, :, xp_, T:T + 2]), reads=[tXR[xp_]], writes=[tXR[xs_]])
                        P.op("pool", f_cp(XR[:, :, xp_, T + 2:T + 3], XR[:, :, xs_, 2:3]), reads=[tXR[xs_]], writes=[tXR[xp_]])
                if i == unt - 1:
                    P.op("pool", lambda h, o=XR[:, :, xs_, T + 2:T + 3]: h.memset(o, 0.0), writes=[tXR[xs_]])
                for m in range(4):
                    mb = mmbank()
                    for c in range(8):
                        P.op("pe", f_mm(PSF[mb][:, :], WIN[:, c, 1280 + m * 128:1280 + (m + 1) * 128], HN[:, c, :], c == 0, c == 7),
                             reads=[tWIN, tHN[c]], writes=[tPSF[mb]])
                    P.op("act", f_act(GG[s][:, m, :], PSF[mb][:, :], AF.Gelu_apprx_tanh), reads=[tPSF[mb]], writes=[tGG[s]])
                    yield
                P.dma("sp", f_dma(dview(gg_d, t0), GG[s][:, :, :]), "ggo", reads=[tGG[s]], writes=[TGG[gi]])

            for i in range(unt + 2):
                if i + 1 < unt:
                    xload(i + 1)
                gens = []
                if i < unt:
                    gens.append(proj(i))
                if 0 <= i - 2 < unt:
                    if i == unt + 1:
                        pending_tail[0] = lruG(i - 2)
                    else:
                        gens.append(lruG(i - 2))
                if i == 0 and tail is not None:
                    gens.append(tail)
                interleave(gens)
                ng = ngen_for(i + 1) if i + 1 < unt else None
                if 1 <= i <= unt:
                    mix(i - 1, ng)
                elif ng is not None:
                    for _ in ng:
                        pass
        if pending_tail[0] is not None:
            for _ in pending_tail[0]:
                pass
            pending_tail[0] = None
        P.barrier()

        ar.reset(base_mark)
        WG = ar.alloc("wg", [128, 16, 128], BF16)
        WOUT = ar.alloc("wout", [128, 8, D], BF16)
        tWG, tWOUT = Trk("wg"), Trk("wout")
        P.dma("pool", f_dma(WG[:], wg_d[l].rearrange("j p m -> p j m")), "wg", writes=[tWG])
        wsrc = wout_d[l].rearrange("(c p) n -> p c n", p=128)
        for q2 in range(2):
            P.dma("pool", f_dma(WOUT[:, :, q2 * 512:(q2 + 1) * 512], wsrc[:, :, q2 * 512:(q2 + 1) * 512]),
                  "wout%d" % q2, writes=[tWOUT])
        XTb = [ar.alloc("xt", [128, 8, T], F32) for _ in range(2)]
        tXb = [trks("x", 8) for _ in range(2)]
        XCi = [ar.alloc("xci", [128, 4, T], F32) for _ in range(2)]
        tXCi = [trks("xci", 4) for _ in range(2)]
        HFi = [ar.alloc("hfi", [128, 4, T], F32) for _ in range(2)]
        tHFi = trks("hfi", 2)
        GGi = [ar.alloc("ggi", [128, 4, T], BF16) for _ in range(2)]
        tGGi = trks("ggi", 2)
        ANi = [ar.alloc("ani", [128, 4, T], BF16) for _ in range(2)]
        tANi = trks("ani", 2)
        XCB = ar.alloc("xcb", [128, 4, T], BF16)
        tXCB = trks("xcb", 4)
        GB, tGB = alloc_gb()
        HB = [ar.alloc("hb", [128, 4, T], F32) for _ in range(2)]
        tHB = [trks("hb", 4) for _ in range(2)]
        LR = ar.alloc("lr", [128, 4, T], F32)
        tLR = trks("lr", 4)
        SQ = ar.alloc("sq", [128, 4, T], BF16)
        tSQ = Trk("sq")
        LN = ar.alloc("ln", [128, 4, T], BF16)
        tLN = trks("ln", 4)
        TMPb = ar.alloc("tmp", [128, T], F32)
        tTMPb = Trk("tmp")
        RSb = ar.alloc("rs", [128, T], F32)
        tRSb = Trk("rs")
        WMb = [ar.alloc("wm", [128, 8, D], BF16) for _ in range(2)]
        tWMb = trks("wm", 2)
        rr = [0]

        def bank6():
            b = rr[0] % 6
            rr[0] += 1
            return b

        modg = mod_gen(l + 1, WMb, tWMb, bank6) if l + 1 < L else None
        for (ut0, unt, ulink) in UNITS:

            def frontB(idx):
                j = unt - 1 - idx
                gi = ut0 + j
                t0 = gi * T
                s = idx % 2
                P.dma("sp", f_dma(XCi[s][:, :, :], dview(xc_d, t0)), "xci%d" % s, reads=[TXC[gi]], writes=tXCi[s])
                P.dma("sp", f_dma(HFi[s][:, :, :], dview(hf_d, t0)), "hfi%d" % s, reads=[THF[gi]], writes=[tHFi[s]])
                P.dma("sp", f_dma(GGi[s][:, :, :], dview(gg_d, t0)), "ggi%d" % s, reads=[TGG[gi]], writes=[tGGi[s]])
                P.dma("sp", f_dma(ANi[s][:, :, :], dview(an_d, t0)), "ani%d" % s, reads=[TAN[gi]], writes=[tANi[s]])
                P.dma("sp", f_dma(XTb[s][:, :, :], dview(xin_d, t0)), "xtb%d" % s, reads=[xin_t[gi]], writes=tXb[s])
                for c in range(4):
                    P.op("pool", f_cp(XCB[:, c, :], XCi[s][:, c, :]), reads=[tXCi[s][c]], writes=[tXCB[c]])
                yield
                yield from gates_gen(l, 1, XCi[s], tXCi[s], XCB, tXCB, WG, tWG, GB, tGB, bank6)
                for c in range(4):
                    if ulink is not None and j == ulink - 1:
                        P.op("dve", f_ts(GB["RA"][c][:, T - 1:T], GB["RA"][c][:, T - 1:T], LINK[:, 0:1], None, ALU.mult),
                             reads=[tGB["RA"][c], tLINK], writes=[tGB["RA"][c]])
                    init = 0.0 if idx == 0 else HB[1 - s][:, c, 0:1]
                    rd = [tGB["RA"][c], tGB["IU"][c]] + ([] if idx == 0 else [tHB[1 - s][c]])
                    P.op("dve", f_scan(HB[s][:, c, ::-1], GB["RA"][c][:, ::-1], GB["IU"][c][:, ::-1], init),
                         reads=rd, writes=[tHB[s][c]])
                    if c % 2 == 1:
                        yield

            def backB(idx):
                j = unt - 1 - idx
                gi = ut0 + j
                seg = gi // 4
                t0 = gi * T
                s = idx % 2
                for c in range(4):
                    P.op("dve", f_tt(LR[:, c, :], HFi[s][:, c, :], HB[s][:, c, :], ALU.add),
                         reads=[tHFi[s], tHB[s][c]], writes=[tLR[c]])
                    P.op("dve", f_tt(LR[:, c, :], LR[:, c, :], GGi[s][:, c, :], ALU.mult),
                         reads=[tLR[c], tGGi[s]], writes=[tLR[c]])
                    if c % 2 == 1:
                        yield
                P.op("act", f_act(SQ[:, :, :], LR[:, :, :], AF.Square), reads=tLR, writes=[tSQ])
                mb = bank6()
                for c in range(4):
                    P.op("pe", f_mm(PSF[mb][:, :], ONES5[:, :], SQ[:, c, :], c == 0, c == 3), reads=[tSQ, tCONST], writes=[tPSF[mb]])
                P.op("act", f_act(TMPb[:, :], PSF[mb][:, :], AF.Ln, bias=EPSC[:, 0:1]), reads=[tPSF[mb], tCONST], writes=[tTMPb])
                P.op("act", f_act(RSb[:, :], TMPb[:, :], AF.Exp, scale=-0.5), reads=[tTMPb], writes=[tRSb])
                yield
                for c in range(4):
                    P.op("dve", f_stt(LN[:, c, :], LR[:, c, :], vec_ap(l, V_GL + c), RSb[:, :], ALU.mult, ALU.mult),
                         reads=[tLR[c], tVEC, tRSb], writes=[tLN[c]])
                yield
                for m in range(8):
                    mb = bank6()
                    for c in range(4):
                        P.op("pe", f_mm(PSF[mb][:, :], WOUT[:, c, m * 128:(m + 1) * 128], ANi[s][:, c, :], c == 0, False),
                             reads=[tWOUT, tANi[s]], writes=[tPSF[mb]])
                    for c in range(4):
                        P.op("pe", f_mm(PSF[mb][:, :], WOUT[:, 4 + c, m * 128:(m + 1) * 128], LN[:, c, :], False, c == 3),
                             reads=[tWOUT, tLN[c]], writes=[tPSF[mb]])
                    P.op("dve", f_stt(XTb[s][:, m, :], PSF[mb][:, :], mod_ap(l, 2, m, seg), XTb[s][:, m, :], ALU.mult, ALU.add),
                         reads=[tPSF[mb], tMOD, tXb[s][m]], writes=[tXb[s][m]])
                    if m % 2 == 1:
                        yield
                P.dma("sp", f_dma(dview(xb_d, t0), XTb[s][:, :, :]), "xbo%d" % s, reads=tXb[s], writes=[TX["xb"][gi]])

            interleave([frontB(0)])
            for idx in range(unt):
                gens = [backB(idx)]
                if idx + 1 < unt:
                    gens.append(frontB(idx + 1))
                interleave(gens)
                if modg is not None:
                    next(modg, None)
        if modg is not None:
            for _ in modg:
                pass
        P.barrier()

        ar.reset(base_mark)
        WFI = ar.alloc("wfi", [128, 8, 2 * DFF], BF16)
        WFO = ar.alloc("wfo", [128, 22, D], BF16)
        tWFI = trks("wfi", 8)
        tWFO = trks("wfo", 4)
        JB = (0, 6, 12, 17, 22)
        wsrc = wfi_d[l].rearrange("(c p) n -> p c n", p=128)
        for q8 in (0, 4, 1, 5, 2, 6, 3, 7):
            P.dma("pool", f_dma(WFI[:, :, q8 * 704:(q8 + 1) * 704], wsrc[:, :, q8 * 704:(q8 + 1) * 704]),
                  "wfi%d" % q8, writes=[tWFI[q8]])
        wsrc = wfo_d[l].rearrange("(c p) n -> p c n", p=128)
        for q4 in range(4):
            P.dma("pool", f_dma(WFO[:, JB[q4]:JB[q4 + 1], :], wsrc[:, JB[q4]:JB[q4 + 1], :]),
                  "wfo%d" % q4, writes=[tWFO[q4]])
        XTf = [ar.alloc("xt", [128, 8, T], F32) for _ in range(2)]
        tXf = [trks("x", 8) for _ in range(2)]
        HN = ar.alloc("hn", [128, 8, T], BF16)
        tHN = trks("hn", 8)
        TMP = [ar.alloc("tmp", [128, T], F32) for _ in range(3)]
        tTMP = trks("tmp", 3)
        RS = ar.alloc("rs", [128, T], F32)
        tRS = Trk("rs")
        ACTB = ar.alloc("actb", [128, 11, T], BF16)
        tACTB = trks("actb", 11)
        SG = [ar.alloc("sg", [128, T], F32) for _ in range(2)]
        tSG = trks("sg", 2)
        wfo_t = lambda jj: tWFO[0 if jj < 6 else 1 if jj < 12 else 2 if jj < 17 else 3]

        def f_load(gi):
            P.dma("sp", f_dma(XTf[gi % 2][:, :, :], dview(xb_d, gi * T)), "xtf%d" % (gi % 2), reads=[TX["xb"][gi]], writes=tXf[gi % 2])

        def f_norm_gen(gi):
            return norm_gen(XTf[gi % 2], tXf[gi % 2], HN, tHN, TMP, tTMP, RS, tRS, l, 4, 3, gi // 4, 0)

        def f_gu(gi, j0):
            for jl in range(11):
                jj = j0 + jl
                gb, ub = jj % 2, 2 + jj % 2
                for half, pb in ((0, gb), (1, ub)):
                    col = half * DFF + jj * 128
                    for c in range(8):
                        P.op("pe", f_mm(PSF[pb][:, :], WFI[:, c, col:col + 128], HN[:, c, :], c == 0, c == 7),
                             reads=[tWFI[col // 704], tWFI[(col + 127) // 704], tHN[c]], writes=[tPSF[pb]])
                P.op("act", f_act(SG[jj % 2][:, :], PSF[gb][:, :], AF.Silu), reads=[tPSF[gb]], writes=[tSG[jj % 2]])
                P.op("dve", f_tt(ACTB[:, jl, :], SG[jj % 2][:, :], PSF[ub][:, :], ALU.mult),
                     reads=[tSG[jj % 2], tPSF[ub]], writes=[tACTB[jl]])

        def f_out(gi, j0, ng=None):
            XTt, tX, seg = XTf[gi % 2], tXf[gi % 2], gi // 4
            if ng is not None:
                next(ng, None)
            for m in range(8):
                pb = 4 + m % 2
                for jl in range(11):
                    jj = j0 + jl
                    P.op("pe", f_mm(PSF[pb][:, :], WFO[:, jj, m * 128:(m + 1) * 128], ACTB[:, jl, :], jl == 0, jl == 10),
                         reads=[wfo_t(jj), tACTB[jl]], writes=[tPSF[pb]])
                P.op("dve", f_stt(XTt[:, m, :], PSF[pb][:, :], mod_ap(l, 5, m, seg), XTt[:, m, :], ALU.mult, ALU.add),
                     reads=[tPSF[pb], tMOD, tX[m]], writes=[tX[m]])
                if ng is not None and m >= 1:
                    next(ng, None)
            if ng is not None:
                for _ in ng:
                    pass

        def f_fin(gi):
            XTt, tX, t0 = XTf[gi % 2], tXf[gi % 2], gi * T
            if l == L - 1:
                P.op("act", f_act(ACTB[:, 0:8, :], XTt[:, :, :], AF.Square), reads=tX, writes=tACTB[0:8])
                for c in range(8):
                    P.op("pe", f_mm(PSF[4][:, :], ONES[:, :], ACTB[:, c, :], c == 0, c == 7), reads=[tACTB[c], tCONST], writes=[tPSF[4]])
                P.op("act", f_act(SG[0][:, :], PSF[4][:, :], AF.Sqrt, bias=EPSC[:, 0:1]), reads=[tPSF[4], tCONST], writes=[tSG[0]])
                P.op("dve", lambda h, o=SG[1][:, :], i=SG[0][:, :]: h.reciprocal(out=o, in_=i), reads=[tSG[0]], writes=[tSG[1]])
                for c in range(8):
                    P.op("dve", f_stt(XTt[:, c, :], XTt[:, c, :], VEC[:, DEPTH * NV + c:DEPTH * NV + c + 1], SG[1][:, :],
                                      ALU.mult, ALU.mult), reads=[tX[c], tVEC, tSG[1]], writes=[tX[c]])
            o = P.dma("sp", f_dma(dview(xout_d, t0), XTt[:, :, :]), "xfo%d" % (gi % 2), reads=tX, writes=[xout_t[gi]])
            if l == L - 1:
                final_out.append(o)

        f_load(0)
        for _ in f_norm_gen(0):
            pass
        for gi in range(NT):
            if gi + 1 < NT:
                f_load(gi + 1)
            f_gu(gi, 0)
            f_out(gi, 0)
            f_gu(gi, 11)
            f_out(gi, 11, f_norm_gen(gi + 1) if gi + 1 < NT else None)
            f_fin(gi)
        P.barrier()
        xin_d, xin_t = xa_d, TX["xa"]

    P.emit(nc, final_waits=final_out)
    return nc, P


def _segments():
    segs = []
    for k in range(4):
        segs.append([("p", k, 0), ("p", k, 2048), ("s", k, 0)])
    for k in range(4):
        segs.append([("s", 4 + 3 * k + r, 0) for r in range(3)])
    return segs


def _dtab(linked):
    q = np.arange(128)[:, None]
    sidx = np.arange(384)[None, :] - 128
    dist = np.abs(q - sidx).astype(np.float32)
    base = np.where(dist <= 128, -dist, -1.0e9).astype(np.float32)
    d15 = base.copy()
    d16 = base.copy()
    if not linked:
        d15[:, 256:384] = -1.0e9
        d16[:, 0:128] = -1.0e9
    return np.concatenate([base, d15, d16], axis=1).astype(np.float32)


def _fm(v):
    v = np.asarray(v, np.float32)
    return v.reshape(-1, 128).T


_CACHE = {}


def kernel(x_prompt, x_sample, c_prompt, c_sample, w_mod, b_mod, g_norm1, w_in, sink, conv_w, conv_b,
           w_rg, b_rg, w_ig, b_ig, lam, g_attn_out, g_lru_out, w_out, g_norm2, w_ffn_in, w_ffn_out, g_final,
           _depth=DEPTH):
    f32 = np.float32
    segs = _segments()
    xs = {"p": np.asarray(x_prompt, f32), "s": np.asarray(x_sample, f32)}
    cs = {"p": np.asarray(c_prompt, f32), "s": np.asarray(c_sample, f32)}
    vecs = np.zeros((128, DEPTH * NV + 8), f32)
    for l in range(DEPTH):
        o = l * NV
        vecs[:, o + V_G1:o + V_G1 + 8] = _fm(g_norm1[l])
        vecs[:, o + V_G2:o + V_G2 + 8] = _fm(g_norm2[l])
        vecs[:, o + V_BMOD:o + V_BMOD + 48] = _fm(b_mod[l])
        for tap in range(4):
            vecs[:, o + V_CW + tap * 4:o + V_CW + tap * 4 + 4] = _fm(conv_w[l][tap])
        vecs[:, o + V_CB:o + V_CB + 4] = _fm(conv_b[l])
        for d in range(2):
            vecs[:, o + V_BRG + d * 4:o + V_BRG + d * 4 + 4] = _fm(b_rg[l][d])
            vecs[:, o + V_BIG + d * 4:o + V_BIG + d * 4 + 4] = _fm(b_ig[l][d])
            vecs[:, o + V_LAM + d * 4:o + V_LAM + d * 4 + 4] = _fm(lam[l][d])
        vecs[:, o + V_GA:o + V_GA + 4] = _fm(g_attn_out[l])
        vecs[:, o + V_GL:o + V_GL + 4] = _fm(g_lru_out[l])
        vecs[:, o + V_SINK:o + V_SINK + 8] = np.asarray(sink[l], f32)[None, :]
    vecs[:, DEPTH * NV:DEPTH * NV + 8] = _fm(g_final)
    perm = np.concatenate([np.concatenate([np.arange(c * 64, c * 64 + 64), np.arange((4 + c) * 64, (4 + c) * 64 + 64)])
                           for c in range(4)] + [np.arange(512, DIN)])
    w_in_p = np.ascontiguousarray(np.asarray(w_in, f32)[:, :, perm])
    wg = np.zeros((DEPTH, 16, 128, 128), f32)
    for l in range(DEPTH):
        for d in range(2):
            for g, w in ((0, w_rg), (1, w_ig)):
                for c in range(4):
                    j = (d * 2 + g) * 4 + c
                    wg[l, j, 0:64, 0:64] = w[l][d][2 * c]
                    wg[l, j, 64:128, 64:128] = w[l][d][2 * c + 1]
    shared = {"vecs": vecs, "w_mod": np.asarray(w_mod, f32), "w_in": w_in_p, "wg": wg,
              "w_out": np.asarray(w_out, f32), "w_ffn_in": np.asarray(w_ffn_in, f32),
              "w_ffn_out": np.asarray(w_ffn_out, f32)}
    in_maps = []
    for k in range(8):
        xt = np.empty((8, 128, NTOK), f32)
        ct = np.empty((128, 8, 3), f32)
        for si, (kind, b, t0) in enumerate(segs[k]):
            xt[:, :, si * 2048:(si + 1) * 2048] = xs[kind][b, t0:t0 + 2048, :].T.reshape(8, 128, 2048)
            ct[:, :, si] = cs[kind][b].reshape(8, 128).T
        linked = k < 4
        m = dict(shared)
        m["xT"] = xt
        m["cT"] = np.ascontiguousarray(ct.reshape(128, 24))
        m["link"] = np.full((128, 1), 1.0 if linked else 0.0, f32)
        m["dtab"] = _dtab(linked)
        in_maps.append(m)
    if _depth not in _CACHE:
        _CACHE[_depth] = build_program(_depth)[0]
    nc = _CACHE[_depth]
    res = run_bass_kernel_spmd(nc, in_maps, core_ids=list(range(8)))
    y_p = np.empty((4, 4096, D), f32)
    y_s = np.empty((16, 2048, D), f32)
    ys = {"p": y_p, "s": y_s}
    for k in range(8):
        yt = np.asarray(res.results[k]["yT"], f32)
        for si, (kind, b, t0) in enumerate(segs[k]):
            ys[kind][b, t0:t0 + 2048, :] = yt[:, :, si * 2048:(si + 1) * 2048].reshape(D, 2048).T
    return (y_p, y_s)
```

```python
import contextlib
import numpy as np
import concourse.bass as bass
import concourse.mybir as mybir
from concourse.bass_utils import run_bass_kernel_spmd

F32 = mybir.dt.float32
BF16 = mybir.dt.bfloat16
AF = mybir.ActivationFunctionType
ALU = mybir.AluOpType
AX = mybir.AxisListType

D = 1024
DEPTH = 4
NTOK = 6144
T = 512
NT = NTOK // T
DIN = 1792
DFF = 2816
EPS = 1e-6
NV = 124
V_G1, V_G2, V_BMOD, V_CW, V_CB, V_BRG, V_BIG, V_LAM, V_GA, V_GL, V_SINK = 0, 8, 16, 64, 80, 84, 92, 100, 108, 112, 116
UNITS = ((0, 8, 4), (8, 4, None))

ENGS = ("pe", "act", "dve", "pool", "sp")


class Trk:
    __slots__ = ("name", "w", "rs")

    def __init__(self, name=""):
        self.name = name
        self.w = None
        self.rs = []


def trks(name, n):
    return [Trk("%s%d" % (name, i)) for i in range(n)]


class Op:
    __slots__ = ("eng", "idx", "fn", "waits", "dma", "need_inc", "semval", "dkey")

    def __init__(self, eng, idx, fn, dma, dkey):
        self.eng = eng
        self.idx = idx
        self.fn = fn
        self.waits = []
        self.dma = dma
        self.need_inc = False
        self.semval = None
        self.dkey = dkey


class Prog:
    def __init__(self):
        self.ops = {e: [] for e in ENGS}
        self.waited = {}
        self.waited_dma = {}
        self.dma_keys = {}
        self.last_dma = {}
        self.last_comp = {}

    def op(self, eng, fn, reads=(), writes=(), dma=False, dkey=None):
        lst = self.ops[eng]
        o = Op(eng, len(lst), fn, dma, dkey)
        deps = []
        for t in reads:
            if t.w is not None:
                deps.append((t.w, True))
        for t in writes:
            if t.w is not None:
                deps.append((t.w, True))
            deps.extend((r, False) for r in reversed(t.rs))
        for d, isw in deps:
            self._add_wait(o, d, isw)
        for t in reads:
            t.rs.append(o)
        for t in writes:
            t.w = o
            t.rs = []
        lst.append(o)
        if dma:
            self.last_dma[dkey] = o
        else:
            self.last_comp[eng] = o
        return o

    def _add_wait(self, o, d, isw=True, force=False):
        if d is o:
            return
        if d.dma:
            key = (o.eng, d.dkey)
            prev = self.waited_dma.get(key)
            if prev is not None and prev >= d.semval:
                return
            self.waited_dma[key] = d.semval
            o.waits.append(d)
            return
        if d.eng == o.eng and not force:
            if o.eng == "pe":
                return
        key = (o.eng, d.eng)
        prev = self.waited.get(key, -1)
        if prev >= d.idx:
            return
        self.waited[key] = d.idx
        o.waits.append(d)
        d.need_inc = True

    def dma(self, eng, fn, dkey, reads=(), writes=()):
        self.dma_keys.setdefault(dkey, 0)
        self.dma_keys[dkey] += 16
        val = self.dma_keys[dkey]
        o = self.op(eng, fn, reads=reads, writes=writes, dma=True, dkey=dkey)
        o.semval = val
        return o

    def barrier(self):
        comp = dict(self.last_comp)
        dm = dict(self.last_dma)
        spn = self.op("sp", lambda h: h.nop())
        for d in list(comp.values()) + list(dm.values()):
            self._add_wait(spn, d, True, force=True)
        for e in ENGS:
            if e == "sp":
                continue
            n = self.op(e, lambda h: h.nop())
            self._add_wait(n, spn, True)

    def emit(self, nc, final_waits=()):
        for e in ENGS:
            c = 0
            for o in self.ops[e]:
                if o.dma:
                    continue
                if o.need_inc:
                    c += 1
                    o.semval = c
        with contextlib.ExitStack() as st:
            esem = {e: st.enter_context(nc.semaphore("S_" + e)) for e in ENGS}
            dsem = {k: st.enter_context(nc.semaphore("D_%d" % i)) for i, k in enumerate(self.dma_keys)}
            block = st.enter_context(nc.Block())
            prog = self

            def run(engname, h):
                for o in prog.ops[engname]:
                    for d in o.waits:
                        if d.dma:
                            h.wait_ge(dsem[d.dkey], d.semval)
                        else:
                            h.wait_ge(esem[d.eng], d.semval)
                    ins = o.fn(h)
                    if o.dma:
                        ins.then_inc(dsem[o.dkey], 16)
                    elif o.need_inc:
                        ins.then_inc(esem[o.eng], 1)
                if engname == "sp":
                    for o in final_waits:
                        h.wait_ge(dsem[o.dkey], o.semval)

            @block.tensor
            def _(h):
                run("pe", h)

            @block.scalar
            def _(h):
                run("act", h)

            @block.vector
            def _(h):
                run("dve", h)

            @block.gpsimd
            def _(h):
                run("pool", h)

            @block.sync
            def _(h):
                run("sp", h)


def f_mm(out, lhsT, rhs, start, stop):
    return lambda h: h.matmul(out, lhsT=lhsT, rhs=rhs, start=start, stop=stop)


def f_tr(out, in_, ident):
    return lambda h: h.transpose(out, in_, ident)


def f_act(out, in_, func, bias=None, scale=None, accum_out=None):
    kw = {}
    if bias is not None:
        kw["bias"] = bias
    if scale is not None:
        kw["scale"] = scale
    if accum_out is not None:
        kw["accum_out"] = accum_out
    return lambda h: h.activation(out=out, in_=in_, func=func, **kw)


def f_tt(out, in0, in1, op):
    return lambda h: h.tensor_tensor(out=out, in0=in0, in1=in1, op=op)


def f_ts(out, in0, s1, s2, op0, op1=None):
    if op1 is None:
        return lambda h: h.tensor_scalar(out=out, in0=in0, scalar1=s1, scalar2=None, op0=op0)
    return lambda h: h.tensor_scalar(out=out, in0=in0, scalar1=s1, scalar2=s2, op0=op0, op1=op1)


def f_stt(out, in0, scalar, in1, op0, op1):
    return lambda h: h.scalar_tensor_tensor(out=out, in0=in0, scalar=scalar, in1=in1, op0=op0, op1=op1)


def f_cp(out, in_):
    return lambda h: h.tensor_copy(out=out, in_=in_)


def f_dma(out, in_):
    return lambda h: h.dma_start(out=out, in_=in_)


def f_scan(out, d0, d1, init):
    return lambda h: h.tensor_tensor_scan(out=out, data0=d0, data1=d1, initial=init, op0=ALU.mult, op1=ALU.add)


class Arena:
    def __init__(self, nc, lo, hi):
        self.nc, self.lo, self.hi, self.cur, self.n = nc, lo, hi, lo, 0

    def reset(self, to=None):
        self.cur = self.lo if to is None else to

    def mark(self):
        return self.cur

    def alloc(self, name, shape, dt):
        esz = 4 if dt == F32 else 2
        nbytes = int(np.prod(shape[1:])) * esz
        off = (self.cur + 63) // 64 * 64
        assert off + nbytes <= self.hi, ("SBUF arena overflow", name, off, nbytes, self.hi)
        self.cur = off + nbytes
        self.n += 1
        return self.nc.alloc_sbuf_tensor_at("%s_%d" % (name, self.n), list(shape), dt, offset=off)


def build_program(depth=DEPTH):
    nc = bass.Bass("TRN2", target_bir_lowering=False)
    L = depth
    dr = lambda name, shape, dt=F32, kind="ExternalInput": nc.dram_tensor(name, list(shape), dt, kind=kind).ap()
    xT_d = dr("xT", [8, 128, NTOK])
    cT_d = dr("cT", [128, 24])
    link_d = dr("link", [128, 1])
    dtab_d = dr("dtab", [128, 3 * 384])
    vecs_d = dr("vecs", [128, DEPTH * NV + 8])
    wmod_d = dr("w_mod", [DEPTH, D, 6 * D])
    win_d = dr("w_in", [DEPTH, D, DIN])
    wg_d = dr("wg", [DEPTH, 16, 128, 128])
    wout_d = dr("w_out", [DEPTH, D, D])
    wfi_d = dr("w_ffn_in", [DEPTH, D, 2 * DFF])
    wfo_d = dr("w_ffn_out", [DEPTH, DFF, D])
    yT_d = dr("yT", [8, 128, NTOK], kind="ExternalOutput")
    xa_d = dr("xa_s", [8, 128, NTOK], kind="Internal")
    xb_d = dr("xb_s", [8, 128, NTOK], kind="Internal")
    xc_d = dr("xc_s", [4, 128, NTOK], kind="Internal")
    hf_d = dr("hf_s", [4, 128, NTOK], kind="Internal")
    gg_d = dr("gg_s", [4, 128, NTOK], BF16, kind="Internal")
    an_d = dr("an_s", [4, 128, NTOK], BF16, kind="Internal")

    def dview(d, t0, n=T):
        return d[:, :, t0:t0 + n].rearrange("c p t -> p c t")

    P = Prog()
    ar = Arena(nc, 16640, 229376)
    TX = {"in": trks("xin", NT), "xa": trks("xa", NT), "xb": trks("xb", NT), "y": trks("y", NT)}
    TXC, THF, TGG, TAN = trks("xcs", NT), trks("hfs", NT), trks("ggs", NT), trks("ans", NT)

    VEC = ar.alloc("vec", [128, DEPTH * NV + 8], F32)
    MOD = ar.alloc("mod", [128, DEPTH * 6 * 8 * 3], F32)
    C8 = ar.alloc("c8", [128, DEPTH * 8], F32)
    SK8 = ar.alloc("sk8", [128, DEPTH * 8], F32)
    HB2 = ar.alloc("hb2", [128, DEPTH * 16], F32)
    QUARTER = ar.alloc("quarter", [128, 1], F32)
    CA = ar.alloc("ca", [128, 24], BF16)
    DT = ar.alloc("dtab", [128, 3 * 384], F32)
    LINK = ar.alloc("link", [128, 1], F32)
    ONES = ar.alloc("ones", [128, 128], BF16)
    ONES5 = ar.alloc("ones5", [128, 128], BF16)
    IDN = ar.alloc("idn", [128, 128], BF16)
    EPSC = ar.alloc("epsc", [128, 1], F32)
    ONEC = ar.alloc("onec", [128, 1], F32)
    NHALFC = ar.alloc("nhalfc", [128, 1], F32)
    tVEC, tMOD, tC8, tSK8, tDT, tLINK, tCONST = (Trk(n) for n in ("vec", "mod", "c8", "sk8", "dt", "link", "const"))
    base_mark = ar.mark()

    PSF = [nc.alloc_psum_tensor("psf%d" % i, [128, 512], F32) for i in range(6)]
    PSB = [nc.alloc_psum_tensor("psb%d" % i, [128, 1024], BF16) for i in range(2)]
    tPSF = trks("psf", 6)
    tPSB = trks("psb", 2)

    def mod_ap(l, k, c, seg):
        o = ((l * 6 + k) * 8 + c) * 3 + seg
        return MOD[:, o:o + 1]

    def vec_ap(l, off, n=1):
        return VEC[:, l * NV + off:l * NV + off + n]

    P.dma("sp", f_dma(VEC[:], vecs_d), "vec", writes=[tVEC])
    P.dma("sp", f_dma(DT[:], dtab_d), "dt", writes=[tDT])
    P.dma("sp", f_dma(LINK[:], link_d), "link", writes=[tLINK])
    P.op("pool", lambda h: h.memset(ONES[:], 1.0 / 1024), writes=[tCONST])
    P.op("pool", lambda h: h.memset(ONES5[:], 1.0 / 512), writes=[tCONST])
    P.op("pool", lambda h: h.memset(EPSC[:], EPS), writes=[tCONST])
    P.op("pool", lambda h: h.memset(ONEC[:], 1.0), writes=[tCONST])
    P.op("pool", lambda h: h.memset(NHALFC[:], -0.5), writes=[tCONST])
    P.op("pool", lambda h: h.memset(QUARTER[:], 0.25), writes=[tCONST])
    P.op("pool", lambda h: h.memset(IDN[:], 0.0), writes=[tCONST])
    P.op("pool", lambda h: h.affine_select(out=IDN[:], in_=IDN[:], pattern=[[-1, 128]], compare_op=ALU.not_equal,
                                            fill=1.0, base=0, channel_multiplier=1), reads=[tCONST], writes=[tCONST])
    CT = ar.alloc("ct", [128, 24], F32)
    tCT, tCA = Trk("ct"), Trk("ca")
    P.dma("sp", f_dma(CT[:], cT_d), "ct", writes=[tCT])
    P.op("act", f_act(CA[:], CT[:], AF.Silu), reads=[tCT], writes=[tCA])
    modit = [0]

    def mod_gen(l, WM, tWM, bankfn):
        for k in range(6):
            s = modit[0] % 2
            modit[0] += 1
            src = wmod_d[l][:, k * D:(k + 1) * D].rearrange("(c p) n -> p c n", p=128)
            P.dma("pool", f_dma(WM[s][:], src), "wm%d" % s, writes=[tWM[s]])
            pb = bankfn()
            ps = PSF[pb]
            for fc in range(8):
                for kc in range(8):
                    P.op("pe", f_mm(ps[:, fc * 3:fc * 3 + 3], WM[s][:, kc, fc * 128:(fc + 1) * 128],
                                    CA[:, kc * 3:kc * 3 + 3], kc == 0, kc == 7),
                         reads=[tWM[s], tCA], writes=[tPSF[pb]])
            o = (l * 6 + k) * 24
            for seg in range(3):
                P.op("dve", f_tt(MOD[:, o + seg:o + 24:3], ps[:, seg:24:3], vec_ap(l, V_BMOD + k * 8, 8), ALU.add),
                     reads=[tPSF[pb], tVEC], writes=[tMOD])
            if k in (1, 4):
                gof = V_G1 if k == 1 else V_G2
                for seg in range(3):
                    P.op("dve", f_stt(MOD[:, o + seg:o + 24:3], MOD[:, o + seg:o + 24:3], 1.0, vec_ap(l, gof, 8),
                                      ALU.add, ALU.mult), reads=[tMOD, tVEC], writes=[tMOD])
            yield

    WM0 = [ar.alloc("wm", [128, 8, D], BF16) for _ in range(2)]
    tWM0 = trks("wm", 2)
    pbk0 = [0]

    def pbank0():
        pbk0[0] += 1
        return pbk0[0] % 6

    for _ in mod_gen(0, WM0, tWM0, pbank0):
        pass
    E1 = ar.alloc("e1", [128, DEPTH * 8], F32)
    E2 = ar.alloc("e2", [128, DEPTH * 8], F32)
    tE = Trk("e")
    for l in range(L):
        sl = slice(l * 8, l * 8 + 8)
        P.op("act", f_act(E1[:, sl], vec_ap(l, V_LAM, 8), AF.Exp, scale=-1.0), reads=[tVEC], writes=[tE])
        P.op("dve", f_ts(E2[:, sl], E1[:, sl], -0.25, 1.0 / 3, ALU.mult, ALU.add), reads=[tE], writes=[tE])
        P.op("dve", f_tt(E2[:, sl], E2[:, sl], E1[:, sl], ALU.mult), reads=[tE], writes=[tE])
        P.op("dve", f_ts(E2[:, sl], E2[:, sl], -1.0, 0.5, ALU.mult, ALU.add), reads=[tE], writes=[tE])
        P.op("dve", f_tt(E2[:, sl], E2[:, sl], E1[:, sl], ALU.mult), reads=[tE], writes=[tE])
        P.op("dve", f_ts(E2[:, sl], E2[:, sl], -1.0, 1.0, ALU.mult, ALU.add), reads=[tE], writes=[tE])
        P.op("dve", f_tt(E2[:, sl], E2[:, sl], E1[:, sl], ALU.mult), reads=[tE], writes=[tE])
        P.op("dve", f_ts(C8[:, sl], E2[:, sl], -4.0, None, ALU.mult), reads=[tE], writes=[tC8])
        P.op("dve", f_ts(HB2[:, l * 16:l * 16 + 16], vec_ap(l, V_BRG, 16), 0.5, None, ALU.mult), reads=[tVEC], writes=[tC8])
        P.op("dve", f_ts(SK8[:, sl], vec_ap(l, V_SINK, 8), 8.0, None, ALU.mult), reads=[tVEC], writes=[tSK8])
    P.barrier()
    ar.reset(base_mark)

    def norm_gen(XTt, tX, HN, tHN, TMP, tTMP, RS, tRS, l, kg, ksh, seg, mmb):
        P.op("act", f_act(HN[:, :, :], XTt[:, :, :], AF.Square), reads=tX, writes=tHN)
        yield
        ps = PSF[mmb]
        for c in range(8):
            P.op("pe", f_mm(ps[:, :], ONES[:, :], HN[:, c, :], c == 0, c == 7), reads=[tHN[c], tCONST], writes=[tPSF[mmb]])
        P.op("act", f_act(TMP[0][:, :], ps[:, :], AF.Sqrt, bias=EPSC[:, 0:1]), reads=[tPSF[mmb], tCONST], writes=[tTMP[0]])
        P.op("dve", lambda h, o=RS[:, :], i=TMP[0][:, :]: h.reciprocal(out=o, in_=i), reads=[tTMP[0]], writes=[tRS])
        yield
        for c in range(8):
            b = 1 + (c % 2)
            P.op("dve", f_stt(TMP[b][:, :], XTt[:, c, :], mod_ap(l, kg, c, seg), RS[:, :], ALU.mult, ALU.mult),
                 reads=[tX[c], tMOD, tRS], writes=[tTMP[b]])
            P.op("act", f_act(HN[:, c, :], TMP[b][:, :], AF.Identity, bias=mod_ap(l, ksh, c, seg)),
                 reads=[tTMP[b], tMOD], writes=[tHN[c]])
            if c % 2 == 1:
                yield

    def norm_mod(*a):
        for _ in norm_gen(*a):
            pass

    def interleave(gens):
        gens = list(gens)
        while gens:
            for g in list(gens):
                try:
                    next(g)
                except StopIteration:
                    gens.remove(g)

    def g1(l, d, c, XCB, tXCB, WG, tWG, GB, tGB, bankfn):
        for g, nm in ((0, "RA"), (1, "IU")):
            mb = bankfn()
            hb = HB2[:, l * 16 + g * 8 + d * 4 + c:l * 16 + g * 8 + d * 4 + c + 1]
            P.op("pe", f_mm(PSF[mb][:, :], WG[:, (d * 2 + g) * 4 + c, :], XCB[:, c, :], True, True),
                 reads=[tWG, tXCB[c]], writes=[tPSF[mb]])
            P.op("act", f_act(GB[nm][c][:, :], PSF[mb][:, :], AF.Tanh, bias=hb, scale=0.5),
                 reads=[tPSF[mb], tC8], writes=[tGB[nm][c]])
        hc8 = C8[:, l * 8 + d * 4 + c:l * 8 + d * 4 + c + 1]
        P.op("act", f_act(GB["RA"][c][:, :], GB["RA"][c][:, :], AF.Exp, bias=hc8, scale=hc8),
             reads=[tGB["RA"][c], tC8], writes=[tGB["RA"][c]])
        P.op("act", f_act(GB["S"][c][:, :], GB["RA"][c][:, :], AF.Square), reads=[tGB["RA"][c]], writes=[tGB["S"][c]])

    def g2(c, XC, tXC, GB, tGB):
        P.op("dve", f_ts(GB["S"][c][:, :], GB["S"][c][:, :], 0.99999994, -1.0, ALU.min, ALU.mult),
             reads=[tGB["S"][c]], writes=[tGB["S"][c]])
        P.op("dve", f_stt(GB["IU"][c][:, :], GB["IU"][c][:, :], 1.0, XC[:, c, :], ALU.add, ALU.mult),
             reads=[tGB["IU"][c], tXC[c]], writes=[tGB["IU"][c]])

    def gsq(c, GB, tGB):
        P.op("act", f_act(GB["S"][c][:, :], GB["S"][c][:, :], AF.Sqrt, bias=QUARTER[:, 0:1], scale=0.25),
             reads=[tGB["S"][c], tCONST], writes=[tGB["S"][c]])

    def g3(c, GB, tGB):
        P.op("dve", f_tt(GB["IU"][c][:, :], GB["IU"][c][:, :], GB["S"][c][:, :], ALU.mult),
             reads=[tGB["IU"][c], tGB["S"][c]], writes=[tGB["IU"][c]])

    def gates_gen(l, d, XC, tXC, XCB, tXCB, WG, tWG, GB, tGB, bankfn):
        for c in range(4):
            g1(l, d, c, XCB, tXCB, WG, tWG, GB, tGB, bankfn)
            yield
        for c in range(4):
            g2(c, XC, tXC, GB, tGB)
        yield
        for c in range(4):
            gsq(c, GB, tGB)
        yield
        for c in range(4):
            g3(c, GB, tGB)
        yield

    def alloc_gb():
        GB = {n: [ar.alloc("gb" + n, [128, T], F32) for _ in range(4)] for n in ("RA", "IU", "S")}
        tGB = {n: trks("gb" + n, 4) for n in ("RA", "IU", "S")}
        return GB, tGB

    final_out = []
    pre_win = [None]
    pre_b = [None]
    xin_d, xin_t = xT_d, TX["in"]
    for l in range(L):
        xout_d, xout_t = (yT_d, TX["y"]) if l == L - 1 else (xa_d, TX["xa"])
        ar.reset(base_mark)
        WIN = ar.alloc("win", [128, 8, DIN], BF16)
        WG = ar.alloc("wg", [128, 16, 128], BF16)
        def load_win(l_, WIN_, WG_, tWIN_, tWG_, extra_w=()):
            wsrc_ = win_d[l_].rearrange("(c p) n -> p c n", p=128)
            for q4 in range(4):
                P.dma("pool", f_dma(WIN_[:, :, q4 * 448:(q4 + 1) * 448], wsrc_[:, :, q4 * 448:(q4 + 1) * 448]),
                      "win%d" % q4, writes=[tWIN_] + list(extra_w))
            P.dma("pool", f_dma(WG_[:], wg_d[l_].rearrange("j p m -> p j m")), "wg", writes=[tWG_] + list(extra_w))

        if pre_win[0] is None:
            tWIN, tWG = Trk("win"), Trk("wg")
            load_win(l, WIN, WG, tWIN, tWG)
        else:
            tWIN, tWG = pre_win[0]
            pre_win[0] = None
        XTt = ar.alloc("xt", [128, 8, T], F32)
        tX = trks("x", 8)
        HN = ar.alloc("hn", [128, 8, T], BF16)
        tHN = trks("hn", 8)
        TMP = [ar.alloc("tmp", [128, T], F32) for _ in range(3)]
        tTMP = trks("tmp", 3)
        RS = ar.alloc("rs", [128, T], F32)
        tRS = Trk("rs")
        S = 8 * T
        KU = ar.alloc("ku", [128, S], BF16)
        tKU = trks("ku", 8)
        VU = ar.alloc("vu", [128, 32, 128], BF16)
        tVU = trks("vu", 8)
        QR = ar.alloc("qr", [128, 4, 2, T], BF16)
        tQR = trks("qr", 2)
        XR = ar.alloc("xr", [128, 4, 3, T + 3], F32)
        tXR = trks("xr", 3)
        XC = [ar.alloc("xc", [128, 4, T], F32)] * 2
        tXCs = [trks("xc", 4)] * 2
        CARRY = ar.alloc("carry", [128, 4], F32)
        tCARRY = trks("carry", 4)
        XCB = ar.alloc("xcb", [128, 4, T], BF16)
        tXCB = trks("xcb", 4)
        GB, tGB = alloc_gb()
        HF = [ar.alloc("hf", [128, 4, T], F32)] * 2
        tHF = [trks("hf", 4)] * 2
        GG = [ar.alloc("gg", [128, 4, T], BF16)] * 2
        tGG = [Trk("gg")] * 2
        ANT = [ar.alloc("ant", [128, 4, T], BF16) for _ in range(2)]
        tANT = trks("ant", 2)
        TT = [ar.alloc("tt", [128, 384], F32) for _ in range(4)]
        tTT = trks("tt", 4)
        PM = [ar.alloc("pm", [128, 384], BF16) for _ in range(4)]
        tPM = trks("pm", 4)
        PT = [ar.alloc("pt", [128, 3, 128], BF16) for _ in range(4)]
        tPT = trks("pt", 4)
        ST = ar.alloc("st", [128, 128], F32)
        tSTp = [{n: Trk("st" + n) for n in ("es", "rden", "ssq", "rsa")} for _ in range(2)]
        tMXp, tNEGBp, tRSUMp = [trks("mx", 8) for _ in range(2)], [trks("negb", 8) for _ in range(2)], [trks("rsum", 8) for _ in range(2)]
        tPSBh = trks("psbh", 2)
        ATT = ar.alloc("att", [128, 512], F32)
        tATT = Trk("att")
        ATN = ar.alloc("atn", [128, 512], BF16)
        tATN = Trk("atn")
        JUNK = ar.alloc("junk", [128, 512], BF16)
        CTMP = ar.alloc("ctmp", [128, T], F32)
        tCTMP = Trk("ctmp")
        tJUNK = Trk("junk")
        pending_tail = [None]
        for (ut0, unt, ulink) in UNITS:
            mmrr = [0]

            def mmbank():
                b = mmrr[0] % 3
                mmrr[0] += 1
                return b

            gbank = [0]

            def gatebank():
                b = 3 + gbank[0] % 2
                gbank[0] += 1
                return b

            hcnt = [0]

            def lru_conv(j):
                s = j % 3
                for c in range(4):
                    cw = lambda tap: vec_ap(l, V_CW + tap * 4 + c)
                    P.op("pool", f_ts(XC[0][:, c, :], XR[:, c, s, 0:T], cw(0), vec_ap(l, V_CB + c), ALU.mult, ALU.add),
                         reads=[tXR[s], tVEC], writes=[tXCs[0][c]])
                    for tap in (1, 2, 3):
                        P.op("pool", f_ts(CTMP[:, :], XR[:, c, s, tap:tap + T], cw(tap), 0.0, ALU.mult, ALU.add),
                             reads=[tXR[s], tVEC], writes=[tCTMP])
                        P.op("pool", f_tt(XC[0][:, c, :], XC[0][:, c, :], CTMP[:, :], ALU.add),
                             reads=[tCTMP, tXCs[0][c]], writes=[tXCs[0][c]])
                    P.op("pool", f_cp(XCB[:, c, :], XC[0][:, c, :]), reads=[tXCs[0][c]], writes=[tXCB[c]])

            def lruG(j, ut0=ut0, ulink=ulink):
                gi = ut0 + j
                for c in range(4):
                    g1(l, 0, c, XCB, tXCB, WG, tWG, GB, tGB, gatebank)
                    yield
                for c in range(4):
                    g2(c, XC[0], tXCs[0], GB, tGB)
                yield
                for c in range(4):
                    gsq(c, GB, tGB)
                yield
                for c in range(4):
                    g3(c, GB, tGB)
                    if ulink is not None and j == ulink:
                        P.op("dve", f_ts(GB["RA"][c][:, 0:1], GB["RA"][c][:, 0:1], LINK[:, 0:1], None, ALU.mult),
                             reads=[tGB["RA"][c], tLINK], writes=[tGB["RA"][c]])
                    init = 0.0 if j == 0 else CARRY[:, c:c + 1]
                    rd = [tGB["RA"][c], tGB["IU"][c]] + ([] if j == 0 else [tCARRY[c]])
                    P.op("dve", f_scan(HF[0][:, c, :], GB["RA"][c][:, :], GB["IU"][c][:, :], init), reads=rd, writes=[tHF[0][c]])
                    P.op("dve", f_cp(CARRY[:, c:c + 1], HF[0][:, c, T - 1:T]), reads=[tHF[0][c]], writes=[tCARRY[c]])
                    yield
                t0 = gi * T
                P.dma("sp", f_dma(dview(xc_d, t0), XC[0][:, :, :]), "xco", reads=tXCs[0], writes=[TXC[gi]])
                P.dma("sp", f_dma(dview(hf_d, t0), HF[0][:, :, :]), "hfo", reads=tHF[0], writes=[THF[gi]])

            def mix(j, ngen=None):
                gi = ut0 + j
                qs = j % 2
                asl = j % 2
                nblk = unt * 4
                lru_conv(j)

                def lru_step(p):
                    pass

                def geom(b):
                    n = 4 * j + b
                    lo = max(n - 1, 0)
                    hi = min(n + 1, nblk - 1)
                    nkb = hi - lo + 1
                    doff = 0 if lo == n - 1 else 128
                    tab = 0
                    if ulink is not None and n == ulink * 4 - 1:
                        tab = 1
                    if ulink is not None and n == ulink * 4:
                        tab = 2
                    tl = sorted(set((lo // 4, hi // 4)))
                    return n, lo, hi, nkb, doff, tab, [tKU[tt] for tt in tl], [tVU[tt] for tt in tl]

                sbank = {}

                def SA(p):
                    b, c = p // 4, p % 4
                    n, lo, hi, nkb, doff, tab, ktr, vtr = geom(b)
                    nk = nkb * 128
                    for hh in range(2):
                        sb = hcnt[0] % 3
                        hcnt[0] += 1
                        sbank[(p, hh)] = sb
                        pl, ph = hh * 64, hh * 64 + 64
                        P.op("pe", f_mm(PSF[sb][:, 0:nk], QR[pl:ph, c, qs, b * 128:(b + 1) * 128],
                                        KU[pl:ph, lo * 128:(hi + 1) * 128], True, True),
                             reads=[tQR[qs]] + ktr, writes=[tPSF[sb]])

                def SB(p, hh):
                    b, c = p // 4, p % 4
                    n, lo, hi, nkb, doff, tab, ktr, vtr = geom(b)
                    nk = nkb * 128
                    so = (b % 2) * 64
                    Dv = DT[:, tab * 384 + doff:tab * 384 + doff + nk]
                    if True:
                        h_ = c + 4 * hh
                        sb = sbank[(p, hh)]
                        r4 = (c * 2 + hh) % 4
                        slope8 = 8.0 * 2.0 ** (-(h_ + 1))
                        P.op("dve", f_stt(TT[r4][:, 0:nk], Dv, slope8, PSF[sb][:, 0:nk], ALU.mult, ALU.add),
                             reads=[tDT, tPSF[sb]], writes=[tTT[r4]])
                        P.op("dve", lambda h, o=ST[:, so + h_:so + h_ + 1], i=TT[r4][:, 0:nk]: h.reduce_max(out=o, in_=i, axis=AX.X),
                             reads=[tTT[r4]], writes=[tMXp[b % 2][h_]])
                        P.op("dve", f_ts(ST[:, so + 8 + h_:so + 9 + h_], ST[:, so + h_:so + h_ + 1],
                                         SK8[:, l * 8 + h_:l * 8 + h_ + 1], -0.125, ALU.max, ALU.mult),
                             reads=[tMXp[b % 2][h_], tSK8], writes=[tNEGBp[b % 2][h_]])
                        P.op("act", f_act(PM[r4][:, 0:nk], TT[r4][:, 0:nk], AF.Exp, bias=ST[:, so + 8 + h_:so + 9 + h_], scale=0.125,
                                          accum_out=ST[:, so + 16 + h_:so + 17 + h_]),
                             reads=[tTT[r4], tNEGBp[b % 2][h_]], writes=[tPM[r4], tRSUMp[b % 2][h_]])

                def SC(p):
                    b, c = p // 4, p % 4
                    n, lo, hi, nkb, doff, tab, ktr, vtr = geom(b)
                    nk = nkb * 128
                    for hh in range(2):
                        r4 = (c * 2 + hh) % 4
                        pbk = hh
                        for jj in range(nkb):
                            P.op("pe", f_tr(PSB[pbk][:, jj * 128:(jj + 1) * 128],
                                            PM[r4][:, jj * 128:(jj + 1) * 128], IDN[:, :]),
                                 reads=[tPM[r4], tCONST], writes=[tPSB[pbk]])
                        P.op("act", f_act(PT[r4][:, 0:nkb, :],
                                          PSB[pbk][:, 0:nk].rearrange("p (j q) -> p j q", q=128), AF.Copy),
                             reads=[tPSB[pbk]], writes=[tPT[r4]])

                def SD(p):
                    b, c = p // 4, p % 4
                    n, lo, hi, nkb, doff, tab, ktr, vtr = geom(b)
                    for hh in range(2):
                        h_ = c + 4 * hh
                        r4 = (c * 2 + hh) % 4
                        for jj in range(nkb):
                            P.op("pe", f_mm(PSF[5][:, h_ * 64:(h_ + 1) * 64], PT[r4][:, jj, :],
                                            VU[:, lo + jj, hh * 64:(hh + 1) * 64], jj == 0, jj == nkb - 1),
                                 reads=[tPT[r4]] + vtr, writes=[tPSF[5]])

                def EPI(b, k):
                    so = (b % 2) * 64
                    tST = tSTp[b % 2]
                    if k == 0:
                        P.op("dve", f_tt(ST[:, so + 24:so + 32], ST[:, so + 8:so + 16], vec_ap(l, V_SINK, 8), ALU.add),
                             reads=tNEGBp[b % 2] + [tVEC], writes=[tST["es"]])
                        P.op("act", f_act(ST[:, so + 24:so + 32], ST[:, so + 24:so + 32], AF.Exp), reads=[tST["es"]], writes=[tST["es"]])
                    elif k == 1:
                        P.op("dve", f_tt(ST[:, so + 32:so + 40], ST[:, so + 24:so + 32], ST[:, so + 16:so + 24], ALU.add),
                             reads=[tST["es"]] + tRSUMp[b % 2], writes=[tST["rden"]])
                        P.op("dve", lambda h, o=ST[:, so + 32:so + 40]: h.reciprocal(out=o, in_=o), reads=[tST["rden"]], writes=[tST["rden"]])
                        P.op("dve", f_tt(ATT[:, :].rearrange("p (h d) -> p h d", d=64),
                                         PSF[5][:, :].rearrange("p (h d) -> p h d", d=64),
                                         ST[:, so + 32:so + 40].unsqueeze(2).to_broadcast([128, 8, 64]), ALU.mult),
                             reads=[tPSF[5], tST["rden"]], writes=[tATT])
                    elif k == 2:
                        P.op("act", f_act(JUNK[:, :], ATT[:, :], AF.Square, accum_out=ST[:, so + 40:so + 41]),
                             reads=[tATT], writes=[tJUNK, tST["ssq"]])
                    elif k == 3:
                        P.op("dve", f_ts(ST[:, so + 41:so + 42], ST[:, so + 40:so + 41], 1.0 / 512, EPS, ALU.mult, ALU.add),
                             reads=[tST["ssq"]], writes=[tST["rsa"]])
                        P.op("pool", f_tt(ST[:, so + 41:so + 42], ST[:, so + 41:so + 42], NHALFC[:, 0:1], ALU.pow),
                             reads=[tST["rsa"], tCONST], writes=[tST["rsa"]])
                    elif k == 4:
                        P.op("dve", f_ts(ATN[:, :], ATT[:, :], ST[:, so + 41:so + 42], None, ALU.mult),
                             reads=[tATT, tST["rsa"]], writes=[tATN])
                    elif k == 5:
                        for cc in range(4):
                            P.op("pe", f_tr(PSB[1][:, 512 + cc * 128:512 + (cc + 1) * 128], ATN[:, cc * 128:(cc + 1) * 128], IDN[:, :]),
                                 reads=[tATN, tCONST], writes=[tPSB[1]])
                        for cc in range(4):
                            P.op("act", f_act(ANT[asl][:, cc, b * 128:(b + 1) * 128], PSB[1][:, 512 + cc * 128:512 + (cc + 1) * 128],
                                              AF.Identity, scale=vec_ap(l, V_GA + cc)),
                                 reads=[tPSB[1], tVEC], writes=[tANT[asl]])

                NP = 16
                SA(0)
                for p in range(NP + 6):
                    if p < NP:
                        SB(p, 0)
                    if p + 1 < NP:
                        SA(p + 1)
                    if p < NP:
                        SB(p, 1)
                    if 0 <= p - 2 < NP:
                        SD(p - 2)
                    for b_ in range(4):
                        k_ = p - (4 * b_ + 4)
                        if 0 <= k_ < 6:
                            EPI(b_, k_)
                    if 0 <= p - 1 < NP:
                        SC(p - 1)
                    lru_step(p)
                    if ngen is not None and p >= 2:
                        next(ngen, None)
                if ngen is not None:
                    for _ in ngen:
                        pass
                P.dma("sp", f_dma(dview(an_d, gi * T), ANT[asl][:, :, :]), "ano%d" % asl, reads=[tANT[asl]], writes=[TAN[gi]])

            def xload(i):
                gi = ut0 + i
                P.dma("sp", f_dma(XTt[:, :, :], dview(xin_d, gi * T)), "xt", reads=[xin_t[gi]], writes=tX)

            def ngen_for(i):
                return norm_gen(XTt, tX, HN, tHN, TMP, tTMP, RS, tRS, l, 1, 0, (ut0 + i) // 4, gatebank())

            tail = pending_tail[0]
            pending_tail[0] = None
            xload(0)
            for _ in ngen_for(0):
                if tail is not None:
                    next(tail, None)
            def proj(i):
                gi = ut0 + i
                seg = gi // 4
                t0 = gi * T
                s = i % 2
                xs_ = i % 3
                xp_ = (i - 1) % 3
                for m in range(4):
                    mb = mmbank()
                    for c in range(8):
                        P.op("pe", f_mm(PSF[mb][:, :], WIN[:, c, m * 128:(m + 1) * 128], HN[:, c, :], c == 0, c == 7),
                             reads=[tWIN, tHN[c]], writes=[tPSF[mb]])
                    P.op("act", f_act(QR[:, m, s, :], PSF[mb][:, :], AF.Copy), reads=[tPSF[mb]], writes=[tQR[s]])
                    yield
                mb = mmbank()
                for c in range(8):
                    P.op("pe", f_mm(PSF[mb][:, :], WIN[:, c, 512:640], HN[:, c, :], c == 0, c == 7),
                         reads=[tWIN, tHN[c]], writes=[tPSF[mb]])
                P.op("dve", f_cp(KU[:, i * T:(i + 1) * T], PSF[mb][:, :]), reads=[tPSF[mb]], writes=[tKU[i]])
                yield
                mb = mmbank()
                for b in range(4):
                    for c in range(8):
                        P.op("pe", f_mm(PSF[mb][:, b * 128:(b + 1) * 128], HN[:, c, b * 128:(b + 1) * 128],
                                        WIN[:, c, 640:768], c == 0, c == 7),
                             reads=[tWIN, tHN[c]], writes=[tPSF[mb]])
                P.op("act", f_act(VU[:, i * 4:(i + 1) * 4, :], PSF[mb][:, :].rearrange("p (b f) -> p b f", f=128), AF.Copy),
                     reads=[tPSF[mb]], writes=[tVU[i]])
                yield
                for m in range(4):
                    mb = mmbank()
                    for c in range(8):
                        P.op("pe", f_mm(PSF[mb][:, :], WIN[:, c, 768 + m * 128:768 + (m + 1) * 128], HN[:, c, :], c == 0, c == 7),
                             reads=[tWIN, tHN[c]], writes=[tPSF[mb]])
                    P.op("dve", f_cp(XR[:, m, xs_, 2:T + 2], PSF[mb][:, :]), reads=[tPSF[mb]], writes=[tXR[xs_]])
                    yield
                linked = (ulink is not None and i == ulink)
                if i == 0:
                    P.op("pool", lambda h, o=XR[:, :, xs_, 0:2]: h.memset(o, 0.0), writes=[tXR[xs_]])
                else:
                    if linked:
                        P.op("pool", f_ts(XR[:, :, xs_, 0:2], XR[:, :, xp_, T:T + 2], LINK[:, 0:1], 0.0, ALU.mult, ALU.add),
                             reads=[tXR[xp_], tLINK], writes=[tXR[xs_]])
                        P.op("pool", f_ts(XR[:, :, xp_, T + 2:T + 3], XR[:, :, xs_, 2:3], LINK[:, 0:1], 0.0, ALU.mult, ALU.add),
                             reads=[tXR[xs_], tLINK], writes=[tXR[xp_]])
                    else:
                        P.op("pool", f_cp(XR[:, :, xs_, 0:2], XR[:, :, xp_, T:T + 2]), reads=[tXR[xp_]], writes=[tXR[xs_]])
                        P.op("pool", f_cp(XR[:, :, xp_, T + 2:T + 3], XR[:, :, xs_, 2:3]), reads=[tXR[xs_]], writes=[tXR[xp_]])
                if i == unt - 1:
                    P.op("pool", lambda h, o=XR[:, :, xs_, T + 2:T + 3]: h.memset(o, 0.0), writes=[tXR[xs_]])
                for m in range(4):
                    mb = mmbank()
                    for c in range(8):
                        P.op("pe", f_mm(PSF[mb][:, :], WIN[:, c, 1280 + m * 128:1280 + (m + 1) * 128], HN[:, c, :], c == 0, c == 7),
                             reads=[tWIN, tHN[c]], writes=[tPSF[mb]])
                    P.op("act", f_act(GG[s][:, m, :], PSF[mb][:, :], AF.Gelu_apprx_tanh), reads=[tPSF[mb]], writes=[tGG[s]])
                    yield
                P.dma("sp", f_dma(dview(gg_d, t0), GG[s][:, :, :]), "ggo", reads=[tGG[s]], writes=[TGG[gi]])

            for i in range(unt + 2):
                if i + 1 < unt:
                    xload(i + 1)
                gens = []
                if i < unt:
                    gens.append(proj(i))
                if 0 <= i - 2 < unt:
                    if i == unt + 1:
                        pending_tail[0] = lruG(i - 2)
                    else:
                        gens.append(lruG(i - 2))
                if i == 0 and tail is not None:
                    gens.append(tail)
                interleave(gens)
                if i == unt - 1 and ut0 + unt == NT:
                    cur_ = ar.cur
                    ar.reset(base_mark)
                    WGb_ = ar.alloc("wg", [128, 16, 128], BF16)
                    WOUTb_ = ar.alloc("wout", [128, 8, D], BF16)
                    ar.cur = cur_
                    tWGb_, tWOUTb_ = Trk("wg"), Trk("wout")
                    P.dma("pool", f_dma(WGb_[:], wg_d[l].rearrange("j p m -> p j m")), "wgb", writes=[tWGb_, tWIN])
                    wsrc_ = wout_d[l].rearrange("(c p) n -> p c n", p=128)
                    for q2 in range(2):
                        P.dma("pool", f_dma(WOUTb_[:, :, q2 * 512:(q2 + 1) * 512], wsrc_[:, :, q2 * 512:(q2 + 1) * 512]),
                              "wout%d" % q2, writes=[tWOUTb_, tWIN])
                    pre_b[0] = (tWGb_, tWOUTb_)
                ng = ngen_for(i + 1) if i + 1 < unt else None
                if 1 <= i <= unt:
                    mix(i - 1, ng)
                elif ng is not None:
                    for _ in ng:
                        pass
        if pending_tail[0] is not None:
            for _ in pending_tail[0]:
                pass
            pending_tail[0] = None
        P.barrier()

        ar.reset(base_mark)
        WG = ar.alloc("wg", [128, 16, 128], BF16)
        WOUT = ar.alloc("wout", [128, 8, D], BF16)
        assert pre_b[0] is not None
        tWG, tWOUT = pre_b[0]
        pre_b[0] = None
        XTb = [ar.alloc("xt", [128, 8, T], F32) for _ in range(2)]
        tXb = [trks("x", 8) for _ in range(2)]
        XCi = [ar.alloc("xci", [128, 4, T], F32) for _ in range(2)]
        tXCi = [trks("xci", 4) for _ in range(2)]
        HFi = [ar.alloc("hfi", [128, 4, T], F32) for _ in range(2)]
        tHFi = trks("hfi", 2)
        GGi = [ar.alloc("ggi", [128, 4, T], BF16) for _ in range(2)]
        tGGi = trks("ggi", 2)
        ANi = [ar.alloc("ani", [128, 4, T], BF16) for _ in range(2)]
        tANi = trks("ani", 2)
        XCB = ar.alloc("xcb", [128, 4, T], BF16)
        tXCB = trks("xcb", 4)
        GB, tGB = alloc_gb()
        HB = [ar.alloc("hb", [128, 4, T], F32) for _ in range(2)]
        tHB = [trks("hb", 4) for _ in range(2)]
        LR = ar.alloc("lr", [128, 4, T], F32)
        tLR = trks("lr", 4)
        SQ = ar.alloc("sq", [128, 4, T], BF16)
        tSQ = Trk("sq")
        LN = ar.alloc("ln", [128, 4, T], BF16)
        tLN = trks("ln", 4)
        TMPb = ar.alloc("tmp", [128, T], F32)
        tTMPb = Trk("tmp")
        RSb = ar.alloc("rs", [128, T], F32)
        tRSb = Trk("rs")
        WMb = [ar.alloc("wm", [128, 8, D], BF16) for _ in range(2)]
        tWMb = trks("wm", 2)
        rr = [0]

        def bank6():
            b = rr[0] % 6
            rr[0] += 1
            return b

        modg = mod_gen(l + 1, WMb, tWMb, bank6) if l + 1 < L else None
        for (ut0, unt, ulink) in UNITS:

            def frontB(idx):
                j = unt - 1 - idx
                gi = ut0 + j
                t0 = gi * T
                s = idx % 2
                P.dma("sp", f_dma(XCi[s][:, :, :], dview(xc_d, t0)), "xci%d" % s, reads=[TXC[gi]], writes=tXCi[s])
                P.dma("sp", f_dma(HFi[s][:, :, :], dview(hf_d, t0)), "hfi%d" % s, reads=[THF[gi]], writes=[tHFi[s]])
                P.dma("sp", f_dma(GGi[s][:, :, :], dview(gg_d, t0)), "ggi%d" % s, reads=[TGG[gi]], writes=[tGGi[s]])
                P.dma("sp", f_dma(ANi[s][:, :, :], dview(an_d, t0)), "ani%d" % s, reads=[TAN[gi]], writes=[tANi[s]])
                P.dma("sp", f_dma(XTb[s][:, :, :], dview(xin_d, t0)), "xtb%d" % s, reads=[xin_t[gi]], writes=tXb[s])
                for c in range(4):
                    P.op("pool", f_cp(XCB[:, c, :], XCi[s][:, c, :]), reads=[tXCi[s][c]], writes=[tXCB[c]])
                yield
                yield from gates_gen(l, 1, XCi[s], tXCi[s], XCB, tXCB, WG, tWG, GB, tGB, bank6)
                for c in range(4):
                    if ulink is not None and j == ulink - 1:
                        P.op("dve", f_ts(GB["RA"][c][:, T - 1:T], GB["RA"][c][:, T - 1:T], LINK[:, 0:1], None, ALU.mult),
                             reads=[tGB["RA"][c], tLINK], writes=[tGB["RA"][c]])
                    init = 0.0 if idx == 0 else HB[1 - s][:, c, 0:1]
                    rd = [tGB["RA"][c], tGB["IU"][c]] + ([] if idx == 0 else [tHB[1 - s][c]])
                    P.op("dve", f_scan(HB[s][:, c, ::-1], GB["RA"][c][:, ::-1], GB["IU"][c][:, ::-1], init),
                         reads=rd, writes=[tHB[s][c]])
                    if c % 2 == 1:
                        yield

            def backB(idx):
                j = unt - 1 - idx
                gi = ut0 + j
                seg = gi // 4
                t0 = gi * T
                s = idx % 2
                for c in range(4):
                    P.op("dve", f_tt(LR[:, c, :], HFi[s][:, c, :], HB[s][:, c, :], ALU.add),
                         reads=[tHFi[s], tHB[s][c]], writes=[tLR[c]])
                    P.op("dve", f_tt(LR[:, c, :], LR[:, c, :], GGi[s][:, c, :], ALU.mult),
                         reads=[tLR[c], tGGi[s]], writes=[tLR[c]])
                    if c % 2 == 1:
                        yield
                P.op("act", f_act(SQ[:, :, :], LR[:, :, :], AF.Square), reads=tLR, writes=[tSQ])
                mb = bank6()
                for c in range(4):
                    P.op("pe", f_mm(PSF[mb][:, :], ONES5[:, :], SQ[:, c, :], c == 0, c == 3), reads=[tSQ, tCONST], writes=[tPSF[mb]])
                P.op("act", f_act(TMPb[:, :], PSF[mb][:, :], AF.Ln, bias=EPSC[:, 0:1]), reads=[tPSF[mb], tCONST], writes=[tTMPb])
                P.op("act", f_act(RSb[:, :], TMPb[:, :], AF.Exp, scale=-0.5), reads=[tTMPb], writes=[tRSb])
                yield
                for c in range(4):
                    P.op("dve", f_stt(LN[:, c, :], LR[:, c, :], vec_ap(l, V_GL + c), RSb[:, :], ALU.mult, ALU.mult),
                         reads=[tLR[c], tVEC, tRSb], writes=[tLN[c]])
                yield
                for m in range(8):
                    mb = bank6()
                    for c in range(4):
                        P.op("pe", f_mm(PSF[mb][:, :], WOUT[:, c, m * 128:(m + 1) * 128], ANi[s][:, c, :], c == 0, False),
                             reads=[tWOUT, tANi[s]], writes=[tPSF[mb]])
                    for c in range(4):
                        P.op("pe", f_mm(PSF[mb][:, :], WOUT[:, 4 + c, m * 128:(m + 1) * 128], LN[:, c, :], False, c == 3),
                             reads=[tWOUT, tLN[c]], writes=[tPSF[mb]])
                    P.op("dve", f_stt(XTb[s][:, m, :], PSF[mb][:, :], mod_ap(l, 2, m, seg), XTb[s][:, m, :], ALU.mult, ALU.add),
                         reads=[tPSF[mb], tMOD, tXb[s][m]], writes=[tXb[s][m]])
                    if m % 2 == 1:
                        yield
                P.dma("sp", f_dma(dview(xb_d, t0), XTb[s][:, :, :]), "xbo%d" % s, reads=tXb[s], writes=[TX["xb"][gi]])

            interleave([frontB(0)])
            for idx in range(unt):
                gens = [backB(idx)]
                if idx + 1 < unt:
                    gens.append(frontB(idx + 1))
                interleave(gens)
                if modg is not None:
                    next(modg, None)
        if modg is not None:
            for _ in modg:
                pass
        P.barrier()

        ar.reset(base_mark)
        WFI = ar.alloc("wfi", [128, 8, 2 * DFF], BF16)
        WFO = ar.alloc("wfo", [128, 22, D], BF16)
        tWFI = trks("wfi", 8)
        tWFO = trks("wfo", 4)
        JB = (0, 6, 12, 17, 22)
        wsrc = wfi_d[l].rearrange("(c p) n -> p c n", p=128)
        for q8 in (0, 4, 1, 5, 2, 6, 3, 7):
            P.dma("pool", f_dma(WFI[:, :, q8 * 704:(q8 + 1) * 704], wsrc[:, :, q8 * 704:(q8 + 1) * 704]),
                  "wfi%d" % q8, writes=[tWFI[q8]])
        wsrc = wfo_d[l].rearrange("(c p) n -> p c n", p=128)
        for q4 in range(4):
            P.dma("pool", f_dma(WFO[:, JB[q4]:JB[q4 + 1], :], wsrc[:, JB[q4]:JB[q4 + 1], :]),
                  "wfo%d" % q4, writes=[tWFO[q4]])
        XTf = [ar.alloc("xt", [128, 8, T], F32) for _ in range(2)]
        tXf = [trks("x", 8) for _ in range(2)]
        HN = ar.alloc("hn", [128, 8, T], BF16)
        tHN = trks("hn", 8)
        TMP = [ar.alloc("tmp", [128, T], F32) for _ in range(3)]
        tTMP = trks("tmp", 3)
        RS = ar.alloc("rs", [128, T], F32)
        tRS = Trk("rs")
        ACTB = ar.alloc("actb", [128, 11, T], BF16)
        tACTB = trks("actb", 11)
        SG = [ar.alloc("sg", [128, T], F32) for _ in range(2)]
        tSG = trks("sg", 2)
        wfo_t = lambda jj: tWFO[0 if jj < 6 else 1 if jj < 12 else 2 if jj < 17 else 3]

        def f_load(gi):
            P.dma("sp", f_dma(XTf[gi % 2][:, :, :], dview(xb_d, gi * T)), "xtf%d" % (gi % 2), reads=[TX["xb"][gi]], writes=tXf[gi % 2])

        def f_norm_gen(gi):
            return norm_gen(XTf[gi % 2], tXf[gi % 2], HN, tHN, TMP, tTMP, RS, tRS, l, 4, 3, gi // 4, 0)

        def f_gu(gi, j0):
            for jl in range(11):
                jj = j0 + jl
                gb, ub = jj % 2, 2 + jj % 2
                for half, pb in ((0, gb), (1, ub)):
                    col = half * DFF + jj * 128
                    for c in range(8):
                        P.op("pe", f_mm(PSF[pb][:, :], WFI[:, c, col:col + 128], HN[:, c, :], c == 0, c == 7),
                             reads=[tWFI[col // 704], tWFI[(col + 127) // 704], tHN[c]], writes=[tPSF[pb]])
                P.op("act", f_act(SG[jj % 2][:, :], PSF[gb][:, :], AF.Silu), reads=[tPSF[gb]], writes=[tSG[jj % 2]])
                P.op("dve", f_tt(ACTB[:, jl, :], SG[jj % 2][:, :], PSF[ub][:, :], ALU.mult),
                     reads=[tSG[jj % 2], tPSF[ub]], writes=[tACTB[jl]])

        def f_out(gi, j0, ng=None):
            XTt, tX, seg = XTf[gi % 2], tXf[gi % 2], gi // 4
            if ng is not None:
                next(ng, None)
            for m in range(8):
                pb = 4 + m % 2
                for jl in range(11):
                    jj = j0 + jl
                    P.op("pe", f_mm(PSF[pb][:, :], WFO[:, jj, m * 128:(m + 1) * 128], ACTB[:, jl, :], jl == 0, jl == 10),
                         reads=[wfo_t(jj), tACTB[jl]], writes=[tPSF[pb]])
                P.op("dve", f_stt(XTt[:, m, :], PSF[pb][:, :], mod_ap(l, 5, m, seg), XTt[:, m, :], ALU.mult, ALU.add),
                     reads=[tPSF[pb], tMOD, tX[m]], writes=[tX[m]])
                if ng is not None and m >= 1:
                    next(ng, None)
            if ng is not None:
                for _ in ng:
                    pass

        def f_fin(gi):
            XTt, tX, t0 = XTf[gi % 2], tXf[gi % 2], gi * T
            if l == L - 1:
                P.op("act", f_act(ACTB[:, 0:8, :], XTt[:, :, :], AF.Square), reads=tX, writes=tACTB[0:8])
                for c in range(8):
                    P.op("pe", f_mm(PSF[4][:, :], ONES[:, :], ACTB[:, c, :], c == 0, c == 7), reads=[tACTB[c], tCONST], writes=[tPSF[4]])
                P.op("act", f_act(SG[0][:, :], PSF[4][:, :], AF.Sqrt, bias=EPSC[:, 0:1]), reads=[tPSF[4], tCONST], writes=[tSG[0]])
                P.op("dve", lambda h, o=SG[1][:, :], i=SG[0][:, :]: h.reciprocal(out=o, in_=i), reads=[tSG[0]], writes=[tSG[1]])
                for c in range(8):
                    P.op("dve", f_stt(XTt[:, c, :], XTt[:, c, :], VEC[:, DEPTH * NV + c:DEPTH * NV + c + 1], SG[1][:, :],
                                      ALU.mult, ALU.mult), reads=[tX[c], tVEC, tSG[1]], writes=[tX[c]])
            o = P.dma("sp", f_dma(dview(xout_d, t0), XTt[:, :, :]), "xfo%d" % (gi % 2), reads=tX, writes=[xout_t[gi]])
            if l == L - 1:
                final_out.append(o)

        f_load(0)
        for _ in f_norm_gen(0):
            pass
        for gi in range(NT):
            if gi + 1 < NT:
                f_load(gi + 1)
            f_gu(gi, 0)
            f_out(gi, 0)
            f_gu(gi, 11)
            if gi == NT - 1 and l + 1 < L:
                cur_ = ar.cur
                ar.reset(base_mark)
                WINn_ = ar.alloc("win", [128, 8, DIN], BF16)
                WGn_ = ar.alloc("wg", [128, 16, 128], BF16)
                ar.cur = cur_
                tWINn_, tWGn_ = Trk("win"), Trk("wg")
                load_win(l + 1, WINn_, WGn_, tWINn_, tWGn_, extra_w=tWFI)
                pre_win[0] = (tWINn_, tWGn_)
            f_out(gi, 11, f_norm_gen(gi + 1) if gi + 1 < NT else None)
            f_fin(gi)
        P.barrier()
        xin_d, xin_t = xa_d, TX["xa"]

    P.emit(nc, final_waits=final_out)
    return nc, P


def _segments():
    segs = []
    for k in range(4):
        segs.append([("p", k, 0), ("p", k, 2048), ("s", k, 0)])
    for k in range(4):
        segs.append([("s", 4 + 3 * k + r, 0) for r in range(3)])
    return segs


def _dtab(linked):
    q = np.arange(128)[:, None]
    sidx = np.arange(384)[None, :] - 128
    dist = np.abs(q - sidx).astype(np.float32)
    base = np.where(dist <= 128, -dist, -1.0e9).astype(np.float32)
    d15 = base.copy()
    d16 = base.copy()
    if not linked:
        d15[:, 256:384] = -1.0e9
        d16[:, 0:128] = -1.0e9
    return np.concatenate([base, d15, d16], axis=1).astype(np.float32)


def _fm(v):
    v = np.asarray(v, np.float32)
    return v.reshape(-1, 128).T


_CACHE = {}


def kernel(x_prompt, x_sample, c_prompt, c_sample, w_mod, b_mod, g_norm1, w_in, sink, conv_w, conv_b,
           w_rg, b_rg, w_ig, b_ig, lam, g_attn_out, g_lru_out, w_out, g_norm2, w_ffn_in, w_ffn_out, g_final,
           _depth=DEPTH):
    f32 = np.float32
    segs = _segments()
    xs = {"p": np.asarray(x_prompt, f32), "s": np.asarray(x_sample, f32)}
    cs = {"p": np.asarray(c_prompt, f32), "s": np.asarray(c_sample, f32)}
    vecs = np.zeros((128, DEPTH * NV + 8), f32)
    for l in range(DEPTH):
        o = l * NV
        vecs[:, o + V_G1:o + V_G1 + 8] = _fm(g_norm1[l])
        vecs[:, o + V_G2:o + V_G2 + 8] = _fm(g_norm2[l])
        vecs[:, o + V_BMOD:o + V_BMOD + 48] = _fm(b_mod[l])
        for tap in range(4):
            vecs[:, o + V_CW + tap * 4:o + V_CW + tap * 4 + 4] = _fm(conv_w[l][tap])
        vecs[:, o + V_CB:o + V_CB + 4] = _fm(conv_b[l])
        for d in range(2):
            vecs[:, o + V_BRG + d * 4:o + V_BRG + d * 4 + 4] = _fm(b_rg[l][d])
            vecs[:, o + V_BIG + d * 4:o + V_BIG + d * 4 + 4] = _fm(b_ig[l][d])
            vecs[:, o + V_LAM + d * 4:o + V_LAM + d * 4 + 4] = _fm(lam[l][d])
        vecs[:, o + V_GA:o + V_GA + 4] = _fm(g_attn_out[l])
        vecs[:, o + V_GL:o + V_GL + 4] = _fm(g_lru_out[l])
        vecs[:, o + V_SINK:o + V_SINK + 8] = np.asarray(sink[l], f32)[None, :]
    vecs[:, DEPTH * NV:DEPTH * NV + 8] = _fm(g_final)
    perm = np.concatenate([np.concatenate([np.arange(c * 64, c * 64 + 64), np.arange((4 + c) * 64, (4 + c) * 64 + 64)])
                           for c in range(4)] + [np.arange(512, DIN)])
    w_in_p = np.ascontiguousarray(np.asarray(w_in, f32)[:, :, perm])
    wg = np.zeros((DEPTH, 16, 128, 128), f32)
    for l in range(DEPTH):
        for d in range(2):
            for g, w in ((0, w_rg), (1, w_ig)):
                for c in range(4):
                    j = (d * 2 + g) * 4 + c
                    wg[l, j, 0:64, 0:64] = w[l][d][2 * c]
                    wg[l, j, 64:128, 64:128] = w[l][d][2 * c + 1]
    shared = {"vecs": vecs, "w_mod": np.asarray(w_mod, f32), "w_in": w_in_p, "wg": wg,
              "w_out": np.asarray(w_out, f32), "w_ffn_in": np.asarray(w_ffn_in, f32),
              "w_ffn_out": np.asarray(w_ffn_out, f32)}
    in_maps = []
    for k in range(8):
        xt = np.empty((8, 128, NTOK), f32)
        ct = np.empty((128, 8, 3), f32)
        for si, (kind, b, t0) in enumerate(segs[k]):
            xt[:, :, si * 2048:(si + 1) * 2048] = xs[kind][b, t0:t0 + 2048, :].T.reshape(8, 128, 2048)
            ct[:, :, si] = cs[kind][b].reshape(8, 128).T
        linked = k < 4
        m = dict(shared)
        m["xT"] = xt
        m["cT"] = np.ascontiguousarray(ct.reshape(128, 24))
        m["link"] = np.full((128, 1), 1.0 if linked else 0.0, f32)
        m["dtab"] = _dtab(linked)
        in_maps.append(m)
    if _depth not in _CACHE:
        _CACHE[_depth] = build_program(_depth)[0]
    nc = _CACHE[_depth]
    res = run_bass_kernel_spmd(nc, in_maps, core_ids=list(range(8)))
    y_p = np.empty((4, 4096, D), f32)
    y_s = np.empty((16, 2048, D), f32)
    ys = {"p": y_p, "s": y_s}
    for k in range(8):
        yt = np.asarray(res.results[k]["yT"], f32)
        for si, (kind, b, t0) in enumerate(segs[k]):
            ys[kind][b, t0:t0 + 2048, :] = yt[:, :, si * 2048:(si + 1) * 2048].reshape(D, 2048).T
    return (y_p, y_s)
```

```python
import contextlib
import numpy as np
import concourse.bass as bass
import concourse.mybir as mybir
from concourse.bass_utils import run_bass_kernel_spmd

F32 = mybir.dt.float32
BF16 = mybir.dt.bfloat16
AF = mybir.ActivationFunctionType
ALU = mybir.AluOpType
AX = mybir.AxisListType

D = 1024
DEPTH = 4
NTOK = 6144
T = 512
NT = NTOK // T
DIN = 1792
DFF = 2816
EPS = 1e-6
NV = 124
V_G1, V_G2, V_BMOD, V_CW, V_CB, V_BRG, V_BIG, V_LAM, V_GA, V_GL, V_SINK = 0, 8, 16, 64, 80, 84, 92, 100, 108, 112, 116
UNITS = ((0, 8, 4), (8, 4, None))

ENGS = ("pe", "act", "dve", "pool", "sp")


class Trk:
    __slots__ = ("name", "w", "rs")

    def __init__(self, name=""):
        self.name = name
        self.w = None
        self.rs = []


def trks(name, n):
    return [Trk("%s%d" % (name, i)) for i in range(n)]


class Op:
    __slots__ = ("eng", "idx", "fn", "waits", "dma", "need_inc", "semval", "dkey")

    def __init__(self, eng, idx, fn, dma, dkey):
        self.eng = eng
        self.idx = idx
        self.fn = fn
        self.waits = []
        self.dma = dma
        self.need_inc = False
        self.semval = None
        self.dkey = dkey


class Prog:
    def __init__(self):
        self.ops = {e: [] for e in ENGS}
        self.waited = {}
        self.waited_dma = {}
        self.dma_keys = {}
        self.last_dma = {}
        self.last_comp = {}

    def op(self, eng, fn, reads=(), writes=(), dma=False, dkey=None):
        lst = self.ops[eng]
        o = Op(eng, len(lst), fn, dma, dkey)
        deps = []
        for t in reads:
            if t.w is not None:
                deps.append((t.w, True))
        for t in writes:
            if t.w is not None:
                deps.append((t.w, True))
            deps.extend((r, False) for r in reversed(t.rs))
        for d, isw in deps:
            self._add_wait(o, d, isw)
        for t in reads:
            t.rs.append(o)
        for t in writes:
            t.w = o
            t.rs = []
        lst.append(o)
        if dma:
            self.last_dma[dkey] = o
        else:
            self.last_comp[eng] = o
        return o

    def _add_wait(self, o, d, isw=True, force=False):
        if d is o:
            return
        if d.dma:
            key = (o.eng, d.dkey)
            prev = self.waited_dma.get(key)
            if prev is not None and prev >= d.semval:
                return
            self.waited_dma[key] = d.semval
            o.waits.append(d)
            return
        if d.eng == o.eng and not force:
            if o.eng == "pe":
                return
        key = (o.eng, d.eng)
        prev = self.waited.get(key, -1)
        if prev >= d.idx:
            return
        self.waited[key] = d.idx
        o.waits.append(d)
        d.need_inc = True

    def dma(self, eng, fn, dkey, reads=(), writes=()):
        self.dma_keys.setdefault(dkey, 0)
        self.dma_keys[dkey] += 16
        val = self.dma_keys[dkey]
        o = self.op(eng, fn, reads=reads, writes=writes, dma=True, dkey=dkey)
        o.semval = val
        return o

    def barrier(self):
        comp = dict(self.last_comp)
        dm = dict(self.last_dma)
        spn = self.op("sp", lambda h: h.nop())
        for d in list(comp.values()) + list(dm.values()):
            self._add_wait(spn, d, True, force=True)
        for e in ENGS:
            if e == "sp":
                continue
            n = self.op(e, lambda h: h.nop())
            self._add_wait(n, spn, True)

    def emit(self, nc, final_waits=()):
        for e in ENGS:
            c = 0
            for o in self.ops[e]:
                if o.dma:
                    continue
                if o.need_inc:
                    c += 1
                    o.semval = c
        with contextlib.ExitStack() as st:
            esem = {e: st.enter_context(nc.semaphore("S_" + e)) for e in ENGS}
            dsem = {k: st.enter_context(nc.semaphore("D_%d" % i)) for i, k in enumerate(self.dma_keys)}
            block = st.enter_context(nc.Block())
            prog = self

            def run(engname, h):
                for o in prog.ops[engname]:
                    for d in o.waits:
                        if d.dma:
                            h.wait_ge(dsem[d.dkey], d.semval)
                        else:
                            h.wait_ge(esem[d.eng], d.semval)
                    ins = o.fn(h)
                    if o.dma:
                        ins.then_inc(dsem[o.dkey], 16)
                    elif o.need_inc:
                        ins.then_inc(esem[o.eng], 1)
                if engname == "sp":
                    for o in final_waits:
                        h.wait_ge(dsem[o.dkey], o.semval)

            @block.tensor
            def _(h):
                run("pe", h)

            @block.scalar
            def _(h):
                run("act", h)

            @block.vector
            def _(h):
                run("dve", h)

            @block.gpsimd
            def _(h):
                run("pool", h)

            @block.sync
            def _(h):
                run("sp", h)


def f_mm(out, lhsT, rhs, start, stop):
    return lambda h: h.matmul(out, lhsT=lhsT, rhs=rhs, start=start, stop=stop)


def f_tr(out, in_, ident):
    return lambda h: h.transpose(out, in_, ident)


def f_act(out, in_, func, bias=None, scale=None, accum_out=None):
    kw = {}
    if bias is not None:
        kw["bias"] = bias
    if scale is not None:
        kw["scale"] = scale
    if accum_out is not None:
        kw["accum_out"] = accum_out
    return lambda h: h.activation(out=out, in_=in_, func=func, **kw)


def f_tt(out, in0, in1, op):
    return lambda h: h.tensor_tensor(out=out, in0=in0, in1=in1, op=op)


def f_ts(out, in0, s1, s2, op0, op1=None):
    if op1 is None:
        return lambda h: h.tensor_scalar(out=out, in0=in0, scalar1=s1, scalar2=None, op0=op0)
    return lambda h: h.tensor_scalar(out=out, in0=in0, scalar1=s1, scalar2=s2, op0=op0, op1=op1)


def f_stt(out, in0, scalar, in1, op0, op1):
    return lambda h: h.scalar_tensor_tensor(out=out, in0=in0, scalar=scalar, in1=in1, op0=op0, op1=op1)


def f_cp(out, in_):
    return lambda h: h.tensor_copy(out=out, in_=in_)


def f_dma(out, in_):
    return lambda h: h.dma_start(out=out, in_=in_)


def f_scan(out, d0, d1, init):
    return lambda h: h.tensor_tensor_scan(out=out, data0=d0, data1=d1, initial=init, op0=ALU.mult, op1=ALU.add)


class Arena:
    def __init__(self, nc, lo, hi):
        self.nc, self.lo, self.hi, self.cur, self.n = nc, lo, hi, lo, 0

    def reset(self, to=None):
        self.cur = self.lo if to is None else to

    def mark(self):
        return self.cur

    def alloc(self, name, shape, dt):
        esz = 4 if dt == F32 else 2
        nbytes = int(np.prod(shape[1:])) * esz
        off = (self.cur + 63) // 64 * 64
        assert off + nbytes <= self.hi, ("SBUF arena overflow", name, off, nbytes, self.hi)
        self.cur = off + nbytes
        self.n += 1
        return self.nc.alloc_sbuf_tensor_at("%s_%d" % (name, self.n), list(shape), dt, offset=off)


def build_program(depth=DEPTH):
    nc = bass.Bass("TRN2", target_bir_lowering=False)
    L = depth
    dr = lambda name, shape, dt=F32, kind="ExternalInput": nc.dram_tensor(name, list(shape), dt, kind=kind).ap()
    xT_d = dr("xT", [8, 128, NTOK])
    cT_d = dr("cT", [128, 24])
    link_d = dr("link", [128, 1])
    dtab_d = dr("dtab", [128, 3 * 384])
    vecs_d = dr("vecs", [128, DEPTH * NV + 8])
    wmod_d = dr("w_mod", [DEPTH, D, 6 * D])
    win_d = dr("w_in", [DEPTH, D, DIN])
    wg_d = dr("wg", [DEPTH, 16, 128, 128])
    wout_d = dr("w_out", [DEPTH, D, D])
    wfi_d = dr("w_ffn_in", [DEPTH, D, 2 * DFF])
    wfo_d = dr("w_ffn_out", [DEPTH, DFF, D])
    yT_d = dr("yT", [8, 128, NTOK], kind="ExternalOutput")
    xa_d = dr("xa_s", [8, 128, NTOK], kind="Internal")
    xb_d = dr("xb_s", [8, 128, NTOK], kind="Internal")
    xc_d = dr("xc_s", [4, 128, NTOK], kind="Internal")
    hf_d = dr("hf_s", [4, 128, NTOK], kind="Internal")
    gg_d = dr("gg_s", [4, 128, NTOK], BF16, kind="Internal")
    an_d = dr("an_s", [4, 128, NTOK], BF16, kind="Internal")

    def dview(d, t0, n=T):
        return d[:, :, t0:t0 + n].rearrange("c p t -> p c t")

    P = Prog()
    ar = Arena(nc, 16640, 229376)
    TX = {"in": trks("xin", NT), "xa": trks("xa", NT), "xb": trks("xb", NT), "y": trks("y", NT)}
    TXC, THF, TGG, TAN = trks("xcs", NT), trks("hfs", NT), trks("ggs", NT), trks("ans", NT)

    VEC = ar.alloc("vec", [128, DEPTH * NV + 8], F32)
    MOD = ar.alloc("mod", [128, DEPTH * 6 * 8 * 3], F32)
    C8 = ar.alloc("c8", [128, DEPTH * 8], F32)
    SK8 = ar.alloc("sk8", [128, DEPTH * 8], F32)
    HB2 = ar.alloc("hb2", [128, DEPTH * 16], F32)
    QUARTER = ar.alloc("quarter", [128, 1], F32)
    CA = ar.alloc("ca", [128, 24], BF16)
    DT = ar.alloc("dtab", [128, 3 * 384], F32)
    LINK = ar.alloc("link", [128, 1], F32)
    ONES = ar.alloc("ones", [128, 128], BF16)
    ONES5 = ar.alloc("ones5", [128, 128], BF16)
    IDN = ar.alloc("idn", [128, 128], BF16)
    EPSC = ar.alloc("epsc", [128, 1], F32)
    ONEC = ar.alloc("onec", [128, 1], F32)
    NHALFC = ar.alloc("nhalfc", [128, 1], F32)
    tVEC, tMOD, tC8, tSK8, tDT, tLINK, tCONST = (Trk(n) for n in ("vec", "mod", "c8", "sk8", "dt", "link", "const"))
    base_mark = ar.mark()

    PSF = [nc.alloc_psum_tensor("psf%d" % i, [128, 512], F32) for i in range(6)]
    PSB = [nc.alloc_psum_tensor("psb%d" % i, [128, 1024], BF16) for i in range(2)]
    tPSF = trks("psf", 6)
    tPSB = trks("psb", 2)

    def mod_ap(l, k, c, seg):
        o = ((l * 6 + k) * 8 + c) * 3 + seg
        return MOD[:, o:o + 1]

    def vec_ap(l, off, n=1):
        return VEC[:, l * NV + off:l * NV + off + n]

    P.dma("sp", f_dma(VEC[:], vecs_d), "vec", writes=[tVEC])
    P.dma("sp", f_dma(DT[:], dtab_d), "dt", writes=[tDT])
    P.dma("sp", f_dma(LINK[:], link_d), "link", writes=[tLINK])
    P.op("pool", lambda h: h.memset(ONES[:], 1.0 / 1024), writes=[tCONST])
    P.op("pool", lambda h: h.memset(ONES5[:], 1.0 / 512), writes=[tCONST])
    P.op("pool", lambda h: h.memset(EPSC[:], EPS), writes=[tCONST])
    P.op("pool", lambda h: h.memset(ONEC[:], 1.0), writes=[tCONST])
    P.op("pool", lambda h: h.memset(NHALFC[:], -0.5), writes=[tCONST])
    P.op("pool", lambda h: h.memset(QUARTER[:], 0.25), writes=[tCONST])
    P.op("pool", lambda h: h.memset(IDN[:], 0.0), writes=[tCONST])
    P.op("pool", lambda h: h.affine_select(out=IDN[:], in_=IDN[:], pattern=[[-1, 128]], compare_op=ALU.not_equal,
                                            fill=1.0, base=0, channel_multiplier=1), reads=[tCONST], writes=[tCONST])
    CT = ar.alloc("ct", [128, 24], F32)
    tCT, tCA = Trk("ct"), Trk("ca")
    P.dma("sp", f_dma(CT[:], cT_d), "ct", writes=[tCT])
    P.op("act", f_act(CA[:], CT[:], AF.Silu), reads=[tCT], writes=[tCA])
    modit = [0]

    def mod_gen(l, WM, tWM, bankfn):
        for k in range(6):
            s = modit[0] % 2
            modit[0] += 1
            src = wmod_d[l][:, k * D:(k + 1) * D].rearrange("(c p) n -> p c n", p=128)
            P.dma("pool", f_dma(WM[s][:], src), "wm%d" % s, writes=[tWM[s]])
            pb = bankfn()
            ps = PSF[pb]
            for fc in range(8):
                for kc in range(8):
                    P.op("pe", f_mm(ps[:, fc * 3:fc * 3 + 3], WM[s][:, kc, fc * 128:(fc + 1) * 128],
                                    CA[:, kc * 3:kc * 3 + 3], kc == 0, kc == 7),
                         reads=[tWM[s], tCA], writes=[tPSF[pb]])
            o = (l * 6 + k) * 24
            for seg in range(3):
                P.op("dve", f_tt(MOD[:, o + seg:o + 24:3], ps[:, seg:24:3], vec_ap(l, V_BMOD + k * 8, 8), ALU.add),
                     reads=[tPSF[pb], tVEC], writes=[tMOD])
            if k in (1, 4):
                gof = V_G1 if k == 1 else V_G2
                for seg in range(3):
                    P.op("dve", f_stt(MOD[:, o + seg:o + 24:3], MOD[:, o + seg:o + 24:3], 1.0, vec_ap(l, gof, 8),
                                      ALU.add, ALU.mult), reads=[tMOD, tVEC], writes=[tMOD])
            yield

    WM0 = [ar.alloc("wm", [128, 8, D], BF16) for _ in range(2)]
    tWM0 = trks("wm", 2)
    pbk0 = [0]

    def pbank0():
        pbk0[0] += 1
        return pbk0[0] % 6

    for _ in mod_gen(0, WM0, tWM0, pbank0):
        pass
    E1 = ar.alloc("e1", [128, DEPTH * 8], F32)
    E2 = ar.alloc("e2", [128, DEPTH * 8], F32)
    tE = Trk("e")
    for l in range(L):
        sl = slice(l * 8, l * 8 + 8)
        P.op("act", f_act(E1[:, sl], vec_ap(l, V_LAM, 8), AF.Exp, scale=-1.0), reads=[tVEC], writes=[tE])
        P.op("dve", f_ts(E2[:, sl], E1[:, sl], -0.25, 1.0 / 3, ALU.mult, ALU.add), reads=[tE], writes=[tE])
        P.op("dve", f_tt(E2[:, sl], E2[:, sl], E1[:, sl], ALU.mult), reads=[tE], writes=[tE])
        P.op("dve", f_ts(E2[:, sl], E2[:, sl], -1.0, 0.5, ALU.mult, ALU.add), reads=[tE], writes=[tE])
        P.op("dve", f_tt(E2[:, sl], E2[:, sl], E1[:, sl], ALU.mult), reads=[tE], writes=[tE])
        P.op("dve", f_ts(E2[:, sl], E2[:, sl], -1.0, 1.0, ALU.mult, ALU.add), reads=[tE], writes=[tE])
        P.op("dve", f_tt(E2[:, sl], E2[:, sl], E1[:, sl], ALU.mult), reads=[tE], writes=[tE])
        P.op("dve", f_ts(C8[:, sl], E2[:, sl], -4.0, None, ALU.mult), reads=[tE], writes=[tC8])
        P.op("dve", f_ts(HB2[:, l * 16:l * 16 + 16], vec_ap(l, V_BRG, 16), 0.5, None, ALU.mult), reads=[tVEC], writes=[tC8])
        P.op("dve", f_ts(SK8[:, sl], vec_ap(l, V_SINK, 8), 8.0, None, ALU.mult), reads=[tVEC], writes=[tSK8])
    P.barrier()
    ar.reset(base_mark)

    def norm_gen(XTt, tX, HN, tHN, TMP, tTMP, RS, tRS, l, kg, ksh, seg, mmb):
        P.op("act", f_act(HN[:, :, :], XTt[:, :, :], AF.Square), reads=tX, writes=tHN)
        yield
        ps = PSF[mmb]
        for c in range(8):
            P.op("pe", f_mm(ps[:, :], ONES[:, :], HN[:, c, :], c == 0, c == 7), reads=[tHN[c], tCONST], writes=[tPSF[mmb]])
        P.op("act", f_act(TMP[0][:, :], ps[:, :], AF.Sqrt, bias=EPSC[:, 0:1]), reads=[tPSF[mmb], tCONST], writes=[tTMP[0]])
        P.op("dve", lambda h, o=RS[:, :], i=TMP[0][:, :]: h.reciprocal(out=o, in_=i), reads=[tTMP[0]], writes=[tRS])
        yield
        for c in range(8):
            b = 1 + (c % 2)
            P.op("dve", f_stt(TMP[b][:, :], XTt[:, c, :], mod_ap(l, kg, c, seg), RS[:, :], ALU.mult, ALU.mult),
                 reads=[tX[c], tMOD, tRS], writes=[tTMP[b]])
            P.op("act", f_act(HN[:, c, :], TMP[b][:, :], AF.Identity, bias=mod_ap(l, ksh, c, seg)),
                 reads=[tTMP[b], tMOD], writes=[tHN[c]])
            if c % 2 == 1:
                yield

    def norm_mod(*a):
        for _ in norm_gen(*a):
            pass

    def interleave(gens):
        gens = list(gens)
        while gens:
            for g in list(gens):
                try:
                    next(g)
                except StopIteration:
                    gens.remove(g)

    def g1(l, d, c, XCB, tXCB, WG, tWG, GB, tGB, bankfn):
        for g, nm in ((0, "RA"), (1, "IU")):
            mb = bankfn()
            hb = HB2[:, l * 16 + g * 8 + d * 4 + c:l * 16 + g * 8 + d * 4 + c + 1]
            P.op("pe", f_mm(PSF[mb][:, :], WG[:, (d * 2 + g) * 4 + c, :], XCB[:, c, :], True, True),
                 reads=[tWG, tXCB[c]], writes=[tPSF[mb]])
            P.op("act", f_act(GB[nm][c][:, :], PSF[mb][:, :], AF.Tanh, bias=hb, scale=0.5),
                 reads=[tPSF[mb], tC8], writes=[tGB[nm][c]])
        hc8 = C8[:, l * 8 + d * 4 + c:l * 8 + d * 4 + c + 1]
        P.op("act", f_act(GB["RA"][c][:, :], GB["RA"][c][:, :], AF.Exp, bias=hc8, scale=hc8),
             reads=[tGB["RA"][c], tC8], writes=[tGB["RA"][c]])
        P.op("act", f_act(GB["S"][c][:, :], GB["RA"][c][:, :], AF.Square), reads=[tGB["RA"][c]], writes=[tGB["S"][c]])

    def g2(c, XC, tXC, GB, tGB):
        P.op("dve", f_ts(GB["S"][c][:, :], GB["S"][c][:, :], 0.99999994, -1.0, ALU.min, ALU.mult),
             reads=[tGB["S"][c]], writes=[tGB["S"][c]])
        P.op("dve", f_stt(GB["IU"][c][:, :], GB["IU"][c][:, :], 1.0, XC[:, c, :], ALU.add, ALU.mult),
             reads=[tGB["IU"][c], tXC[c]], writes=[tGB["IU"][c]])

    def gsq(c, GB, tGB):
        P.op("act", f_act(GB["S"][c][:, :], GB["S"][c][:, :], AF.Sqrt, bias=QUARTER[:, 0:1], scale=0.25),
             reads=[tGB["S"][c], tCONST], writes=[tGB["S"][c]])

    def g3(c, GB, tGB):
        P.op("dve", f_tt(GB["IU"][c][:, :], GB["IU"][c][:, :], GB["S"][c][:, :], ALU.mult),
             reads=[tGB["IU"][c], tGB["S"][c]], writes=[tGB["IU"][c]])

    def gates_gen(l, d, XC, tXC, XCB, tXCB, WG, tWG, GB, tGB, bankfn):
        for c in range(4):
            g1(l, d, c, XCB, tXCB, WG, tWG, GB, tGB, bankfn)
            yield
        for c in range(4):
            g2(c, XC, tXC, GB, tGB)
        yield
        for c in range(4):
            gsq(c, GB, tGB)
        yield
        for c in range(4):
            g3(c, GB, tGB)
        yield

    def alloc_gb():
        GB = {n: [ar.alloc("gb" + n, [128, T], F32) for _ in range(4)] for n in ("RA", "IU", "S")}
        tGB = {n: trks("gb" + n, 4) for n in ("RA", "IU", "S")}
        return GB, tGB

    final_out = []
    pre_win = [None]
    pre_b = [None]
    xin_d, xin_t = xT_d, TX["in"]
    for l in range(L):
        xout_d, xout_t = (yT_d, TX["y"]) if l == L - 1 else (xa_d, TX["xa"])
        ar.reset(base_mark)
        WIN = ar.alloc("win", [128, 8, DIN], BF16)
        WG = ar.alloc("wg", [128, 16, 128], BF16)
        def load_win(l_, WIN_, WG_, tWIN_, tWG_, extra_w=()):
            wsrc_ = win_d[l_].rearrange("(c p) n -> p c n", p=128)
            for q4 in range(4):
                P.dma("pool", f_dma(WIN_[:, :, q4 * 448:(q4 + 1) * 448], wsrc_[:, :, q4 * 448:(q4 + 1) * 448]),
                      "win%d" % q4, writes=[tWIN_] + list(extra_w))
            P.dma("pool", f_dma(WG_[:], wg_d[l_].rearrange("j p m -> p j m")), "wg", writes=[tWG_] + list(extra_w))

        if pre_win[0] is None:
            tWIN, tWG = Trk("win"), Trk("wg")
            load_win(l, WIN, WG, tWIN, tWG)
        else:
            tWIN, tWG = pre_win[0]
            pre_win[0] = None
        XTt = ar.alloc("xt", [128, 8, T], F32)
        tX = trks("x", 8)
        HN = ar.alloc("hn", [128, 8, T], BF16)
        tHN = trks("hn", 8)
        TMP = [ar.alloc("tmp", [128, T], F32) for _ in range(3)]
        tTMP = trks("tmp", 3)
        RS = ar.alloc("rs", [128, T], F32)
        tRS = Trk("rs")
        S = 8 * T
        KU = ar.alloc("ku", [128, S], BF16)
        tKU = trks("ku", 8)
        VU = ar.alloc("vu", [128, 32, 128], BF16)
        tVU = trks("vu", 8)
        QR = ar.alloc("qr", [128, 4, 2, T], BF16)
        tQR = trks("qr", 2)
        XR = ar.alloc("xr", [128, 4, 3, T + 3], F32)
        tXR = trks("xr", 3)
        XC = [ar.alloc("xc", [128, 4, T], F32)] * 2
        tXCs = [trks("xc", 4)] * 2
        CARRY = ar.alloc("carry", [128, 4], F32)
        tCARRY = trks("carry", 4)
        XCB = ar.alloc("xcb", [128, 4, T], BF16)
        tXCB = trks("xcb", 4)
        GB, tGB = alloc_gb()
        HF = [ar.alloc("hf", [128, 4, T], F32)] * 2
        tHF = [trks("hf", 4)] * 2
        GG = [ar.alloc("gg", [128, 4, T], BF16)] * 2
        tGG = [Trk("gg")] * 2
        ANT = [ar.alloc("ant", [128, 4, T], BF16) for _ in range(2)]
        tANT = trks("ant", 2)
        TT = [ar.alloc("tt", [128, 384], F32) for _ in range(4)]
        tTT = trks("tt", 4)
        PM = [ar.alloc("pm", [128, 384], BF16) for _ in range(4)]
        tPM = trks("pm", 4)
        PT = [ar.alloc("pt", [128, 3, 128], BF16) for _ in range(4)]
        tPT = trks("pt", 4)
        ST = ar.alloc("st", [128, 128], F32)
        tSTp = [{n: Trk("st" + n) for n in ("es", "rden", "ssq", "rsa")} for _ in range(2)]
        tMXp, tNEGBp, tRSUMp = [trks("mx", 8) for _ in range(2)], [trks("negb", 8) for _ in range(2)], [trks("rsum", 8) for _ in range(2)]
        tPSBh = trks("psbh", 2)
        ATT = ar.alloc("att", [128, 512], F32)
        tATT = Trk("att")
        ATN = ar.alloc("atn", [128, 512], BF16)
        tATN = Trk("atn")
        JUNK = ar.alloc("junk", [128, 512], BF16)
        CTMP = ar.alloc("ctmp", [128, T], F32)
        tCTMP = Trk("ctmp")
        tJUNK = Trk("junk")
        pending_tail = [None]
        for (ut0, unt, ulink) in UNITS:
            mmrr = [0]

            def mmbank():
                b = mmrr[0] % 3
                mmrr[0] += 1
                return b

            gbank = [0]

            def gatebank():
                b = 3 + gbank[0] % 2
                gbank[0] += 1
                return b

            hcnt = [0]

            def lru_conv(j):
                s = j % 3
                for c in range(4):
                    cw = lambda tap: vec_ap(l, V_CW + tap * 4 + c)
                    P.op("pool", f_ts(XC[0][:, c, :], XR[:, c, s, 0:T], cw(0), vec_ap(l, V_CB + c), ALU.mult, ALU.add),
                         reads=[tXR[s], tVEC], writes=[tXCs[0][c]])
                    for tap in (1, 2, 3):
                        P.op("pool", f_ts(CTMP[:, :], XR[:, c, s, tap:tap + T], cw(tap), 0.0, ALU.mult, ALU.add),
                             reads=[tXR[s], tVEC], writes=[tCTMP])
                        P.op("pool", f_tt(XC[0][:, c, :], XC[0][:, c, :], CTMP[:, :], ALU.add),
                             reads=[tCTMP, tXCs[0][c]], writes=[tXCs[0][c]])
                    P.op("pool", f_cp(XCB[:, c, :], XC[0][:, c, :]), reads=[tXCs[0][c]], writes=[tXCB[c]])

            def lruG(j, ut0=ut0, ulink=ulink):
                gi = ut0 + j
                for c in range(4):
                    g1(l, 0, c, XCB, tXCB, WG, tWG, GB, tGB, gatebank)
                    yield
                for c in range(4):
                    g2(c, XC[0], tXCs[0], GB, tGB)
                yield
                for c in range(4):
                    gsq(c, GB, tGB)
                yield
                for c in range(4):
                    g3(c, GB, tGB)
                    if ulink is not None and j == ulink:
                        P.op("dve", f_ts(GB["RA"][c][:, 0:1], GB["RA"][c][:, 0:1], LINK[:, 0:1], None, ALU.mult),
                             reads=[tGB["RA"][c], tLINK], writes=[tGB["RA"][c]])
                    init = 0.0 if j == 0 else CARRY[:, c:c + 1]
                    rd = [tGB["RA"][c], tGB["IU"][c]] + ([] if j == 0 else [tCARRY[c]])
                    P.op("dve", f_scan(HF[0][:, c, :], GB["RA"][c][:, :], GB["IU"][c][:, :], init), reads=rd, writes=[tHF[0][c]])
                    P.op("dve", f_cp(CARRY[:, c:c + 1], HF[0][:, c, T - 1:T]), reads=[tHF[0][c]], writes=[tCARRY[c]])
                    yield
                t0 = gi * T
                P.dma("sp", f_dma(dview(xc_d, t0), XC[0][:, :, :]), "xco", reads=tXCs[0], writes=[TXC[gi]])
                P.dma("sp", f_dma(dview(hf_d, t0), HF[0][:, :, :]), "hfo", reads=tHF[0], writes=[THF[gi]])

            def mix(j, ngen=None):
                gi = ut0 + j
                qs = j % 2
                asl = j % 2
                nblk = unt * 4
                lru_conv(j)

                def lru_step(p):
                    pass

                def geom(b):
                    n = 4 * j + b
                    lo = max(n - 1, 0)
                    hi = min(n + 1, nblk - 1)
                    nkb = hi - lo + 1
                    doff = 0 if lo == n - 1 else 128
                    tab = 0
                    if ulink is not None and n == ulink * 4 - 1:
                        tab = 1
                    if ulink is not None and n == ulink * 4:
                        tab = 2
                    tl = sorted(set((lo // 4, hi // 4)))
                    return n, lo, hi, nkb, doff, tab, [tKU[tt] for tt in tl], [tVU[tt] for tt in tl]

                sbank = {}

                def SA(p):
                    b, c = p // 4, p % 4
                    n, lo, hi, nkb, doff, tab, ktr, vtr = geom(b)
                    nk = nkb * 128
                    for hh in range(2):
                        sb = hcnt[0] % 3
                        hcnt[0] += 1
                        sbank[(p, hh)] = sb
                        pl, ph = hh * 64, hh * 64 + 64
                        P.op("pe", f_mm(PSF[sb][:, 0:nk], QR[pl:ph, c, qs, b * 128:(b + 1) * 128],
                                        KU[pl:ph, lo * 128:(hi + 1) * 128], True, True),
                             reads=[tQR[qs]] + ktr, writes=[tPSF[sb]])

                def SB(p, hh):
                    b, c = p // 4, p % 4
                    n, lo, hi, nkb, doff, tab, ktr, vtr = geom(b)
                    nk = nkb * 128
                    so = (b % 2) * 64
                    Dv = DT[:, tab * 384 + doff:tab * 384 + doff + nk]
                    if True:
                        h_ = c + 4 * hh
                        sb = sbank[(p, hh)]
                        r4 = (c * 2 + hh) % 4
                        slope8 = 8.0 * 2.0 ** (-(h_ + 1))
                        P.op("dve", f_stt(TT[r4][:, 0:nk], Dv, slope8, PSF[sb][:, 0:nk], ALU.mult, ALU.add),
                             reads=[tDT, tPSF[sb]], writes=[tTT[r4]])
                        P.op("dve", lambda h, o=ST[:, so + h_:so + h_ + 1], i=TT[r4][:, 0:nk]: h.reduce_max(out=o, in_=i, axis=AX.X),
                             reads=[tTT[r4]], writes=[tMXp[b % 2][h_]])
                        P.op("dve", f_ts(ST[:, so + 8 + h_:so + 9 + h_], ST[:, so + h_:so + h_ + 1],
                                         SK8[:, l * 8 + h_:l * 8 + h_ + 1], -0.125, ALU.max, ALU.mult),
                             reads=[tMXp[b % 2][h_], tSK8], writes=[tNEGBp[b % 2][h_]])
                        P.op("act", f_act(PM[r4][:, 0:nk], TT[r4][:, 0:nk], AF.Exp, bias=ST[:, so + 8 + h_:so + 9 + h_], scale=0.125,
                                          accum_out=ST[:, so + 16 + h_:so + 17 + h_]),
                             reads=[tTT[r4], tNEGBp[b % 2][h_]], writes=[tPM[r4], tRSUMp[b % 2][h_]])

                def SC(p):
                    b, c = p // 4, p % 4
                    n, lo, hi, nkb, doff, tab, ktr, vtr = geom(b)
                    nk = nkb * 128
                    for hh in range(2):
                        r4 = (c * 2 + hh) % 4
                        pbk = hh
                        for jj in range(nkb):
                            P.op("pe", f_tr(PSB[pbk][:, jj * 128:(jj + 1) * 128],
                                            PM[r4][:, jj * 128:(jj + 1) * 128], IDN[:, :]),
                                 reads=[tPM[r4], tCONST], writes=[tPSB[pbk]])
                        P.op("act", f_act(PT[r4][:, 0:nkb, :],
                                          PSB[pbk][:, 0:nk].rearrange("p (j q) -> p j q", q=128), AF.Copy),
                             reads=[tPSB[pbk]], writes=[tPT[r4]])

                def SD(p):
                    b, c = p // 4, p % 4
                    n, lo, hi, nkb, doff, tab, ktr, vtr = geom(b)
                    for hh in range(2):
                        h_ = c + 4 * hh
                        r4 = (c * 2 + hh) % 4
                        for jj in range(nkb):
                            P.op("pe", f_mm(PSF[5][:, h_ * 64:(h_ + 1) * 64], PT[r4][:, jj, :],
                                            VU[:, lo + jj, hh * 64:(hh + 1) * 64], jj == 0, jj == nkb - 1),
                                 reads=[tPT[r4]] + vtr, writes=[tPSF[5]])

                def EPI(b, k):
                    so = (b % 2) * 64
                    tST = tSTp[b % 2]
                    if k == 0:
                        P.op("dve", f_tt(ST[:, so + 24:so + 32], ST[:, so + 8:so + 16], vec_ap(l, V_SINK, 8), ALU.add),
                             reads=tNEGBp[b % 2] + [tVEC], writes=[tST["es"]])
                        P.op("act", f_act(ST[:, so + 24:so + 32], ST[:, so + 24:so + 32], AF.Exp), reads=[tST["es"]], writes=[tST["es"]])
                    elif k == 1:
                        P.op("dve", f_tt(ST[:, so + 32:so + 40], ST[:, so + 24:so + 32], ST[:, so + 16:so + 24], ALU.add),
                             reads=[tST["es"]] + tRSUMp[b % 2], writes=[tST["rden"]])
                        P.op("dve", lambda h, o=ST[:, so + 32:so + 40]: h.reciprocal(out=o, in_=o), reads=[tST["rden"]], writes=[tST["rden"]])
                        P.op("dve", f_tt(ATT[:, :].rearrange("p (h d) -> p h d", d=64),
                                         PSF[5][:, :].rearrange("p (h d) -> p h d", d=64),
                                         ST[:, so + 32:so + 40].unsqueeze(2).to_broadcast([128, 8, 64]), ALU.mult),
                             reads=[tPSF[5], tST["rden"]], writes=[tATT])
                    elif k == 2:
                        P.op("act", f_act(JUNK[:, :], ATT[:, :], AF.Square, accum_out=ST[:, so + 40:so + 41]),
                             reads=[tATT], writes=[tJUNK, tST["ssq"]])
                    elif k == 3:
                        P.op("dve", f_ts(ST[:, so + 41:so + 42], ST[:, so + 40:so + 41], 1.0 / 512, EPS, ALU.mult, ALU.add),
                             reads=[tST["ssq"]], writes=[tST["rsa"]])
                        P.op("pool", f_tt(ST[:, so + 41:so + 42], ST[:, so + 41:so + 42], NHALFC[:, 0:1], ALU.pow),
                             reads=[tST["rsa"], tCONST], writes=[tST["rsa"]])
                    elif k == 4:
                        P.op("dve", f_ts(ATN[:, :], ATT[:, :], ST[:, so + 41:so + 42], None, ALU.mult),
                             reads=[tATT, tST["rsa"]], writes=[tATN])
                    elif k == 5:
                        for cc in range(4):
                            P.op("pe", f_tr(PSB[1][:, 512 + cc * 128:512 + (cc + 1) * 128], ATN[:, cc * 128:(cc + 1) * 128], IDN[:, :]),
                                 reads=[tATN, tCONST], writes=[tPSB[1]])
                        for cc in range(4):
                            P.op("act", f_act(ANT[asl][:, cc, b * 128:(b + 1) * 128], PSB[1][:, 512 + cc * 128:512 + (cc + 1) * 128],
                                              AF.Identity, scale=vec_ap(l, V_GA + cc)),
                                 reads=[tPSB[1], tVEC], writes=[tANT[asl]])

                NP = 16
                SA(0)
                for p in range(NP + 2):
                    if p < NP:
                        SB(p, 0)
                    if p + 1 < NP:
                        SA(p + 1)
                    if p < NP:
                        SB(p, 1)
                    if 0 <= p - 2 < NP:
                        SD(p - 2)
                    for b_ in range(4):
                        k_ = p - (4 * b_ + 4)
                        if 0 <= k_ < 6:
                            EPI(b_, k_)
                    if 0 <= p - 1 < NP:
                        SC(p - 1)
                    lru_step(p)
                    if ngen is not None and p >= 2:
                        next(ngen, None)
                if ngen is not None:
                    for _ in ngen:
                        pass

                def epi_tail():
                    for k_ in range(2, 6):
                        EPI(3, k_)
                        yield
                    P.dma("sp", f_dma(dview(an_d, gi * T), ANT[asl][:, :, :]), "ano%d" % asl, reads=[tANT[asl]], writes=[TAN[gi]])

                return epi_tail()

            def xload(i):
                gi = ut0 + i
                P.dma("sp", f_dma(XTt[:, :, :], dview(xin_d, gi * T)), "xt", reads=[xin_t[gi]], writes=tX)

            def ngen_for(i):
                return norm_gen(XTt, tX, HN, tHN, TMP, tTMP, RS, tRS, l, 1, 0, (ut0 + i) // 4, gatebank())

            tail = pending_tail[0]
            pending_tail[0] = None
            epi_t = [None]
            xload(0)
            for _ in ngen_for(0):
                if tail is not None:
                    next(tail, None)
            def proj(i):
                gi = ut0 + i
                seg = gi // 4
                t0 = gi * T
                s = i % 2
                xs_ = i % 3
                xp_ = (i - 1) % 3
                for m in range(4):
                    mb = mmbank()
                    for c in range(8):
                        P.op("pe", f_mm(PSF[mb][:, :], WIN[:, c, m * 128:(m + 1) * 128], HN[:, c, :], c == 0, c == 7),
                             reads=[tWIN, tHN[c]], writes=[tPSF[mb]])
                    P.op("act", f_act(QR[:, m, s, :], PSF[mb][:, :], AF.Copy), reads=[tPSF[mb]], writes=[tQR[s]])
                    yield
                mb = mmbank()
                for c in range(8):
                    P.op("pe", f_mm(PSF[mb][:, :], WIN[:, c, 512:640], HN[:, c, :], c == 0, c == 7),
                         reads=[tWIN, tHN[c]], writes=[tPSF[mb]])
                P.op("dve", f_cp(KU[:, i * T:(i + 1) * T], PSF[mb][:, :]), reads=[tPSF[mb]], writes=[tKU[i]])
                yield
                mb = mmbank()
                for b in range(4):
                    for c in range(8):
                        P.op("pe", f_mm(PSF[mb][:, b * 128:(b + 1) * 128], HN[:, c, b * 128:(b + 1) * 128],
                                        WIN[:, c, 640:768], c == 0, c == 7),
                             reads=[tWIN, tHN[c]], writes=[tPSF[mb]])
                P.op("act", f_act(VU[:, i * 4:(i + 1) * 4, :], PSF[mb][:, :].rearrange("p (b f) -> p b f", f=128), AF.Copy),
                     reads=[tPSF[mb]], writes=[tVU[i]])
                yield
                for m in range(4):
                    mb = mmbank()
                    for c in range(8):
                        P.op("pe", f_mm(PSF[mb][:, :], WIN[:, c, 768 + m * 128:768 + (m + 1) * 128], HN[:, c, :], c == 0, c == 7),
                             reads=[tWIN, tHN[c]], writes=[tPSF[mb]])
                    P.op("dve", f_cp(XR[:, m, xs_, 2:T + 2], PSF[mb][:, :]), reads=[tPSF[mb]], writes=[tXR[xs_]])
                    yield
                linked = (ulink is not None and i == ulink)
                if i == 0:
                    P.op("pool", lambda h, o=XR[:, :, xs_, 0:2]: h.memset(o, 0.0), writes=[tXR[xs_]])
                else:
                    if linked:
                        P.op("pool", f_ts(XR[:, :, xs_, 0:2], XR[:, :, xp_, T:T + 2], LINK[:, 0:1], 0.0, ALU.mult, ALU.add),
                             reads=[tXR[xp_], tLINK], writes=[tXR[xs_]])
                        P.op("pool", f_ts(XR[:, :, xp_, T + 2:T + 3], XR[:, :, xs_, 2:3], LINK[:, 0:1], 0.0, ALU.mult, ALU.add),
                             reads=[tXR[xs_], tLINK], writes=[tXR[xp_]])
                    else:
                        P.op("pool", f_cp(XR[:, :, xs_, 0:2], XR[:, :, xp_, T:T + 2]), reads=[tXR[xp_]], writes=[tXR[xs_]])
                        P.op("pool", f_cp(XR[:, :, xp_, T + 2:T + 3], XR[:, :, xs_, 2:3]), reads=[tXR[xs_]], writes=[tXR[xp_]])
                if i == unt - 1:
                    P.op("pool", lambda h, o=XR[:, :, xs_, T + 2:T + 3]: h.memset(o, 0.0), writes=[tXR[xs_]])
                for m in range(4):
                    mb = mmbank()
                    for c in range(8):
                        P.op("pe", f_mm(PSF[mb][:, :], WIN[:, c, 1280 + m * 128:1280 + (m + 1) * 128], HN[:, c, :], c == 0, c == 7),
                             reads=[tWIN, tHN[c]], writes=[tPSF[mb]])
                    P.op("act", f_act(GG[s][:, m, :], PSF[mb][:, :], AF.Gelu_apprx_tanh), reads=[tPSF[mb]], writes=[tGG[s]])
                    yield
                P.dma("sp", f_dma(dview(gg_d, t0), GG[s][:, :, :]), "ggo", reads=[tGG[s]], writes=[TGG[gi]])

            for i in range(unt + 2):
                if i + 1 < unt:
                    xload(i + 1)
                gens = []
                if i < unt:
                    gens.append(proj(i))
                if 0 <= i - 2 < unt:
                    if i == unt + 1:
                        pending_tail[0] = lruG(i - 2)
                    else:
                        gens.append(lruG(i - 2))
                if i == 0 and tail is not None:
                    gens.append(tail)
                if epi_t[0] is not None:
                    gens.append(epi_t[0])
                    epi_t[0] = None
                interleave(gens)
                if i == unt - 1 and ut0 + unt == NT:
                    cur_ = ar.cur
                    ar.reset(base_mark)
                    WGb_ = ar.alloc("wg", [128, 16, 128], BF16)
                    WOUTb_ = ar.alloc("wout", [128, 8, D], BF16)
                    ar.cur = cur_
                    tWGb_, tWOUTb_ = Trk("wg"), Trk("wout")
                    P.dma("pool", f_dma(WGb_[:], wg_d[l].rearrange("j p m -> p j m")), "wgb", writes=[tWGb_, tWIN])
                    wsrc_ = wout_d[l].rearrange("(c p) n -> p c n", p=128)
                    for q2 in range(2):
                        P.dma("pool", f_dma(WOUTb_[:, :, q2 * 512:(q2 + 1) * 512], wsrc_[:, :, q2 * 512:(q2 + 1) * 512]),
                              "wout%d" % q2, writes=[tWOUTb_, tWIN])
                    pre_b[0] = (tWGb_, tWOUTb_)
                ng = ngen_for(i + 1) if i + 1 < unt else None
                if 1 <= i <= unt:
                    epi_t[0] = mix(i - 1, ng)
                elif ng is not None:
                    for _ in ng:
                        pass
        if pending_tail[0] is not None:
            for _ in pending_tail[0]:
                pass
            pending_tail[0] = None
        P.barrier()

        ar.reset(base_mark)
        WG = ar.alloc("wg", [128, 16, 128], BF16)
        WOUT = ar.alloc("wout", [128, 8, D], BF16)
        assert pre_b[0] is not None
        tWG, tWOUT = pre_b[0]
        pre_b[0] = None
        XTb = [ar.alloc("xt", [128, 8, T], F32) for _ in range(2)]
        tXb = [trks("x", 8) for _ in range(2)]
        XCi = [ar.alloc("xci", [128, 4, T], F32) for _ in range(2)]
        tXCi = [trks("xci", 4) for _ in range(2)]
        HFi = [ar.alloc("hfi", [128, 4, T], F32) for _ in range(2)]
        tHFi = trks("hfi", 2)
        GGi = [ar.alloc("ggi", [128, 4, T], BF16) for _ in range(2)]
        tGGi = trks("ggi", 2)
        ANi = [ar.alloc("ani", [128, 4, T], BF16) for _ in range(2)]
        tANi = trks("ani", 2)
        XCB = ar.alloc("xcb", [128, 4, T], BF16)
        tXCB = trks("xcb", 4)
        GB, tGB = alloc_gb()
        HB = [ar.alloc("hb", [128, 4, T], F32) for _ in range(2)]
        tHB = [trks("hb", 4) for _ in range(2)]
        LR = ar.alloc("lr", [128, 4, T], F32)
        tLR = trks("lr", 4)
        SQ = ar.alloc("sq", [128, 4, T], BF16)
        tSQ = Trk("sq")
        LN = ar.alloc("ln", [128, 4, T], BF16)
        tLN = trks("ln", 4)
        TMPb = ar.alloc("tmp", [128, T], F32)
        tTMPb = Trk("tmp")
        RSb = ar.alloc("rs", [128, T], F32)
        tRSb = Trk("rs")
        WMb = [ar.alloc("wm", [128, 8, D], BF16) for _ in range(2)]
        tWMb = trks("wm", 2)
        rr = [0]

        def bank6():
            b = rr[0] % 6
            rr[0] += 1
            return b

        modg = mod_gen(l + 1, WMb, tWMb, bank6) if l + 1 < L else None
        for (ut0, unt, ulink) in UNITS:

            def frontB(idx):
                j = unt - 1 - idx
                gi = ut0 + j
                t0 = gi * T
                s = idx % 2
                P.dma("sp", f_dma(XCi[s][:, :, :], dview(xc_d, t0)), "xci%d" % s, reads=[TXC[gi]], writes=tXCi[s])
                P.dma("sp", f_dma(HFi[s][:, :, :], dview(hf_d, t0)), "hfi%d" % s, reads=[THF[gi]], writes=[tHFi[s]])
                P.dma("sp", f_dma(GGi[s][:, :, :], dview(gg_d, t0)), "ggi%d" % s, reads=[TGG[gi]], writes=[tGGi[s]])
                P.dma("sp", f_dma(ANi[s][:, :, :], dview(an_d, t0)), "ani%d" % s, reads=[TAN[gi]], writes=[tANi[s]])
                P.dma("sp", f_dma(XTb[s][:, :, :], dview(xin_d, t0)), "xtb%d" % s, reads=[xin_t[gi]], writes=tXb[s])
                for c in range(4):
                    P.op("pool", f_cp(XCB[:, c, :], XCi[s][:, c, :]), reads=[tXCi[s][c]], writes=[tXCB[c]])
                yield
                yield from gates_gen(l, 1, XCi[s], tXCi[s], XCB, tXCB, WG, tWG, GB, tGB, bank6)
                for c in range(4):
                    if ulink is not None and j == ulink - 1:
                        P.op("dve", f_ts(GB["RA"][c][:, T - 1:T], GB["RA"][c][:, T - 1:T], LINK[:, 0:1], None, ALU.mult),
                             reads=[tGB["RA"][c], tLINK], writes=[tGB["RA"][c]])
                    init = 0.0 if idx == 0 else HB[1 - s][:, c, 0:1]
                    rd = [tGB["RA"][c], tGB["IU"][c]] + ([] if idx == 0 else [tHB[1 - s][c]])
                    P.op("dve", f_scan(HB[s][:, c, ::-1], GB["RA"][c][:, ::-1], GB["IU"][c][:, ::-1], init),
                         reads=rd, writes=[tHB[s][c]])
                    if c % 2 == 1:
                        yield

            def backB(idx):
                j = unt - 1 - idx
                gi = ut0 + j
                seg = gi // 4
                t0 = gi * T
                s = idx % 2
                for c in range(4):
                    P.op("dve", f_tt(LR[:, c, :], HFi[s][:, c, :], HB[s][:, c, :], ALU.add),
                         reads=[tHFi[s], tHB[s][c]], writes=[tLR[c]])
                    P.op("dve", f_tt(LR[:, c, :], LR[:, c, :], GGi[s][:, c, :], ALU.mult),
                         reads=[tLR[c], tGGi[s]], writes=[tLR[c]])
                    if c % 2 == 1:
                        yield
                P.op("act", f_act(SQ[:, :, :], LR[:, :, :], AF.Square), reads=tLR, writes=[tSQ])
                mb = bank6()
                for c in range(4):
                    P.op("pe", f_mm(PSF[mb][:, :], ONES5[:, :], SQ[:, c, :], c == 0, c == 3), reads=[tSQ, tCONST], writes=[tPSF[mb]])
                P.op("act", f_act(TMPb[:, :], PSF[mb][:, :], AF.Ln, bias=EPSC[:, 0:1]), reads=[tPSF[mb], tCONST], writes=[tTMPb])
                P.op("act", f_act(RSb[:, :], TMPb[:, :], AF.Exp, scale=-0.5), reads=[tTMPb], writes=[tRSb])
                yield
                for c in range(4):
                    P.op("dve", f_stt(LN[:, c, :], LR[:, c, :], vec_ap(l, V_GL + c), RSb[:, :], ALU.mult, ALU.mult),
                         reads=[tLR[c], tVEC, tRSb], writes=[tLN[c]])
                yield
                for m in range(8):
                    mb = bank6()
                    for c in range(4):
                        P.op("pe", f_mm(PSF[mb][:, :], WOUT[:, c, m * 128:(m + 1) * 128], ANi[s][:, c, :], c == 0, False),
                             reads=[tWOUT, tANi[s]], writes=[tPSF[mb]])
                    for c in range(4):
                        P.op("pe", f_mm(PSF[mb][:, :], WOUT[:, 4 + c, m * 128:(m + 1) * 128], LN[:, c, :], False, c == 3),
                             reads=[tWOUT, tLN[c]], writes=[tPSF[mb]])
                    P.op("dve", f_stt(XTb[s][:, m, :], PSF[mb][:, :], mod_ap(l, 2, m, seg), XTb[s][:, m, :], ALU.mult, ALU.add),
                         reads=[tPSF[mb], tMOD, tXb[s][m]], writes=[tXb[s][m]])
                    if m % 2 == 1:
                        yield
                P.dma("sp", f_dma(dview(xb_d, t0), XTb[s][:, :, :]), "xbo%d" % s, reads=tXb[s], writes=[TX["xb"][gi]])

            interleave([frontB(0)])
            for idx in range(unt):
                gens = [backB(idx)]
                if idx + 1 < unt:
                    gens.append(frontB(idx + 1))
                interleave(gens)
                if modg is not None:
                    next(modg, None)
        if modg is not None:
            for _ in modg:
                pass
        P.barrier()

        ar.reset(base_mark)
        WFI = ar.alloc("wfi", [128, 8, 2 * DFF], BF16)
        WFO = ar.alloc("wfo", [128, 22, D], BF16)
        tWFI = trks("wfi", 8)
        tWFO = trks("wfo", 4)
        JB = (0, 6, 12, 17, 22)
        wsrc = wfi_d[l].rearrange("(c p) n -> p c n", p=128)
        for q8 in (0, 4, 1, 5, 2, 6, 3, 7):
            P.dma("pool", f_dma(WFI[:, :, q8 * 704:(q8 + 1) * 704], wsrc[:, :, q8 * 704:(q8 + 1) * 704]),
                  "wfi%d" % q8, writes=[tWFI[q8]])
        wsrc = wfo_d[l].rearrange("(c p) n -> p c n", p=128)
        for q4 in range(4):
            P.dma("pool", f_dma(WFO[:, JB[q4]:JB[q4 + 1], :], wsrc[:, JB[q4]:JB[q4 + 1], :]),
                  "wfo%d" % q4, writes=[tWFO[q4]])
        XTf = [ar.alloc("xt", [128, 8, T], F32) for _ in range(2)]
        tXf = [trks("x", 8) for _ in range(2)]
        HN = ar.alloc("hn", [128, 8, T], BF16)
        tHN = trks("hn", 8)
        TMP = [ar.alloc("tmp", [128, T], F32) for _ in range(3)]
        tTMP = trks("tmp", 3)
        RS = ar.alloc("rs", [128, T], F32)
        tRS = Trk("rs")
        ACTB = ar.alloc("actb", [128, 11, T], BF16)
        tACTB = trks("actb", 11)
        SG = [ar.alloc("sg", [128, T], F32) for _ in range(2)]
        tSG = trks("sg", 2)
        wfo_t = lambda jj: tWFO[0 if jj < 6 else 1 if jj < 12 else 2 if jj < 17 else 3]

        def f_load(gi):
            P.dma("sp", f_dma(XTf[gi % 2][:, :, :], dview(xb_d, gi * T)), "xtf%d" % (gi % 2), reads=[TX["xb"][gi]], writes=tXf[gi % 2])

        def f_norm_gen(gi):
            return norm_gen(XTf[gi % 2], tXf[gi % 2], HN, tHN, TMP, tTMP, RS, tRS, l, 4, 3, gi // 4, 0)

        def f_gu(gi, j0):
            for jl in range(11):
                jj = j0 + jl
                gb, ub = jj % 2, 2 + jj % 2
                for half, pb in ((0, gb), (1, ub)):
                    col = half * DFF + jj * 128
                    for c in range(8):
                        P.op("pe", f_mm(PSF[pb][:, :], WFI[:, c, col:col + 128], HN[:, c, :], c == 0, c == 7),
                             reads=[tWFI[col // 704], tWFI[(col + 127) // 704], tHN[c]], writes=[tPSF[pb]])
                P.op("act", f_act(SG[jj % 2][:, :], PSF[gb][:, :], AF.Silu), reads=[tPSF[gb]], writes=[tSG[jj % 2]])
                P.op("dve", f_tt(ACTB[:, jl, :], SG[jj % 2][:, :], PSF[ub][:, :], ALU.mult),
                     reads=[tSG[jj % 2], tPSF[ub]], writes=[tACTB[jl]])

        def f_out(gi, j0, ng=None):
            XTt, tX, seg = XTf[gi % 2], tXf[gi % 2], gi // 4
            if ng is not None:
                next(ng, None)
            for m in range(8):
                pb = 4 + m % 2
                for jl in range(11):
                    jj = j0 + jl
                    P.op("pe", f_mm(PSF[pb][:, :], WFO[:, jj, m * 128:(m + 1) * 128], ACTB[:, jl, :], jl == 0, jl == 10),
                         reads=[wfo_t(jj), tACTB[jl]], writes=[tPSF[pb]])
                P.op("dve", f_stt(XTt[:, m, :], PSF[pb][:, :], mod_ap(l, 5, m, seg), XTt[:, m, :], ALU.mult, ALU.add),
                     reads=[tPSF[pb], tMOD, tX[m]], writes=[tX[m]])
                if ng is not None and m >= 1:
                    next(ng, None)
            if ng is not None:
                for _ in ng:
                    pass

        def f_fin(gi):
            XTt, tX, t0 = XTf[gi % 2], tXf[gi % 2], gi * T
            if l == L - 1:
                P.op("act", f_act(ACTB[:, 0:8, :], XTt[:, :, :], AF.Square), reads=tX, writes=tACTB[0:8])
                for c in range(8):
                    P.op("pe", f_mm(PSF[4][:, :], ONES[:, :], ACTB[:, c, :], c == 0, c == 7), reads=[tACTB[c], tCONST], writes=[tPSF[4]])
                P.op("act", f_act(SG[0][:, :], PSF[4][:, :], AF.Sqrt, bias=EPSC[:, 0:1]), reads=[tPSF[4], tCONST], writes=[tSG[0]])
                P.op("dve", lambda h, o=SG[1][:, :], i=SG[0][:, :]: h.reciprocal(out=o, in_=i), reads=[tSG[0]], writes=[tSG[1]])
                for c in range(8):
                    P.op("dve", f_stt(XTt[:, c, :], XTt[:, c, :], VEC[:, DEPTH * NV + c:DEPTH * NV + c + 1], SG[1][:, :],
                                      ALU.mult, ALU.mult), reads=[tX[c], tVEC, tSG[1]], writes=[tX[c]])
            o = P.dma("sp", f_dma(dview(xout_d, t0), XTt[:, :, :]), "xfo%d" % (gi % 2), reads=tX, writes=[xout_t[gi]])
            if l == L - 1:
                final_out.append(o)

        f_load(0)
        for _ in f_norm_gen(0):
            pass
        for gi in range(NT):
            if gi + 1 < NT:
                f_load(gi + 1)
            f_gu(gi, 0)
            f_out(gi, 0)
            f_gu(gi, 11)
            if gi == NT - 1 and l + 1 < L:
                cur_ = ar.cur
                ar.reset(base_mark)
                WINn_ = ar.alloc("win", [128, 8, DIN], BF16)
                WGn_ = ar.alloc("wg", [128, 16, 128], BF16)
                ar.cur = cur_
                tWINn_, tWGn_ = Trk("win"), Trk("wg")
                load_win(l + 1, WINn_, WGn_, tWINn_, tWGn_, extra_w=tWFI)
                pre_win[0] = (tWINn_, tWGn_)
            f_out(gi, 11, f_norm_gen(gi + 1) if gi + 1 < NT else None)
            f_fin(gi)
        P.barrier()
        xin_d, xin_t = xa_d, TX["xa"]

    P.emit(nc, final_waits=final_out)
    return nc, P


def _segments():
    segs = []
    for k in range(4):
        segs.append([("p", k, 0), ("p", k, 2048), ("s", k, 0)])
    for k in range(4):
        segs.append([("s", 4 + 3 * k + r, 0) for r in range(3)])
    return segs


def _dtab(linked):
    q = np.arange(128)[:, None]
    sidx = np.arange(384)[None, :] - 128
    dist = np.abs(q - sidx).astype(np.float32)
    base = np.where(dist <= 128, -dist, -1.0e9).astype(np.float32)
    d15 = base.copy()
    d16 = base.copy()
    if not linked:
        d15[:, 256:384] = -1.0e9
        d16[:, 0:128] = -1.0e9
    return np.concatenate([base, d15, d16], axis=1).astype(np.float32)


def _fm(v):
    v = np.asarray(v, np.float32)
    return v.reshape(-1, 128).T


_CACHE = {}


def kernel(x_prompt, x_sample, c_prompt, c_sample, w_mod, b_mod, g_norm1, w_in, sink, conv_w, conv_b,
           w_rg, b_rg, w_ig, b_ig, lam, g_attn_out, g_lru_out, w_out, g_norm2, w_ffn_in, w_ffn_out, g_final,
           _depth=DEPTH):
    f32 = np.float32
    segs = _segments()
    xs = {"p": np.asarray(x_prompt, f32), "s": np.asarray(x_sample, f32)}
    cs = {"p": np.asarray(c_prompt, f32), "s": np.asarray(c_sample, f32)}
    vecs = np.zeros((128, DEPTH * NV + 8), f32)
    for l in range(DEPTH):
        o = l * NV
        vecs[:, o + V_G1:o + V_G1 + 8] = _fm(g_norm1[l])
        vecs[:, o + V_G2:o + V_G2 + 8] = _fm(g_norm2[l])
        vecs[:, o + V_BMOD:o + V_BMOD + 48] = _fm(b_mod[l])
        for tap in range(4):
            vecs[:, o + V_CW + tap * 4:o + V_CW + tap * 4 + 4] = _fm(conv_w[l][tap])
        vecs[:, o + V_CB:o + V_CB + 4] = _fm(conv_b[l])
        for d in range(2):
            vecs[:, o + V_BRG + d * 4:o + V_BRG + d * 4 + 4] = _fm(b_rg[l][d])
            vecs[:, o + V_BIG + d * 4:o + V_BIG + d * 4 + 4] = _fm(b_ig[l][d])
            vecs[:, o + V_LAM + d * 4:o + V_LAM + d * 4 + 4] = _fm(lam[l][d])
        vecs[:, o + V_GA:o + V_GA + 4] = _fm(g_attn_out[l])
        vecs[:, o + V_GL:o + V_GL + 4] = _fm(g_lru_out[l])
        vecs[:, o + V_SINK:o + V_SINK + 8] = np.asarray(sink[l], f32)[None, :]
    vecs[:, DEPTH * NV:DEPTH * NV + 8] = _fm(g_final)
    perm = np.concatenate([np.concatenate([np.arange(c * 64, c * 64 + 64), np.arange((4 + c) * 64, (4 + c) * 64 + 64)])
                           for c in range(4)] + [np.arange(512, DIN)])
    w_in_p = np.ascontiguousarray(np.asarray(w_in, f32)[:, :, perm])
    wg = np.zeros((DEPTH, 16, 128, 128), f32)
    for l in range(DEPTH):
        for d in range(2):
            for g, w in ((0, w_rg), (1, w_ig)):
                for c in range(4):
                    j = (d * 2 + g) * 4 + c
                    wg[l, j, 0:64, 0:64] = w[l][d][2 * c]
                    wg[l, j, 64:128, 64:128] = w[l][d][2 * c + 1]
    shared = {"vecs": vecs, "w_mod": np.asarray(w_mod, f32), "w_in": w_in_p, "wg": wg,
              "w_out": np.asarray(w_out, f32), "w_ffn_in": np.asarray(w_ffn_in, f32),
              "w_ffn_out": np.asarray(w_ffn_out, f32)}
    in_maps = []
    for k in range(8):
        xt = np.empty((8, 128, NTOK), f32)
        ct = np.empty((128, 8, 3), f32)
        for si, (kind, b, t0) in enumerate(segs[k]):
            xt[:, :, si * 2048:(si + 1) * 2048] = xs[kind][b, t0:t0 + 2048, :].T.reshape(8, 128, 2048)
            ct[:, :, si] = cs[kind][b].reshape(8, 128).T
        linked = k < 4
        m = dict(shared)
        m["xT"] = xt
        m["cT"] = np.ascontiguousarray(ct.reshape(128, 24))
        m["link"] = np.full((128, 1), 1.0 if linked else 0.0, f32)
        m["dtab"] = _dtab(linked)
        in_maps.append(m)
    if _depth not in _CACHE:
        _CACHE[_depth] = build_program(_depth)[0]
    nc = _CACHE[_depth]
    res = run_bass_kernel_spmd(nc, in_maps, core_ids=list(range(8)))
    y_p = np.empty((4, 4096, D), f32)
    y_s = np.empty((16, 2048, D), f32)
    ys = {"p": y_p, "s": y_s}
    for k in range(8):
        yt = np.asarray(res.results[k]["yT"], f32)
        for si, (kind, b, t0) in enumerate(segs[k]):
            ys[kind][b, t0:t0 + 2048, :] = yt[:, :, si * 2048:(si + 1) * 2048].reshape(D, 2048).T
    return (y_p, y_s)
```
